# Optimizing a Trainium2 kernel written in Bass

```python
import math
import jax, jax.numpy as jnp
from jax import lax
import numpy as np

D_MODEL = 1024
BATCH = 8
SEQ = 8192
DEPTH = 1
DEC_BATCH = 16
DEC_SEQ = 64
PAST_LEN = 2048

CHUNK = 64
HEAD_DIM = D_MODEL // 16
N_HEADS_SB = (D_MODEL // 2) // HEAD_DIM
N_HEADS_BAND = (D_MODEL // 2) // HEAD_DIM
W_SB = N_HEADS_SB * HEAD_DIM
W_BAND = N_HEADS_BAND * HEAD_DIM
MIX_WIDTH = W_SB + W_BAND
D_FF = 4 * D_MODEL
PAST_CHUNKS = 8
BAND_KEYS = (PAST_CHUNKS + 1) * CHUNK
REL_CLIP = 2 * CHUNK
SB_BLOCK = 128
EPS = 1e-6
NEG_INF = -1e30

kernel_name = "hymba_stickbreak_chunkband_stream"


def rmsnorm(x, g):
    xf = x.astype(jnp.float32)
    y = xf * lax.rsqrt(jnp.mean(xf * xf, axis=-1, keepdims=True) + EPS)
    return (y * g.astype(jnp.float32)).astype(x.dtype)


def project(xn, w_in):
    p = xn @ w_in
    cuts = [W_SB, 2 * W_SB, 3 * W_SB, 3 * W_SB + W_BAND, 3 * W_SB + 2 * W_BAND]
    q_sb, k_sb, v_sb, q_bd, k_bd, v_bd = jnp.split(p, cuts, axis=-1)
    B, T = xn.shape[0], xn.shape[1]
    sb = [a.reshape(B, T, N_HEADS_SB, HEAD_DIM) for a in (q_sb, k_sb, v_sb)]
    bd = [a.reshape(B, T, N_HEADS_BAND, HEAD_DIM) for a in (q_bd, k_bd, v_bd)]
    return sb[0], sb[1], sb[2], bd[0], bd[1], bd[2]


def stick_breaking(q, k, v, q_pos):
    z = jnp.einsum('bqhd,bkhd->bhqk', q, k).astype(jnp.float32) * (1.0 / math.sqrt(HEAD_DIM))
    k_pos = jnp.arange(k.shape[1])
    mask = k_pos[None, :] < q_pos[:, None]
    log_rem = jnp.where(mask, jax.nn.log_sigmoid(-z), 0.0)
    after = lax.cumsum(log_rem, axis=3, reverse=True) - log_rem
    w = jnp.where(mask, jnp.exp(jax.nn.log_sigmoid(z) + after), 0.0)
    return jnp.einsum('bhqk,bkhd->bqhd', w.astype(v.dtype), v)


def stick_breaking_prompt(q, k, v):
    B, S, H, d = q.shape
    nb = S // SB_BLOCK
    q_blocks = jnp.moveaxis(q.reshape(B, nb, SB_BLOCK, H, d), 1, 0)

    def one_block(args):
        qb, bi = args
        q_pos = bi * SB_BLOCK + jnp.arange(SB_BLOCK)
        return stick_breaking(qb, k, v, q_pos)

    out = lax.map(one_block, (q_blocks, jnp.arange(nb)))
    return jnp.moveaxis(out, 0, 1).reshape(B, S, H, d)


def band_attend(q, k, v, q_pos, k_pos, rel_bias):
    s = jnp.einsum('bqhd,bkhd->bhqk', q, k).astype(jnp.float32) * (1.0 / math.sqrt(HEAD_DIM))
    rel = jnp.clip(q_pos[:, None] - k_pos[None, :], -REL_CLIP, REL_CLIP) + REL_CLIP
    s = s + rel_bias.astype(jnp.float32)[:, rel][None]
    s = jnp.where((k_pos >= 0)[None, None, None, :], s, NEG_INF)
    p = jax.nn.softmax(s, axis=-1)
    return jnp.einsum('bhqk,bkhd->bqhd', p.astype(v.dtype), v)


def band_prompt(q, k, v, rel_bias):
    B, S, H, d = q.shape
    nc = S // CHUNK
    pad = PAST_CHUNKS * CHUNK
    k_pad = jnp.pad(k, ((0, 0), (pad, 0), (0, 0), (0, 0)))
    v_pad = jnp.pad(v, ((0, 0), (pad, 0), (0, 0), (0, 0)))

    def one_chunk(c):
        qc = lax.dynamic_slice_in_dim(q, c * CHUNK, CHUNK, axis=1)
        kc = lax.dynamic_slice_in_dim(k_pad, c * CHUNK, BAND_KEYS, axis=1)
        vc = lax.dynamic_slice_in_dim(v_pad, c * CHUNK, BAND_KEYS, axis=1)
        q_pos = c * CHUNK + jnp.arange(CHUNK)
        k_pos = (c - PAST_CHUNKS) * CHUNK + jnp.arange(BAND_KEYS)
        return band_attend(qc, kc, vc, q_pos, k_pos, rel_bias)

    out = lax.map(one_chunk, jnp.arange(nc))
    return jnp.moveaxis(out, 0, 1).reshape(B, S, H, d)


def mix_out(o_sb, o_bd, g_sb, g_bd, w_out):
    B, T = o_sb.shape[0], o_sb.shape[1]
    y_sb = rmsnorm(o_sb.reshape(B, T, W_SB), g_sb)
    y_bd = rmsnorm(o_bd.reshape(B, T, W_BAND), g_bd)
    return jnp.concatenate([y_sb, y_bd], axis=-1) @ w_out


def ffn(h, g, w_up, w_down):
    u = rmsnorm(h, g) @ w_up
    return jnp.square(jax.nn.relu(u)) @ w_down


def setup_inputs(seed: int = 0) -> dict:
    key = jax.random.key(seed)
    ks = jax.random.split(key, 16)
    f32 = jnp.float32
    band_rows = min(PAST_CHUNKS * CHUNK, PAST_LEN)
    nrm = lambda k, shape, s: (jax.random.normal(k, shape, f32) * s)
    return {
        "x_prompt": nrm(ks[0], (BATCH, SEQ, D_MODEL), 1.0),
        "x_sample": nrm(ks[1], (DEC_BATCH, DEC_SEQ, D_MODEL), 1.0),
        "cache_sb_k": nrm(ks[2], (DEPTH, DEC_BATCH, PAST_LEN, N_HEADS_SB, HEAD_DIM), 1.0),
        "cache_sb_v": nrm(ks[3], (DEPTH, DEC_BATCH, PAST_LEN, N_HEADS_SB, HEAD_DIM), 1.0),
        "cache_band_k": nrm(ks[4], (DEPTH, DEC_BATCH, band_rows, N_HEADS_BAND, HEAD_DIM), 1.0),
        "cache_band_v": nrm(ks[5], (DEPTH, DEC_BATCH, band_rows, N_HEADS_BAND, HEAD_DIM), 1.0),
        "norm_mix_g": 1.0 + nrm(ks[6], (DEPTH, D_MODEL), 0.02),
        "w_in": nrm(ks[7], (DEPTH, D_MODEL, 3 * MIX_WIDTH), D_MODEL ** -0.5),
        "rel_bias": nrm(ks[8], (DEPTH, N_HEADS_BAND, 2 * REL_CLIP + 1), 0.1),
        "norm_sb_g": 1.0 + nrm(ks[9], (DEPTH, W_SB), 0.02),
        "norm_band_g": 1.0 + nrm(ks[10], (DEPTH, W_BAND), 0.02),
        "w_out": nrm(ks[11], (DEPTH, MIX_WIDTH, D_MODEL), MIX_WIDTH ** -0.5),
        "norm_ffn_g": 1.0 + nrm(ks[12], (DEPTH, D_MODEL), 0.02),
        "w_up": nrm(ks[13], (DEPTH, D_MODEL, D_FF), D_MODEL ** -0.5),
        "w_down": nrm(ks[14], (DEPTH, D_FF, D_MODEL), D_FF ** -0.5),
        "norm_final_g": 1.0 + nrm(ks[15], (D_MODEL,), 0.02),
    }


def reference(x_prompt, x_sample, cache_sb_k, cache_sb_v, cache_band_k, cache_band_v,
              norm_mix_g, w_in, rel_bias, norm_sb_g, norm_band_g, w_out,
              norm_ffn_g, w_up, w_down, norm_final_g):
    S = x_prompt.shape[1]
    T = x_sample.shape[1]
    past = cache_sb_k.shape[2]
    band_rows = cache_band_k.shape[2]
    keep_p = min(PAST_CHUNKS * CHUNK, S)

    h_p, h_s = x_prompt, x_sample
    sbk_p, sbv_p, bdk_p, bdv_p = [], [], [], []
    sbk_s, sbv_s, bdk_s, bdv_s = [], [], [], []
    for l in range(DEPTH):
        xn = rmsnorm(h_p, norm_mix_g[l])
        q_sb, k_sb, v_sb, q_bd, k_bd, v_bd = project(xn, w_in[l])
        o_sb = stick_breaking_prompt(q_sb, k_sb, v_sb)
        o_bd = band_prompt(q_bd, k_bd, v_bd, rel_bias[l])
        h_p = h_p + mix_out(o_sb, o_bd, norm_sb_g[l], norm_band_g[l], w_out[l])
        h_p = h_p + ffn(h_p, norm_ffn_g[l], w_up[l], w_down[l])
        sbk_p.append(k_sb)
        sbv_p.append(v_sb)
        bdk_p.append(k_bd[:, S - keep_p:])
        bdv_p.append(v_bd[:, S - keep_p:])

        xn = rmsnorm(h_s, norm_mix_g[l])
        q_sb, k_sb, v_sb, q_bd, k_bd, v_bd = project(xn, w_in[l])
        q_pos = past + jnp.arange(T)
        k_all = jnp.concatenate([cache_sb_k[l].astype(k_sb.dtype), k_sb], axis=1)
        v_all = jnp.concatenate([cache_sb_v[l].astype(v_sb.dtype), v_sb], axis=1)
        o_sb = stick_breaking(q_sb, k_all, v_all, q_pos)
        kb_all = jnp.concatenate([cache_band_k[l].astype(k_bd.dtype), k_bd], axis=1)
        vb_all = jnp.concatenate([cache_band_v[l].astype(v_bd.dtype), v_bd], axis=1)
        kb_pos = past - band_rows + jnp.arange(band_rows + T)
        o_bd = band_attend(q_bd, kb_all, vb_all, q_pos, kb_pos, rel_bias[l])
        h_s = h_s + mix_out(o_sb, o_bd, norm_sb_g[l], norm_band_g[l], w_out[l])
        h_s = h_s + ffn(h_s, norm_ffn_g[l], w_up[l], w_down[l])
        sbk_s.append(k_sb)
        sbv_s.append(v_sb)
        bdk_s.append(k_bd)
        bdv_s.append(v_bd)

    y_prompt = rmsnorm(h_p, norm_final_g)
    y_sample = rmsnorm(h_s, norm_final_g)
    sb_k_prompt = jnp.stack(sbk_p)
    sb_v_prompt = jnp.stack(sbv_p)
    band_k_prompt = jnp.stack(bdk_p)
    band_v_prompt = jnp.stack(bdv_p)
    sb_k_sample = jnp.stack(sbk_s)
    sb_v_sample = jnp.stack(sbv_s)
    band_k_sample = jnp.stack(bdk_s)
    band_v_sample = jnp.stack(bdv_s)
    return (y_prompt, y_sample, sb_k_prompt, sb_v_prompt, band_k_prompt, band_v_prompt,
            sb_k_sample, sb_v_sample, band_k_sample, band_v_sample)
```

```python
import contextlib
import numpy as np
import concourse.bass as bass
import concourse.mybir as mybir
from concourse.bass_utils import run_bass_kernel_spmd

F32 = mybir.dt.float32
BF16 = mybir.dt.bfloat16
U8 = mybir.dt.uint8
AF = mybir.ActivationFunctionType
ALU = mybir.AluOpType

D = 1024
W = 512
DFF = 4096
EPS = 1e-6
PAST = 2048
BROWS = 512
TS = 128
LX = 384
SB_TOTAL = 206 * 1024
SHIFT = 20.0
ESHIFT = float(np.exp(20.0))


class Buf:
    __slots__ = ("ap", "w", "r", "lsem", "ssem", "name")

    def __init__(self, ap, name=""):
        self.ap = ap
        self.w = {}
        self.r = {}
        self.lsem = None
        self.ssem = None
        self.name = name


class DSem:
    __slots__ = ("sem", "cnt", "kind")

    def __init__(self, sem):
        self.sem = sem
        self.cnt = 0
        self.kind = "pool"


ENGS = ("pe", "act", "dve", "pool", "sp")


class _Rec:
    def __init__(self):
        self.calls = []

    def __getattr__(self, name):
        def f(*a, **k):
            self.calls.append((name, a, k))
            return self
        return f


class KB:
    def __init__(self, nc, stack):
        self.nc = nc
        self.stack = stack
        self.q = {e: [] for e in ENGS}
        self.esem = {}
        self.ecnt = {}
        self.last = {}
        self.waited = {}
        self.pend = {e: [] for e in ENGS}
        self.nsem = 0
        self.dsems = []
        self.free_dsems = {}
        self.sb_off = 0
        self.sb_mark = 0

    def new_sem(self, name):
        self.nsem += 1
        return self.stack.enter_context(self.nc.semaphore(f"{name}_{self.nsem}"))

    def new_phase(self, name):
        for e in ("pe", "act", "dve", "pool"):
            self.esem[e] = self.new_sem(f"{name}_{e}")
            self.ecnt[e] = 0

    def get_dsem(self, kind="pool"):
        fl = self.free_dsems.setdefault(kind, [])
        if fl:
            return fl.pop()
        d = DSem(self.new_sem("dma" + kind))
        d.kind = kind
        self.dsems.append(d)
        return d

    def _wait(self, eng, tok):
        sem, val, peng = tok
        if peng == "pe" and eng == "pe":
            return
        key = (eng, id(sem))
        if self.waited.get(key, 0) >= val:
            return
        self.waited[key] = val
        self.q[eng].append(lambda e, sem=sem, val=val: e.wait_ge(sem, val))

    @staticmethod
    def _merge(d, tok):
        k = id(tok[0])
        if k not in d or d[k][1] < tok[1]:
            d[k] = tok

    def _deps(self, eng, reads, writes):
        for b in reads:
            for t in b.w.values():
                self._wait(eng, t)
        for b in writes:
            for t in b.w.values():
                self._wait(eng, t)
            for t in b.r.values():
                self._wait(eng, t)

    def _commit(self, tok, reads, writes):
        for b in reads:
            self._merge(b.r, tok)
        for b in writes:
            b.w = {id(tok[0]): tok}
            b.r = {}

    def op(self, eng, fn, reads=(), writes=(), signal=True):
        rec = _Rec()
        fn(rec)
        assert len(rec.calls) == 1
        mname, margs, mkw = rec.calls[0]
        fn = lambda e, mname=mname, margs=margs, mkw=mkw: getattr(e, mname)(*margs, **mkw)
        self._deps(eng, reads, writes)
        if not signal:
            self.q[eng].append(lambda e, fn=fn: fn(e))
            self.pend[eng].append((tuple(reads), tuple(writes)))
            return None
        sem = self.esem[eng]
        self.ecnt[eng] += 1
        tok = (sem, self.ecnt[eng], eng)
        self.q[eng].append(lambda e, fn=fn, sem=sem: fn(e).then_inc(sem, 1))
        for (rs, ws) in self.pend[eng]:
            self._commit(tok, rs, ws)
        self.pend[eng] = []
        self._commit(tok, reads, writes)
        self.last[eng] = tok
        return tok

    def dma(self, qeng, pairs, dsem, reads=(), writes=(), slow=False):
        self._deps(qeng, reads, writes)
        for (o, i) in pairs:
            if slow:
                self.q[qeng].append(
                    lambda e, o=o, i=i, s=dsem.sem: e.dma_start(
                        out=o, in_=i, allow_slow_non_contiguous=True).then_inc(s, 16))
            else:
                self.q[qeng].append(
                    lambda e, o=o, i=i, s=dsem.sem: e.dma_start(out=o, in_=i).then_inc(s, 16))
        dsem.cnt += 16 * len(pairs)
        tok = (dsem.sem, dsem.cnt, "dma")
        self._commit(tok, reads, writes)
        return tok

    def load(self, pairs, dst, reads=()):
        if dst.lsem is None:
            dst.lsem = self.get_dsem("sp")
        return self.dma("sp", pairs, dst.lsem, reads=reads, writes=(dst,))

    def store(self, pairs, src, writes=(), qeng="pool"):
        if src.ssem is None:
            src.ssem = self.get_dsem(qeng)
        return self.dma(qeng, pairs, src.ssem, reads=(src,), writes=writes)

    def barrier(self):
        toks = [self.last[e] for e in ("pe", "act", "dve", "pool") if e in self.last]
        toks += [(d.sem, d.cnt, "dma") for d in self.dsems if d.cnt > 0]
        for e in ENGS:
            assert not self.pend[e], e
            for t in toks:
                if t[2] == e:
                    continue
                sem, val, _ = t
                key = (e, id(sem))
                if self.waited.get(key, 0) >= val:
                    continue
                self.waited[key] = val
                self.q[e].append(lambda en, sem=sem, val=val: en.wait_ge(sem, val))

    def release_dsems(self, bufs):
        for b in bufs:
            for a in ("lsem", "ssem"):
                d = getattr(b, a)
                if d is not None:
                    self.free_dsems.setdefault(d.kind, []).append(d)
                    setattr(b, a, None)


def build_program(S, stop_after=None):
    assert S % 512 == 0
    NT = S + TS
    KEEP = min(512, S)
    NQT = S // 512
    NKB = S // 128

    nc = bass.Bass("TRN2", target_bir_lowering=False)

    def din(name, shape, dt=F32):
        return nc.dram_tensor(name, list(shape), dt, kind="ExternalInput").ap()

    def dout(name, shape, dt=F32):
        return nc.dram_tensor(name, list(shape), dt, kind="ExternalOutput").ap()

    def dscr(name, shape, dt):
        return nc.dram_tensor(name, list(shape), dt, kind="Internal").ap()

    x_p = din("x_p", [S, D])
    x_s = din("x_s", [TS, D])
    csk = din("csk", [2, PAST, W])
    csv = din("csv", [2, PAST, W])
    cbk = din("cbk", [2, BROWS, W])
    cbv = din("cbv", [2, BROWS, W])
    w_in = din("w_in", [D, 3 * D])
    w_out = din("w_out", [D, D])
    w_up = din("w_up", [D, DFF])
    w_down = din("w_down", [DFF, D])
    g_mix = din("g_mix", [D])
    g_ffn = din("g_ffn", [D])
    g_sb = din("g_sb", [W])
    g_bd = din("g_bd", [W])
    g_fin = din("g_fin", [D])
    relb = din("relb", [8, 257])

    y_p = dout("y_p", [S, D])
    y_s = dout("y_s", [TS, D])
    sbk_p = dout("sbk_p", [S, W])
    sbv_p = dout("sbv_p", [S, W])
    bdk_p = dout("bdk_p", [KEEP, W])
    bdv_p = dout("bdv_p", [KEEP, W])
    sbk_s = dout("sbk_s", [TS, W])
    sbv_s = dout("sbv_s", [TS, W])
    bdk_s = dout("bdk_s", [TS, W])
    bdv_s = dout("bdv_s", [TS, W])

    qt_sb = dscr("qt_sb", [4, 128, NT], BF16)
    kt_sb = dscr("kt_sb", [4, 128, NT], BF16)
    qt_bd = dscr("qt_bd", [4, 128, NT], BF16)
    kt_bd = dscr("kt_bd", [4, 128, NT], BF16)
    v_sb = dscr("v_sb", [NT, W], BF16)
    v_bd = dscr("v_bd", [NT, W], BF16)
    ot = dscr("ot", [8, 128, NT], BF16)
    hs = dscr("hs", [NT, D], F32)
    e_all = dscr("e_all", [8, LX], F32)
    xrep = dscr("xrep", [8, 129 * LX], F32)

    stack = contextlib.ExitStack()
    with stack:
        big = stack.enter_context(nc.sbuf_tensor("big", [128, SB_TOTAL], U8))
        psum = stack.enter_context(nc.psum_tensor("psum", [128, 4096], F32))
        kb = KB(nc, stack)

        def sb(nelem, dt, shape=None, name=""):
            size = 4 if dt == F32 else 2
            off = (kb.sb_off + 63) // 64 * 64
            nbytes = nelem * size
            assert off + nbytes <= SB_TOTAL, (name, off, nbytes)
            kb.sb_off = off + nbytes
            ap = big[:, off:off + nbytes].bitcast(dt)
            if shape is not None:
                if len(shape) == 2:
                    ap = ap.rearrange("p (a b) -> p a b", b=shape[1])
                elif len(shape) == 3:
                    ap = ap.rearrange("p (a b c) -> p a b c", b=shape[1], c=shape[2])
            return Buf(ap, name)

        def ring(n, nelem, dt, shape=None, name=""):
            return [sb(nelem, dt, shape, f"{name}{i}") for i in range(n)]

        def bank(b, nb=1):
            return psum[:, b * 512:(b + nb) * 512]

        def pbuf(ap, name=""):
            return Buf(ap, name)

        ident = sb(128, BF16, name="ident")
        tri8 = sb(128, BF16, name="tri8")
        ones8 = sb(128, BF16, name="ones8")
        ones1 = sb(64, BF16, name="ones1")
        mc = sb(128, F32, name="mc")
        mfar = sb(128, F32, name="mfar")
        mnear = sb(256, F32, name="mnear")
        wn = sb(8 * 256, F32, (8, 256), "wn")
        en = sb(8 * 256, F32, (8, 256), "en")
        gfin_t = sb(D, F32, name="gfin")
        gmix_c = sb(8, F32, name="gmixc")
        gffn_c = sb(8, F32, name="gffnc")
        gout_c = sb(8, F32, name="goutc")
        negc = sb(8, F32, name="negc")
        fs3 = sb(512, F32, (2, 256), "fs3")
        fsn = sb(512, F32, (2, 256), "fsn")
        const_end = kb.sb_off

        dram_e = Buf(None, "e_all")
        dram_x = Buf(None, "xrep")
        setup_sem = kb.get_dsem()

        class _Stop(Exception):
            pass

        def plan():
            kb.new_phase("W")
            kb.op("pool", lambda e: e.memset(ident.ap, 0.0), writes=(ident,))
            kb.op("pool", lambda e: e.affine_select(out=ident.ap, in_=ident.ap, pattern=[[-1, 128]],
                                                    compare_op=ALU.not_equal, fill=1.0, base=0,
                                                    channel_multiplier=1), writes=(ident,))
            kb.op("pool", lambda e: e.memset(tri8.ap, -8.0), writes=(tri8,))
            kb.op("pool", lambda e: e.affine_select(out=tri8.ap, in_=tri8.ap, pattern=[[-1, 128]],
                                                    compare_op=ALU.is_ge, fill=0.0, base=0,
                                                    channel_multiplier=1), writes=(tri8,))
            kb.op("pool", lambda e: e.memset(ones8.ap, -8.0), writes=(ones8,))
            kb.op("pool", lambda e: e.memset(ones1.ap, 1.0), writes=(ones1,))
            kb.op("pool", lambda e: e.memset(mc.ap, 1.0), writes=(mc,))
            kb.op("pool", lambda e: e.affine_select(out=mc.ap, in_=mc.ap, pattern=[[1, 128]],
                                                    compare_op=ALU.is_gt, fill=0.0, base=0,
                                                    channel_multiplier=-1), writes=(mc,))
            kb.op("pool", lambda e: e.memset(mfar.ap, 1.0), writes=(mfar,))
            kb.op("pool", lambda e: e.memset(mfar.ap[0:64, 64:128], 0.0), writes=(mfar,))
            kb.op("pool", lambda e: e.memset(mnear.ap, 1.0), writes=(mnear,))
            kb.op("pool", lambda e: e.memset(mnear.ap[64:128, 0:64], 0.0), writes=(mnear,))

            setup_sems = []

            def bc_load(dst, src_ap):
                ds = kb.get_dsem(); setup_sems.append(ds)
                kb.q["pool"].append(lambda e, o=dst.ap, i=src_ap, s=ds.sem:
                                    e.dma_start(out=o, in_=i, allow_slow_non_contiguous=True).then_inc(s, 16))
                ds.cnt += 16
                tok = (ds.sem, ds.cnt, "dma")
                dst.w = {id(tok[0]): tok}

            bc_load(gfin_t, g_fin.rearrange("(o n) -> o n", o=1).broadcast_to([128, D]))
            def col3(b):
                return Buf(b.ap.rearrange("p (c o) -> p c o", o=1))
            gm3 = col3(gmix_c); gf3 = col3(gffn_c)
            bc_load(gm3, g_mix.rearrange("(c p o) -> p c o", p=128, o=1))
            bc_load(gf3, g_ffn.rearrange("(c p o) -> p c o", p=128, o=1))
            gmix_c.w = dict(gm3.w); gffn_c.w = dict(gf3.w)
            gout_c_a = Buf(gout_c.ap[:, 0:4].rearrange("p (c o) -> p c o", o=1))
            gout_c_b = Buf(gout_c.ap[:, 4:8].rearrange("p (c o) -> p c o", o=1))
            bc_load(gout_c_a, g_sb.rearrange("(c p o) -> p c o", p=128, o=1))
            bc_load(gout_c_b, g_bd.rearrange("(c p o) -> p c o", p=128, o=1))
            gout_c.w = dict(gout_c_a.w); gout_c.w.update(gout_c_b.w)
            ng3 = col3(negc)
            bc_load(ng3, relb[:, 256:257].rearrange("(x h) o -> x h o", x=1).broadcast_to([128, 8, 1]))
            negc.w = dict(ng3.w)
            kb.op("dve", lambda e: e.tensor_scalar(out=negc.ap, in0=negc.ap, scalar1=-1.0, scalar2=None,
                                                   op0=ALU.mult), reads=(negc,), writes=(negc,))
            kb.dma("pool", [(e_all[:, 0:129], relb[:, 128:257]),
                          (e_all[:, 129:257].rearrange("h (n o) -> h n o", o=1),
                           relb[:, 256:257].rearrange("h (n o) -> h n o", o=1).broadcast_to([8, 128, 1])),
                          (e_all[:, 257:384], relb[:, 1:128])], setup_sem, writes=(dram_e,), slow=True)
            setup_sem2 = kb.get_dsem(); setup_sem3 = kb.get_dsem()
            kb.dma("pool", [(xrep.rearrange("h (r l) -> h r l", l=LX),
                           e_all.rearrange("h (o l) -> h o l", o=1).broadcast_to([8, 129, LX]))],
                   setup_sem2, reads=(dram_e,), writes=(dram_x,), slow=True)
            en_src = bass.AP(xrep.tensor, 0, [[LX - 1, 128], [129 * LX, 8], [1, 256]])
            kb.dma("pool", [(en.ap, en_src)], setup_sem3, reads=(dram_x,), writes=(en,), slow=True)
            for h in range(8):
                kb.op("act", lambda e, h=h: e.activation(out=en.ap[:, h, :], in_=en.ap[:, h, :], func=AF.Exp,
                                                         bias=negc.ap[:, h:h + 1], scale=1.0),
                      reads=(en, negc), writes=(en,))
            for h in range(8):
                kb.op("dve", lambda e, h=h: e.tensor_tensor(out=wn.ap[:, h, :], in0=en.ap[:, h, :],
                                                            in1=mnear.ap, op=ALU.mult),
                      reads=(en, mnear), writes=(wn,))
            kb.op("pool", lambda e: e.memset(fsn.ap, 0.0), writes=(fsn,))
            for h in range(8):
                b_, hh = h % 2, h // 2
                kb.op("dve", lambda e, h=h, b_=b_, hh=hh: e.tensor_copy(
                    out=fs3.ap[:, b_, hh * 64:(hh + 1) * 64], in_=en.ap[:, h, 128:192]),
                    reads=(en,), writes=(fs3,))
                kb.op("dve", lambda e, h=h, b_=b_, hh=hh: e.tensor_copy(
                    out=fsn.ap[0:64, b_, hh * 64:(hh + 1) * 64], in_=en.ap[0:64, h, 0:64]),
                    reads=(en,), writes=(fsn,))

            if stop_after == "W":
                raise _Stop()
            kb.sb_off = const_end
            w_in_sb = sb(8 * 3072, BF16, (8, 3072), "w_in_sb")
            wst = ring(2, 3072, F32, name="wst")
            for c in range(8):
                st = wst[c % 2]
                kb.load([(st.ap, w_in[c * 128:(c + 1) * 128, :])], st)
                kb.op("dve", lambda e, c=c, st=st: e.tensor_scalar(
                    out=w_in_sb.ap[:, c, :], in0=st.ap, scalar1=gmix_c.ap[:, c:c + 1], scalar2=None,
                    op0=ALU.mult), reads=(st, gmix_c), writes=(w_in_sb,))
            p_mark = kb.sb_off
            xin = ring(4, D, F32, name="xin")
            xn = ring(2, D, BF16, name="xn")
            ssb = ring(4, 4, F32, name="ss")
            xnT = ring(2, 8 * 512, BF16, (8, 512), "xnT")
            fmst = ring(2, 16 * 512, BF16, (16, 512), "fmst")
            tmst = ring(4, 512, F32, name="tmst")
            vst = ring(4, 512, BF16, name="vst")
            ps_fm = [pbuf(bank(0)), pbuf(bank(1)), pbuf(bank(2))]
            ps_tm = [pbuf(bank(3)), pbuf(bank(4)), pbuf(bank(5))]
            ps_T = [pbuf(bank(6).bitcast(BF16)), pbuf(bank(7).bitcast(BF16))]
            dram_q = {}

            def dbuf(key):
                if key not in dram_q:
                    dram_q[key] = Buf(None, str(key))
                return dram_q[key]

            tiles = [(i * 512, 512, False) for i in range(NQT)] + [(S, TS, True)]
            cnt = {"blk": 0, "fm": 0, "tm": 0, "tile": 0, "ev": 0, "pt": 0}

            def rms_block(src_rows, xi, xo, s4):
                kb.load([(xi.ap, src_rows)], xi)
                kb.op("act", lambda e: e.activation(out=xo.ap, in_=xi.ap, func=AF.Square,
                                                    accum_out=s4.ap[:, 0:1]),
                      reads=(xi,), writes=(xo, s4))
                kb.op("act", lambda e: e.activation(out=s4.ap[:, 1:2], in_=s4.ap[:, 0:1], func=AF.Ln,
                                                    scale=1.0 / D, bias=EPS), reads=(s4,), writes=(s4,))
                kb.op("act", lambda e: e.activation(out=s4.ap[:, 2:3], in_=s4.ap[:, 1:2], func=AF.Exp,
                                                    scale=-0.5), reads=(s4,), writes=(s4,))
                kb.op("dve", lambda e: e.tensor_scalar(out=xo.ap, in0=xi.ap, scalar1=s4.ap[:, 2:3],
                                                       scalar2=None, op0=ALU.mult),
                      reads=(xi, s4), writes=(xo,))

            def transpose_block(xo, dstT, blk, evac_eng):
                pT = ps_T[cnt["pt"] % 2]; cnt["pt"] += 1
                for c in range(8):
                    kb.op("pe", lambda e, c=c, pT=pT: e.transpose(pT.ap[:, c * 128:(c + 1) * 128],
                                                                 xo.ap[:, c * 128:(c + 1) * 128], ident.ap),
                          reads=(xo, ident), writes=(pT,), signal=(c == 7))
                src = pT.ap.rearrange("p (c n) -> p c n", n=128)
                dst = dstT.ap[:, :, blk * 128:(blk + 1) * 128]
                if evac_eng == "act":
                    kb.op("act", lambda e: e.activation(out=dst, in_=src, func=AF.Copy),
                          reads=(pT,), writes=(dstT,))
                else:
                    kb.op("dve", lambda e: e.tensor_copy(out=dst, in_=src), reads=(pT,), writes=(dstT,))

            def p_norm(ti):
                (t0, n, is_s) = tiles[ti]
                xT = xnT[ti % 2]
                for blk in range(n // 128):
                    bi = cnt["blk"]
                    xi = xin[bi % 4]; xo = xn[bi % 2]; s4 = ssb[bi % 4]
                    rows = x_s[blk * 128:(blk + 1) * 128, :] if is_s else x_p[t0 + blk * 128:t0 + (blk + 1) * 128, :]
                    rms_block(rows, xi, xo, s4)
                    transpose_block(xo, xT, blk, "dve")
                    cnt["blk"] += 1

            def p_mm(ti):
                (t0, n, is_s) = tiles[ti]
                nb = n // 128
                xT = xnT[ti % 2]
                fst = fmst[ti % 2]
                fm_cols = [0 * 512, 1 * 512, 3 * 512, 4 * 512]
                for g4 in (0, 2, 3):
                    for c4 in range(4):
                        oc = g4 * 4 + c4
                        col0 = fm_cols[g4] + c4 * 128
                        pf = ps_fm[cnt["fm"] % 3]; cnt["fm"] += 1
                        for kc in range(8):
                            kb.op("pe", lambda e, kc=kc, pf=pf, col0=col0: e.matmul(
                                pf.ap[:, 0:n], w_in_sb.ap[:, kc, col0:col0 + 128], xT.ap[:, kc, 0:n],
                                start=(kc == 0), stop=(kc == 7)),
                                reads=(w_in_sb, xT), writes=(pf,), signal=(kc == 7))
                        if cnt["ev"] % 3 != 2:
                            kb.op("act", lambda e, pf=pf, oc=oc: e.activation(out=fst.ap[:, oc, 0:n], in_=pf.ap[:, 0:n],
                                                                              func=AF.Copy),
                                  reads=(pf,), writes=(fst,))
                        else:
                            kb.op("dve", lambda e, pf=pf, oc=oc: e.tensor_copy(out=fst.ap[:, oc, 0:n], in_=pf.ap[:, 0:n]),
                                  reads=(pf,), writes=(fst,))
                        cnt["ev"] += 1
                need_bd = is_s or (t0 + n > S - KEEP)
                for blk in range(nb):
                    r0 = t0 + blk * 128
                    groups = [("k_sb", 512), ("v_sb", 1024), ("v_bd", 2560)]
                    if need_bd:
                        groups.append(("k_bd", 2048))
                    for (gname, gcol) in groups:
                        pt = ps_tm[cnt["tm"] % 3]
                        for kc in range(8):
                            kb.op("pe", lambda e, kc=kc, pt=pt, gcol=gcol, blk=blk: e.matmul(
                                pt.ap, xT.ap[:, kc, blk * 128:(blk + 1) * 128], w_in_sb.ap[:, kc, gcol:gcol + 512],
                                start=(kc == 0), stop=(kc == 7)),
                                reads=(w_in_sb, xT), writes=(pt,), signal=(kc == 7))
                        ts_ = tmst[cnt["tm"] % 4]
                        vs_ = vst[cnt["tm"] % 4]
                        cnt["tm"] += 1
                        kb.op("dve", lambda e, pt=pt, ts_=ts_: e.tensor_copy(out=ts_.ap, in_=pt.ap),
                              reads=(pt,), writes=(ts_,))
                        outs = []
                        if gname == "k_sb":
                            outs.append(sbk_s[blk * 128:(blk + 1) * 128, :] if is_s else sbk_p[r0:r0 + 128, :])
                        elif gname == "v_sb":
                            outs.append(sbv_s[blk * 128:(blk + 1) * 128, :] if is_s else sbv_p[r0:r0 + 128, :])
                        elif gname == "k_bd":
                            outs.append(bdk_s[blk * 128:(blk + 1) * 128, :] if is_s
                                        else bdk_p[r0 - (S - KEEP):r0 - (S - KEEP) + 128, :])
                        elif gname == "v_bd" and need_bd:
                            outs.append(bdv_s[blk * 128:(blk + 1) * 128, :] if is_s
                                        else bdv_p[r0 - (S - KEEP):r0 - (S - KEEP) + 128, :])
                        if outs:
                            kb.store([(o, ts_.ap) for o in outs], ts_)
                        if gname == "k_sb":
                            kbf = vs_
                            kb.op("act", lambda e, ts_=ts_, kbf=kbf: e.activation(out=kbf.ap, in_=ts_.ap, func=AF.Copy),
                                  reads=(ts_,), writes=(kbf,))
                            pT = ps_T[cnt["pt"] % 2]; cnt["pt"] += 1
                            for c4 in range(4):
                                kb.op("pe", lambda e, c4=c4, pT=pT, kbf=kbf: e.transpose(
                                    pT.ap[:, c4 * 128:(c4 + 1) * 128], kbf.ap[:, c4 * 128:(c4 + 1) * 128], ident.ap),
                                    reads=(kbf, ident), writes=(pT,), signal=(c4 == 3))
                            kb.op("dve", lambda e, pT=pT, blk=blk: e.tensor_copy(
                                out=fst.ap[:, 4:8, blk * 128:(blk + 1) * 128],
                                in_=pT.ap[:, 0:512].rearrange("p (c n) -> p c n", n=128)),
                                reads=(pT,), writes=(fst,))
                        if gname in ("v_sb", "v_bd"):
                            kb.op("act", lambda e, ts_=ts_, vs_=vs_: e.activation(out=vs_.ap, in_=ts_.ap, func=AF.Copy),
                                  reads=(ts_,), writes=(vs_,))
                            dstv = v_sb if gname == "v_sb" else v_bd
                            kb.store([(dstv[r0:r0 + 128, :], vs_.ap)], vs_, writes=(dbuf((gname, r0 // 128)),))
                pairs = []
                for g4, dst in enumerate((qt_sb, kt_sb, qt_bd, kt_bd)):
                    pairs.append((dst[:, :, t0:t0 + n].rearrange("h p n -> p h n"), fst.ap[:, g4 * 4:(g4 + 1) * 4, 0:n]))
                kb.store(pairs, fst, writes=(dbuf(("fm", ti)),))
            p_norm(0)
            for ti in range(len(tiles)):
                if ti + 1 < len(tiles):
                    p_norm(ti + 1)
                p_mm(ti)
            kb.barrier()
            kb.release_dsems(wst + xin + fmst + tmst + vst)

            if stop_after == "P":
                raise _Stop()
            kb.new_phase("SB")
            kb.sb_off = const_end
            qkv = []
            for i in range(2):
                qkv.append((sb(S, BF16, name=f"QT{i}"), sb(S, BF16, name=f"KT{i}"),
                            sb(NKB * 128, BF16, (NKB, 128), f"V{i}")))
            l_r = ring(3, 1024, BF16, (2, 512), "L")
            w_r = ring(3, 1024, BF16, (2, 512), "w")
            ra_r = ring(3, 1024, BF16, (2, 512), "ra")
            ost = ring(2, 512, BF16, name="ost")
            zc_ps = [pbuf(bank(2 * i_, 2).rearrange("p (b n) -> p b n", n=512)) for i_ in range(3)]
            o_ps = [pbuf(bank(6)), pbuf(bank(7))]
            dram_ot = {}

            def load_qkv(hp, slot, qsrc, ksrc, vsrc, vkey):
                QT, KT, V = qkv[slot]
                rd = [dbuf(("fm", t)) for t in range(NQT)]
                kb.load([(QT.ap, qsrc[hp, :, 0:S])], QT, reads=rd)
                kb.load([(KT.ap, ksrc[hp, :, 0:S])], KT, reads=rd)
                rdv = [dbuf((vkey, b)) for b in range(NKB)]
                pairs = []
                for b0 in range(0, NKB, 16):
                    b1 = min(NKB, b0 + 16)
                    pairs.append((V.ap[:, b0:b1, :],
                                  vsrc[b0 * 128:b1 * 128, hp * 128:(hp + 1) * 128].rearrange("(b p) f -> p b f", p=128)))
                kb.load(pairs, V, reads=rdv)

            its = []
            for hp in range(4):
                for i in range(NQT):
                    js = list(range(4 * i + 3, -1, -1))
                    for n_, j in enumerate(js):
                        m = j - 4 * i
                        c0 = 128 * m if m > 0 else 0
                        its.append(dict(hp=hp, i=i, j=j, c0=c0, diag=(m >= 0), first=(n_ == 0),
                                        last=(n_ == len(js) - 1), slot=hp % 2, qt=hp * NQT + i))
            NIT = len(its)
            load_qkv(0, 0, qt_sb, kt_sb, v_sb, "v_sb")
            loaded = {0}

            def st_qk(k):
                it = its[k]
                QT, KT, V = qkv[it["slot"]]
                z = zc_ps[k % 3]; c0 = it["c0"]; i = it["i"]; j = it["j"]
                for b in range(2):
                    kb.op("pe", lambda e, b=b: e.matmul(
                        z.ap[:, b, c0:512], KT.ap[b * 64:(b + 1) * 64, j * 128:(j + 1) * 128],
                        QT.ap[b * 64:(b + 1) * 64, i * 512 + c0:(i + 1) * 512], start=True, stop=True),
                        reads=(QT, KT), writes=(z,), signal=(b == 1))

            def st_l(k):
                it = its[k]; z = zc_ps[k % 3]; lb = l_r[k % 3]; c0 = it["c0"]
                if c0 > 0:
                    kb.op("pool", lambda e: e.memset(lb.ap[:, :, 0:c0], 0.0), writes=(lb,))
                kb.op("act", lambda e: e.activation(out=lb.ap[:, :, c0:512], in_=z.ap[:, :, c0:512],
                                                    func=AF.Softplus, scale=0.125), reads=(z,), writes=(lb,))
                if it["diag"]:
                    for b in range(2):
                        kb.op("dve", lambda e, b=b: e.tensor_tensor(out=lb.ap[:, b, c0:c0 + 128],
                                                                    in0=lb.ap[:, b, c0:c0 + 128], in1=mc.ap,
                                                                    op=ALU.mult), reads=(lb, mc), writes=(lb,))

            def st_ra(k):
                it = its[k]
                if it["last"]:
                    return
                lb = l_r[k % 3]; rn = ra_r[(k + 1) % 3]; rc = ra_r[k % 3]
                if it["first"]:
                    kb.op("dve", lambda e: e.tensor_copy(out=rn.ap, in_=lb.ap), reads=(lb,), writes=(rn,))
                else:
                    kb.op("dve", lambda e: e.tensor_tensor(out=rn.ap, in0=rc.ap, in1=lb.ap, op=ALU.add),
                          reads=(rc, lb), writes=(rn,))

            def st_c(k):
                it = its[k]; lb = l_r[k % 3]; rc = ra_r[k % 3]; cp = zc_ps[k % 3]
                for b in range(2):
                    kb.op("pe", lambda e, b=b: e.matmul(cp.ap[:, b, :], tri8.ap, lb.ap[:, b, :], start=False,
                                                        stop=it["first"], skip_group_check=True),
                          reads=(tri8, lb), writes=(cp,), signal=(it["first"] and b == 1))
                    if not it["first"]:
                        kb.op("pe", lambda e, b=b: e.matmul(cp.ap[:, b, :], ones8.ap, rc.ap[:, b, :], start=False,
                                                            stop=True, skip_group_check=True),
                              reads=(ones8, rc), writes=(cp,), signal=(b == 1))

            def st_w(k):
                it = its[k]; cp = zc_ps[k % 3]; wb = w_r[k % 3]; c0 = it["c0"]
                if c0 > 0:
                    kb.op("pool", lambda e: e.memset(wb.ap[:, :, 0:c0], 0.0), writes=(wb,))
                kb.op("act", lambda e: e.activation(out=wb.ap[:, :, c0:512], in_=cp.ap[:, :, c0:512],
                                                    func=AF.Softplus, scale=0.125, bias=-SHIFT),
                      reads=(cp,), writes=(wb,))
                if it["diag"]:
                    for b in range(2):
                        kb.op("dve", lambda e, b=b: e.tensor_tensor(out=wb.ap[:, b, c0:c0 + 128],
                                                                    in0=wb.ap[:, b, c0:c0 + 128], in1=mc.ap,
                                                                    op=ALU.mult), reads=(wb, mc), writes=(wb,))

            def st_pv(k):
                it = its[k]; wb = w_r[k % 3]; QT, KT, V = qkv[it["slot"]]
                op_ = o_ps[it["qt"] % 2]; j = it["j"]
                for b in range(2):
                    kb.op("pe", lambda e, b=b: e.matmul(op_.ap[b * 64:(b + 1) * 64, :], V.ap[:, j, b * 64:(b + 1) * 64],
                                                        wb.ap[:, b, :], start=it["first"], stop=it["last"]),
                          reads=(V, wb), writes=(op_,), signal=(b == 1))
                if it["last"]:
                    os_ = ost[it["qt"] % 2]
                    kb.op("dve", lambda e: e.tensor_scalar(out=os_.ap, in0=op_.ap, scalar1=ESHIFT, scalar2=None,
                                                           op0=ALU.mult), reads=(op_,), writes=(os_,))
                    t0 = it["i"] * 512
                    d_ = Buf(None); dram_ot[(it["hp"], it["i"])] = d_
                    kb.store([(ot[it["hp"], :, t0:t0 + 512], os_.ap)], os_, writes=(d_,))

            st_qk(0)
            for r in range(NIT + 3):
                if 0 <= r - 2 < NIT:
                    it = its[r - 2]
                    if it["i"] == 0 and it["first"] and it["hp"] + 1 < 4 and (it["hp"] + 1) not in loaded:
                        load_qkv(it["hp"] + 1, (it["hp"] + 1) % 2, qt_sb, kt_sb, v_sb, "v_sb")
                        loaded.add(it["hp"] + 1)
                if r + 1 < NIT:
                    st_qk(r + 1)
                if r < NIT:
                    st_l(r)
                if 0 <= r - 1 < NIT:
                    st_w(r - 1)
                if r < NIT:
                    st_ra(r)
                    st_c(r)
                if 0 <= r - 2 < NIT:
                    st_pv(r - 2)
            kb.barrier()

            if stop_after == "SB":
                raise _Stop()
            kb.new_phase("BD")
            wb_r = ring(3, 1024, BF16, (2, 512), "wb")
            rd_r = ring(2, 512, F32, name="rden")
            zb_ps = [pbuf(bank(0, 2).rearrange("p (b n) -> p b n", n=512)),
                     pbuf(bank(2, 2).rearrange("p (b n) -> p b n", n=512))]
            ob_ps = [pbuf(bank(4)), pbuf(bank(5))]
            dn_ps = [pbuf(bank(6)), pbuf(bank(7))]
            load_qkv(0, 0, qt_bd, kt_bd, v_bd, "v_bd")
            brecs = []
            for hp in range(4):
                for i in range(NQT):
                    blocks = [(4 * i + m, 128 * m, 512, "near", m) for m in range(4)]
                    if i > 0:
                        blocks += [(4 * i - 4 + jj, 0, 128 * (jj + 1), "far", jj) for jj in range(4)]
                    for n_, (j, a, b_, kind, m) in enumerate(blocks):
                        brecs.append(dict(hp=hp, i=i, j=j, a=a, b_=b_, kind=kind, m=m, first=(n_ == 0),
                                          last=(n_ == len(blocks) - 1), qi=hp * NQT + i))
            NBR = len(brecs)

            def bd_qk(n):
                rc = brecs[n]; QT, KT, V = qkv[rc["hp"] % 2]; z = zb_ps[n % 2]
                j, a, b_, i = rc["j"], rc["a"], rc["b_"], rc["i"]
                for b in range(2):
                    kb.op("pe", lambda e, b=b: e.matmul(
                        z.ap[:, b, a:b_], KT.ap[b * 64:(b + 1) * 64, j * 128:(j + 1) * 128],
                        QT.ap[b * 64:(b + 1) * 64, i * 512 + a:i * 512 + b_], start=True, stop=True),
                        reads=(QT, KT), writes=(z,), signal=(b == 1))

            wbh = [[Buf(w_.ap[:, b]) for b in range(2)] for w_ in wb_r]

            def bd_exp(n):
                rc = brecs[n]; z = zb_ps[n % 2]; wb = wb_r[n % 3]; wh = wbh[n % 3]
                a, b_, kind, m, hp = rc["a"], rc["b_"], rc["kind"], rc["m"], rc["hp"]
                kb.op("act", lambda e: e.activation(out=wb.ap[:, :, a:b_], in_=z.ap[:, :, a:b_], func=AF.Exp,
                                                    scale=0.125), reads=(z,), writes=(wh[0], wh[1]))
                for b in range(2):
                    h = 2 * hp + b
                    if kind == "near":
                        wd = min(256, 512 - a)
                        kb.op("dve", lambda e, b=b, wd=wd, h=h: e.tensor_tensor(
                            out=wb.ap[:, b, a:a + wd], in0=wb.ap[:, b, a:a + wd], in1=wn.ap[:, h, 0:wd],
                            op=ALU.mult), reads=(wh[b], wn), writes=(wh[b],))
                    else:
                        kb.op("dve", lambda e, b=b: e.tensor_tensor(
                            out=wb.ap[:, b, b_ - 128:b_], in0=wb.ap[:, b, b_ - 128:b_], in1=mfar.ap,
                            op=ALU.mult), reads=(wh[b], mfar), writes=(wh[b],))
                        if m == 3:
                            kb.op("dve", lambda e, b=b, h=h: e.tensor_tensor(
                                out=wb.ap[:, b, 0:128], in0=wb.ap[:, b, 0:128], in1=en.ap[:, h, 128:256],
                                op=ALU.mult), reads=(wh[b], en), writes=(wh[b],))

            def bd_pv(n):
                rc = brecs[n]; QT, KT, V = qkv[rc["hp"] % 2]; wb = wb_r[n % 3]; wh = wbh[n % 3]
                j, a, b_, qi = rc["j"], rc["a"], rc["b_"], rc["qi"]
                op_ = ob_ps[qi % 2]; dn = dn_ps[qi % 2]
                for b in range(2):
                    kb.op("pe", lambda e, b=b: e.matmul(
                        op_.ap[b * 64:(b + 1) * 64, a:b_], V.ap[:, j, b * 64:(b + 1) * 64], wb.ap[:, b, a:b_],
                        start=rc["first"], stop=rc["last"]), reads=(V, wh[b]), writes=(op_,), signal=False)
                for b in range(2):
                    kb.op("pe", lambda e, b=b: e.matmul(
                        dn.ap[b * 64:(b + 1) * 64, a:b_], ones1.ap, wb.ap[:, b, a:b_],
                        start=rc["first"], stop=rc["last"]), reads=(ones1, wh[b]), writes=(dn,), signal=(b == 1))
                if rc["last"]:
                    rd = rd_r[qi % 2]; os_ = ost[qi % 2]
                    kb.op("dve", lambda e: e.reciprocal(out=rd.ap, in_=dn.ap), reads=(dn,), writes=(rd,))
                    kb.op("dve", lambda e: e.tensor_tensor(out=os_.ap, in0=op_.ap, in1=rd.ap, op=ALU.mult),
                          reads=(op_, rd), writes=(os_,))
                    d_ = Buf(None); dram_ot[(4 + rc["hp"], rc["i"])] = d_
                    kb.store([(ot[4 + rc["hp"], :, rc["i"] * 512:(rc["i"] + 1) * 512], os_.ap)], os_, writes=(d_,))

            bloaded = {0}
            bd_qk(0)
            for r in range(NBR + 1):
                if 0 <= r - 1 < NBR:
                    rc = brecs[r - 1]
                    if rc["i"] == 0 and rc["first"] and rc["hp"] + 1 < 4 and (rc["hp"] + 1) not in bloaded:
                        load_qkv(rc["hp"] + 1, (rc["hp"] + 1) % 2, qt_bd, kt_bd, v_bd, "v_bd")
                        bloaded.add(rc["hp"] + 1)
                if r + 1 < NBR:
                    bd_qk(r + 1)
                if r < NBR:
                    bd_exp(r)
                if 0 <= r - 1 < NBR:
                    bd_pv(r - 1)
            kb.barrier()
            kb.release_dsems([b for t in qkv for b in t] + ost)

            if stop_after == "BD":
                raise _Stop()
            kb.new_phase("SA")
            kb.sb_off = const_end
            ktc = sb(2 * 4 * PAST, BF16, (2, 4, PAST), "ktc")
            vc = sb(2 * 16 * W, BF16, (2, 16, W), "vc")
            ktb = sb(2 * 4 * BROWS, BF16, (2, 4, BROWS), "ktb")
            vbc = sb(2 * 4 * W, BF16, (2, 4, W), "vbc")
            cst = ring(2, 4 * W, F32, (4, W), "cst")
            cbf = ring(2, 4 * W, BF16, (4, W), "cbf")
            qs_sb = sb(4 * TS, BF16, (4, TS), "qs_sb")
            ks_sb = sb(4 * 2 * 128, BF16, (4, 2, 128), "ks_sb")
            qs_bd = sb(4 * TS, BF16, (4, TS), "qs_bd")
            ks_bd = sb(4 * 2 * 128, BF16, (4, 2, 128), "ks_bd")
            vs_sb = sb(2 * W, BF16, (2, W), "vs_sb")
            vs_bd = sb(2 * W, BF16, (2, W), "vs_bd")
            sl_r = ring(3, 512, BF16, (2, 256), "sl")
            sw_r = ring(3, 512, BF16, (2, 256), "sw")
            sra_r = ring(3, 512, BF16, (2, 256), "sra")
            sos = ring(2, 256, BF16, (4, 64), "sos")
            srd = ring(2, 256, F32, name="srd")
            fm_s = [dbuf(("fm", NQT))]
            kb.op("pool", lambda e: e.memset(ks_sb.ap, 0.0), writes=(ks_sb,))
            kb.op("pool", lambda e: e.memset(ks_bd.ap, 0.0), writes=(ks_bd,))
            kb.op("pool", lambda e: e.memset(vs_sb.ap, 0.0), writes=(vs_sb,))
            kb.op("pool", lambda e: e.memset(vs_bd.ap, 0.0), writes=(vs_bd,))
            kb.load([(qs_sb.ap, qt_sb[:, :, S:S + TS].rearrange("h p n -> p h n"))], qs_sb, reads=fm_s)
            kb.load([(qs_bd.ap, qt_bd[:, :, S:S + TS].rearrange("h p n -> p h n"))], qs_bd, reads=fm_s)
            kb.load([(ks_sb.ap[:, :, s, 0:64], kt_sb[:, :, S + s * 64:S + (s + 1) * 64].rearrange("h p n -> p h n"))
                     for s in range(2)], ks_sb, reads=fm_s)
            kb.load([(ks_bd.ap[:, :, s, 0:64], kt_bd[:, :, S + s * 64:S + (s + 1) * 64].rearrange("h p n -> p h n"))
                     for s in range(2)], ks_bd, reads=fm_s)
            kb.load([(vs_sb.ap[0:64, s, :], v_sb[S + s * 64:S + (s + 1) * 64, :]) for s in range(2)], vs_sb,
                    reads=[dbuf(("v_sb", S // 128))])
            kb.load([(vs_bd.ap[0:64, s, :], v_bd[S + s * 64:S + (s + 1) * 64, :]) for s in range(2)], vs_bd,
                    reads=[dbuf(("v_bd", S // 128))])
            sT = [pbuf(bank(7).bitcast(BF16))]
            ccnt = 0
            tcnt = 0
            for (ksrc, vsrc, nblk, kdst, vdst) in ((csk, csv, 16, ktc, vc), (cbk, cbv, 4, ktb, vbc)):
                for s in range(2):
                    for g in range(nblk // 4):
                        st = cst[ccnt % 2]; cb = cbf[ccnt % 2]; ccnt += 1
                        kb.load([(st.ap, ksrc[s, g * 512:(g + 1) * 512, :].rearrange("(b p) f -> p b f", p=128))], st)
                        kb.op("dve", lambda e, st=st, cb=cb: e.tensor_copy(out=cb.ap, in_=st.ap), reads=(st,), writes=(cb,))
                        for hp in range(4):
                            pT = sT[0]; tcnt += 1
                            for b4 in range(4):
                                kb.op("pe", lambda e, pT=pT, cb=cb, b4=b4, hp=hp: e.transpose(
                                    pT.ap[:, b4 * 128:(b4 + 1) * 128], cb.ap[:, b4, hp * 128:(hp + 1) * 128], ident.ap),
                                    reads=(cb, ident), writes=(pT,), signal=(b4 == 3))
                            kb.op("act", lambda e, pT=pT, s=s, hp=hp, g=g, kdst=kdst: e.activation(
                                out=kdst.ap[:, s, hp, g * 512:(g + 1) * 512], in_=pT.ap[:, 0:512], func=AF.Copy),
                                reads=(pT,), writes=(kdst,))
                        st2 = cst[ccnt % 2]; ccnt += 1
                        kb.load([(st2.ap, vsrc[s, g * 512:(g + 1) * 512, :].rearrange("(b p) f -> p b f", p=128))], st2)
                        kb.op("dve", lambda e, st2=st2, s=s, g=g, vdst=vdst: e.tensor_copy(
                            out=vdst.ap[:, s, g * 4:(g + 1) * 4, :], in_=st2.ap), reads=(st2,), writes=(vdst,))

            zs_ps = [pbuf(bank(2 * i_, 2).rearrange("p (b n) -> p b n", n=512)) for i_ in range(3)]
            os_ps = [pbuf(bank(6))]
            dns_ps = [pbuf(bank(6)[:, 256:512])]
            dram_ots = Buf(None)
            sits = []
            for s in range(2):
                for n_, j in enumerate([16] + list(range(15, -1, -1))):
                    sits.append(dict(s=s, j=j, first=(n_ == 0), last=(n_ == 16)))
            NS = len(sits)

            def kt_blk(it, b, hh):
                if it["j"] == 16:
                    return ks_sb.ap[b * 64:(b + 1) * 64, hh, it["s"], :]
                return ktc.ap[b * 64:(b + 1) * 64, it["s"], hh, it["j"] * 128:(it["j"] + 1) * 128]

            def v_blk(it, h):
                if it["j"] == 16:
                    return vs_sb.ap[:, it["s"], h * 64:(h + 1) * 64]
                return vc.ap[:, it["s"], it["j"], h * 64:(h + 1) * 64]

            def ss_qk(k):
                it = sits[k]; z = zs_ps[k % 3]; s = it["s"]
                for hh in range(4):
                    for b in range(2):
                        kb.op("pe", lambda e, b=b, hh=hh: e.matmul(
                            z.ap[:, b, hh * 64:(hh + 1) * 64], kt_blk(it, b, hh),
                            qs_sb.ap[b * 64:(b + 1) * 64, hh, s * 64:(s + 1) * 64], start=(hh == 0), stop=(hh == 3),
                            skip_group_check=True),
                            reads=(ktc, ks_sb, qs_sb), writes=(z,), signal=(hh == 3 and b == 1))

            def ss_l(k):
                it = sits[k]; z = zs_ps[k % 3]; lb = sl_r[k % 3]
                kb.op("act", lambda e: e.activation(out=lb.ap, in_=z.ap[:, :, 0:256], func=AF.Softplus, scale=0.125),
                      reads=(z,), writes=(lb,))
                if it["j"] == 16:
                    for b in range(2):
                        for hh in range(4):
                            kb.op("dve", lambda e, b=b, hh=hh: e.tensor_tensor(
                                out=lb.ap[:, b, hh * 64:(hh + 1) * 64], in0=lb.ap[:, b, hh * 64:(hh + 1) * 64],
                                in1=mc.ap[:, 0:64], op=ALU.mult), reads=(lb, mc), writes=(lb,))

            def ss_ra(k):
                it = sits[k]
                if it["last"]:
                    return
                lb = sl_r[k % 3]; rn = sra_r[(k + 1) % 3]; rc = sra_r[k % 3]
                if it["first"]:
                    kb.op("dve", lambda e: e.tensor_copy(out=rn.ap, in_=lb.ap), reads=(lb,), writes=(rn,))
                else:
                    kb.op("dve", lambda e: e.tensor_tensor(out=rn.ap, in0=rc.ap, in1=lb.ap, op=ALU.add),
                          reads=(rc, lb), writes=(rn,))

            def ss_c(k):
                it = sits[k]; lb = sl_r[k % 3]; rc = sra_r[k % 3]; cp = zs_ps[k % 3]
                for b in range(2):
                    kb.op("pe", lambda e, b=b: e.matmul(cp.ap[:, b, 0:256], tri8.ap, lb.ap[:, b, :], start=False,
                                                        stop=it["first"], skip_group_check=True),
                          reads=(tri8, lb), writes=(cp,), signal=(it["first"] and b == 1))
                    if not it["first"]:
                        kb.op("pe", lambda e, b=b: e.matmul(cp.ap[:, b, 0:256], ones8.ap, rc.ap[:, b, :], start=False,
                                                            stop=True, skip_group_check=True),
                              reads=(ones8, rc), writes=(cp,), signal=(b == 1))

            def ss_w(k):
                it = sits[k]; cp = zs_ps[k % 3]; wb = sw_r[k % 3]
                kb.op("act", lambda e: e.activation(out=wb.ap, in_=cp.ap[:, :, 0:256], func=AF.Softplus, scale=0.125,
                                                    bias=-SHIFT), reads=(cp,), writes=(wb,))
                if it["j"] == 16:
                    for b in range(2):
                        for hh in range(4):
                            kb.op("dve", lambda e, b=b, hh=hh: e.tensor_tensor(
                                out=wb.ap[:, b, hh * 64:(hh + 1) * 64], in0=wb.ap[:, b, hh * 64:(hh + 1) * 64],
                                in1=mc.ap[:, 0:64], op=ALU.mult), reads=(wb, mc), writes=(wb,))

            def ss_pv(k):
                it = sits[k]; wb = sw_r[k % 3]; op_ = os_ps[0]; s = it["s"]
                for hh in range(4):
                    for b in range(2):
                        h = 2 * hh + b
                        kb.op("pe", lambda e, b=b, hh=hh, h=h: e.matmul(
                            op_.ap[b * 64:(b + 1) * 64, hh * 64:(hh + 1) * 64], v_blk(it, h),
                            wb.ap[:, b, hh * 64:(hh + 1) * 64], start=(it["first"] and hh == 0),
                            stop=(it["last"] and hh == 3), skip_group_check=True),
                            reads=(vc, vs_sb, wb), writes=(op_,), signal=(hh == 3 and b == 1))
                if it["last"]:
                    os_ = sos[s % 2]
                    kb.op("dve", lambda e: e.tensor_scalar(out=os_.ap.rearrange("p h q -> p (h q)"), in0=op_.ap[:, 0:256],
                                                           scalar1=ESHIFT, scalar2=None, op0=ALU.mult),
                          reads=(op_,), writes=(os_,))
                    kb.store([(ot[0:4, :, S + s * 64:S + (s + 1) * 64].rearrange("h p n -> p h n"), os_.ap)], os_,
                             writes=(dram_ots,))

            ss_qk(0)
            for r in range(NS + 3):
                if r + 1 < NS:
                    ss_qk(r + 1)
                if r < NS:
                    ss_l(r)
                if 0 <= r - 1 < NS:
                    ss_w(r - 1)
                if r < NS:
                    ss_ra(r)
                    ss_c(r)
                if 0 <= r - 2 < NS:
                    ss_pv(r - 2)
            scnt = 0
            for s in range(2):
                op_ = os_ps[0]; dn = dns_ps[0]
                order = [4, 3, 2, 1, 0]
                for n_, j in enumerate(order):
                    z = zs_ps[scnt % 2]; wb = sw_r[scnt % 3]; scnt += 1
                    for hh in range(4):
                        for b in range(2):
                            kt_ap = (ks_bd.ap[b * 64:(b + 1) * 64, hh, s, :] if j == 4
                                     else ktb.ap[b * 64:(b + 1) * 64, s, hh, j * 128:(j + 1) * 128])
                            kb.op("pe", lambda e, b=b, hh=hh, z=z, kt_ap=kt_ap: e.matmul(
                                z.ap[:, b, hh * 64:(hh + 1) * 64], kt_ap,
                                qs_bd.ap[b * 64:(b + 1) * 64, hh, s * 64:(s + 1) * 64], start=True, stop=True),
                                reads=(ktb, ks_bd, qs_bd), writes=(z,), signal=(hh == 3 and b == 1))
                    kb.op("act", lambda e, z=z, wb=wb: e.activation(out=wb.ap, in_=z.ap[:, :, 0:256], func=AF.Exp,
                                                                    scale=0.125), reads=(z,), writes=(wb,))
                    if j == 4:
                        kb.op("dve", lambda e, wb=wb: e.tensor_tensor(out=wb.ap, in0=wb.ap, in1=fsn.ap, op=ALU.mult),
                              reads=(wb, fsn), writes=(wb,))
                    elif j == 3:
                        kb.op("dve", lambda e, wb=wb: e.tensor_tensor(out=wb.ap, in0=wb.ap, in1=fs3.ap, op=ALU.mult),
                              reads=(wb, fs3), writes=(wb,))
                    fst_, lst_ = (n_ == 0), (n_ == 4)
                    for hh in range(4):
                        for b in range(2):
                            h = 2 * hh + b
                            v_ap = vs_bd.ap[:, s, h * 64:(h + 1) * 64] if j == 4 else vbc.ap[:, s, j, h * 64:(h + 1) * 64]
                            kb.op("pe", lambda e, b=b, hh=hh, wb=wb, v_ap=v_ap, fst_=fst_, lst_=lst_: e.matmul(
                                op_.ap[b * 64:(b + 1) * 64, hh * 64:(hh + 1) * 64], v_ap,
                                wb.ap[:, b, hh * 64:(hh + 1) * 64], start=(fst_ and hh == 0),
                                stop=(lst_ and hh == 3), skip_group_check=True),
                                reads=(vbc, vs_bd, wb), writes=(op_,), signal=False)
                    for b in range(2):
                        kb.op("pe", lambda e, b=b, wb=wb, fst_=fst_, lst_=lst_: e.matmul(
                            dn.ap[b * 64:(b + 1) * 64, :], ones1.ap, wb.ap[:, b, :], start=False, stop=lst_,
                            skip_group_check=True), reads=(ones1, wb), writes=(dn, op_), signal=(b == 1))
                rd = srd[s % 2]; os_ = sos[s % 2]
                kb.op("dve", lambda e, rd=rd: e.reciprocal(out=rd.ap, in_=dn.ap), reads=(dn, op_), writes=(rd,))
                kb.op("dve", lambda e, rd=rd, os_=os_: e.tensor_tensor(out=os_.ap.rearrange("p h q -> p (h q)"),
                                                                       in0=op_.ap[:, 0:256], in1=rd.ap, op=ALU.mult),
                      reads=(op_, rd), writes=(os_,))
                kb.store([(ot[4:8, :, S + s * 64:S + (s + 1) * 64].rearrange("h p n -> p h n"), os_.ap)], os_,
                         writes=(dram_ots,))
            kb.barrier()
            kb.release_dsems(cst + sos + [qs_sb, ks_sb, qs_bd, ks_bd, vs_sb, vs_bd])

            if stop_after == "SA":
                raise _Stop()
            kb.new_phase("O")
            kb.sb_off = const_end
            w_out_sb = sb(8 * D, BF16, (8, D), "w_out_sb")
            wst2 = ring(2, 4 * D, F32, (4, D), "wst2")
            for g in range(2):
                st = wst2[g % 2]
                kb.load([(st.ap, w_out[g * 512:(g + 1) * 512, :].rearrange("(c p) n -> p c n", p=128))], st)
                for c in range(4):
                    kb.op("dve", lambda e, st=st, c=c, g=g: e.tensor_scalar(
                        out=w_out_sb.ap[:, g * 4 + c, :], in0=st.ap[:, c, :], scalar1=gout_c.ap[:, g * 4 + c:g * 4 + c + 1],
                        scalar2=None, op0=ALU.mult), reads=(st, gout_c), writes=(w_out_sb,))
            oT_r = ring(2, 8 * 512, BF16, (8, 512), "oT")
            osq_r = ring(2, 8 * 512, BF16, (8, 512), "osq")
            xo_r = ring(3, D, F32, name="xo")
            h_r = ring(3, D, F32, name="h")
            rs_r = ring(4, 8, F32, name="rs")
            st_ps = [pbuf(bank(0)[:, 0:2]), pbuf(bank(1)[:, 0:2])]
            oo_ps = [(pbuf(bank(2, 2)), pbuf(bank(4, 2)))]
            dram_h = {}
            ocnt = 0
            for ti, (t0, n, is_s) in enumerate(tiles):
                oT = oT_r[ti % 2]; osq = osq_r[ti % 2]
                if is_s:
                    rd = [dram_ots]
                else:
                    rd = [dram_ot[(c, ti)] for c in range(8)]
                kb.load([(oT.ap[:, :, 0:n], ot[:, :, t0:t0 + n].rearrange("h p n -> p h n"))], oT, reads=rd)
                kb.op("act", lambda e, oT=oT, osq=osq: e.activation(out=osq.ap[:, :, 0:n], in_=oT.ap[:, :, 0:n],
                                                                    func=AF.Square), reads=(oT,), writes=(osq,))
                for blk in range(n // 128):
                    r0 = t0 + blk * 128
                    xi = xo_r[ocnt % 3]; hb = h_r[ocnt % 3]; rs = rs_r[ocnt % 4]; sp_ = st_ps[ocnt % 2]
                    pa, pb_ = oo_ps[0]
                    ocnt += 1
                    rows = x_s[blk * 128:(blk + 1) * 128, :] if is_s else x_p[r0:r0 + 128, :]
                    kb.load([(xi.ap, rows)], xi)
                    for g in range(2):
                        for c in range(4):
                            kb.op("pe", lambda e, g=g, c=c, sp_=sp_, osq=osq, blk=blk: e.matmul(
                                sp_.ap[:, g:g + 1], osq.ap[:, g * 4 + c, blk * 128:(blk + 1) * 128], ones1.ap[:, 0:1],
                                start=(c == 0), stop=(c == 3), skip_group_check=True),
                                reads=(osq, ones1), writes=(sp_,), signal=(g == 1 and c == 3))
                    kb.op("act", lambda e, rs=rs, sp_=sp_: e.activation(out=rs.ap[:, 0:2], in_=sp_.ap, func=AF.Ln,
                                                                        scale=1.0 / W, bias=EPS),
                          reads=(sp_,), writes=(rs,))
                    kb.op("act", lambda e, rs=rs: e.activation(out=rs.ap[:, 2:4], in_=rs.ap[:, 0:2], func=AF.Exp,
                                                               scale=-0.5), reads=(rs,), writes=(rs,))
                    for g, pg in ((0, pa), (1, pb_)):
                        for nh in range(2):
                            for c in range(4):
                                kb.op("pe", lambda e, g=g, nh=nh, c=c, pg=pg, oT=oT, blk=blk: e.matmul(
                                    pg.ap[:, nh * 512:(nh + 1) * 512], oT.ap[:, g * 4 + c, blk * 128:(blk + 1) * 128],
                                    w_out_sb.ap[:, g * 4 + c, nh * 512:(nh + 1) * 512], start=(c == 0), stop=(c == 3)),
                                    reads=(oT, w_out_sb), writes=(pg,), signal=(nh == 1 and c == 3))
                    kb.op("dve", lambda e, hb=hb, pa=pa, rs=rs, xi=xi: e.scalar_tensor_tensor(
                        out=hb.ap, in0=pa.ap, scalar=rs.ap[:, 2:3], in1=xi.ap, op0=ALU.mult, op1=ALU.add),
                        reads=(pa, rs, xi), writes=(hb,))
                    kb.op("dve", lambda e, hb=hb, pb_=pb_, rs=rs: e.scalar_tensor_tensor(
                        out=hb.ap, in0=pb_.ap, scalar=rs.ap[:, 3:4], in1=hb.ap, op0=ALU.mult, op1=ALU.add),
                        reads=(pb_, rs, hb), writes=(hb,))
                    d_ = Buf(None); dram_h[r0 // 128] = d_
                    kb.store([(hs[r0:r0 + 128, :], hb.ap)], hb, writes=(d_,))
            kb.barrier()
            kb.release_dsems(wst2 + oT_r + xo_r + h_r)

            if stop_after == "O":
                raise _Stop()
            kb.new_phase("F")
            kb.sb_off = const_end
            w_up_sb = sb(8 * DFF, BF16, (8, DFF), "w_up_sb")
            w_dn_sb = sb(32 * D, BF16, (32, D), "w_dn_sb")
            s4f = ring(4, 4, F32, name="s4f")
            s4g = ring(4, 4, F32, name="s4g")
            wst3_off = kb.sb_off
            wst3 = ring(2, DFF, F32, name="wst3")
            for c in range(8):
                st = wst3[c % 2]
                kb.load([(st.ap, w_up[c * 128:(c + 1) * 128, :])], st)
                kb.op("dve", lambda e, st=st, c=c: e.tensor_scalar(
                    out=w_up_sb.ap[:, c, :], in0=st.ap, scalar1=gffn_c.ap[:, c:c + 1], scalar2=None, op0=ALU.mult),
                    reads=(st, gffn_c), writes=(w_up_sb,))
            for g in range(8):
                st = wst3[g % 2]
                stv = st.ap.rearrange("p (c n) -> p c n", n=D)
                kb.load([(stv, w_down[g * 512:(g + 1) * 512, :].rearrange("(c p) n -> p c n", p=128))], st)
                kb.op("act" if g % 2 else "dve",
                      (lambda e, stv=stv, g=g: e.activation(out=w_dn_sb.ap[:, g * 4:(g + 1) * 4, :], in_=stv, func=AF.Copy))
                      if g % 2 else
                      (lambda e, stv=stv, g=g: e.tensor_copy(out=w_dn_sb.ap[:, g * 4:(g + 1) * 4, :], in_=stv)),
                      reads=(st,), writes=(w_dn_sb,))
            FT = 256
            kb.barrier()
            kb.release_dsems(wst3)
            kb.sb_off = wst3_off
            hin = ring(4, D, F32, name="hin")
            junk = ring(1, D, BF16, name="junk")
            hn_r = ring(2, D, BF16, name="hn")
            hnT = ring(2, 8 * FT, BF16, (8, FT), "hnT")
            aT = ring(1, 32 * FT, BF16, (32, FT), "aT")
            rl = ring(3, FT, BF16, name="rl")
            up_ps = [pbuf(bank(0)), pbuf(bank(1))]
            dn_ps2 = [pbuf(bank(2, 2)), pbuf(bank(4, 2))]
            ps_T = [pbuf(bank(6).bitcast(BF16)), pbuf(bank(7).bitcast(BF16))]
            ftiles = []
            for (t0, n, is_s) in tiles:
                for a in range(0, n, FT):
                    ftiles.append((t0 + a, min(FT, n - a), is_s, a))
            fstate = {"fcnt": 0, "ucnt": 0}
            fh = {}

            def f_norm(fi):
                (t0, n, is_s, a0) = ftiles[fi]
                hT = hnT[fi % 2]
                hbufs = []
                for blk in range(n // 128):
                    fcnt = fstate["fcnt"]
                    r0 = t0 + blk * 128
                    hi = hin[fcnt % 4]; ho = hn_r[fcnt % 2]; s4 = s4f[fcnt % 4]
                    hbufs.append((hi, r0))
                    kb.load([(hi.ap, hs[r0:r0 + 128, :])], hi, reads=[dram_h[r0 // 128]])
                    kb.op("act", lambda e: e.activation(out=ho.ap, in_=hi.ap, func=AF.Square, accum_out=s4.ap[:, 0:1]),
                          reads=(hi,), writes=(ho, s4))
                    kb.op("act", lambda e: e.activation(out=s4.ap[:, 1:2], in_=s4.ap[:, 0:1], func=AF.Ln,
                                                        scale=1.0 / D, bias=EPS), reads=(s4,), writes=(s4,))
                    kb.op("act", lambda e: e.activation(out=s4.ap[:, 2:3], in_=s4.ap[:, 1:2], func=AF.Exp,
                                                        scale=-0.5), reads=(s4,), writes=(s4,))
                    kb.op("dve", lambda e: e.tensor_scalar(out=ho.ap, in0=hi.ap, scalar1=s4.ap[:, 2:3],
                                                           scalar2=None, op0=ALU.mult),
                          reads=(hi, s4), writes=(ho,))
                    pT = ps_T[fcnt % 2]
                    for c in range(8):
                        kb.op("pe", lambda e, c=c: e.transpose(pT.ap[:, c * 128:(c + 1) * 128],
                                                               ho.ap[:, c * 128:(c + 1) * 128], ident.ap),
                              reads=(ho, ident), writes=(pT,), signal=(c == 7))
                    kb.op("dve", lambda e: e.tensor_copy(
                        out=hT.ap[:, :, blk * 128:(blk + 1) * 128], in_=pT.ap.rearrange("p (c n) -> p c n", n=128)),
                        reads=(pT,), writes=(hT,))
                    fstate["fcnt"] += 1
                fh[fi] = hbufs

            def f_up(fi):
                (t0, n, is_s, a0) = ftiles[fi]
                hT = hnT[fi % 2]; at = aT[0]
                for fc in range(32):
                    ucnt = fstate["ucnt"]
                    pu = up_ps[ucnt % 2]; rb = rl[ucnt % 3]; fstate["ucnt"] += 1
                    for kc in range(8):
                        kb.op("pe", lambda e, kc=kc: e.matmul(
                            pu.ap[:, 0:n], w_up_sb.ap[:, kc, fc * 128:(fc + 1) * 128], hT.ap[:, kc, 0:n],
                            start=(kc == 0), stop=(kc == 7)), reads=(w_up_sb, hT), writes=(pu,), signal=(kc == 7))
                    kb.op("act", lambda e: e.activation(out=rb.ap[:, 0:n], in_=pu.ap[:, 0:n], func=AF.Relu),
                          reads=(pu,), writes=(rb,))
                    kb.op("dve", lambda e: e.tensor_tensor(out=at.ap[:, fc, 0:n], in0=rb.ap[:, 0:n],
                                                           in1=rb.ap[:, 0:n], op=ALU.mult),
                          reads=(rb,), writes=(at,))

            def f_down(fi):
                (t0, n, is_s, a0) = ftiles[fi]
                at = aT[0]
                for blk in range(n // 128):
                    hi, r0 = fh[fi][blk]
                    pd = dn_ps2[blk % 2]
                    for nh in range(2):
                        for fc in range(32):
                            kb.op("pe", lambda e, nh=nh, fc=fc: e.matmul(
                                pd.ap[:, nh * 512:(nh + 1) * 512], at.ap[:, fc, blk * 128:(blk + 1) * 128],
                                w_dn_sb.ap[:, fc, nh * 512:(nh + 1) * 512], start=(fc == 0), stop=(fc == 31)),
                                reads=(at, w_dn_sb), writes=(pd,), signal=(nh == 1 and fc == 31))
                    s4 = s4g[(fi * 2 + blk) % 4]; jk = junk[0]
                    kb.op("dve", lambda e: e.tensor_tensor(out=hi.ap, in0=pd.ap, in1=hi.ap, op=ALU.add),
                          reads=(pd, hi), writes=(hi,))
                    kb.op("act", lambda e: e.activation(out=jk.ap, in_=hi.ap, func=AF.Square, accum_out=s4.ap[:, 0:1]),
                          reads=(hi,), writes=(jk, s4))
                    kb.op("act", lambda e: e.activation(out=s4.ap[:, 1:2], in_=s4.ap[:, 0:1], func=AF.Ln,
                                                        scale=1.0 / D, bias=EPS), reads=(s4,), writes=(s4,))
                    kb.op("act", lambda e: e.activation(out=s4.ap[:, 2:3], in_=s4.ap[:, 1:2], func=AF.Exp,
                                                        scale=-0.5), reads=(s4,), writes=(s4,))
                    kb.op("dve", lambda e: e.scalar_tensor_tensor(
                        out=hi.ap, in0=hi.ap, scalar=s4.ap[:, 2:3], in1=gfin_t.ap, op0=ALU.mult, op1=ALU.mult),
                        reads=(hi, s4, gfin_t), writes=(hi,))
                    dst = y_s[r0 - S:r0 - S + 128, :] if is_s else y_p[r0:r0 + 128, :]
                    kb.store([(dst, hi.ap)], hi)

            f_norm(0)
            for fi in range(len(ftiles)):
                f_up(fi)
                if fi + 1 < len(ftiles):
                    f_norm(fi + 1)
                f_down(fi)
            kb.barrier()

        try:
            plan()
        except _Stop:
            kb.barrier()
        with nc.Block() as block:
            @block.sync
            def _(e):
                for f in kb.q["sp"]:
                    f(e)

            @block.tensor
            def _(e):
                for f in kb.q["pe"]:
                    f(e)

            @block.scalar
            def _(e):
                for f in kb.q["act"]:
                    f(e)

            @block.vector
            def _(e):
                for f in kb.q["dve"]:
                    f(e)

            @block.gpsimd
            def _(e):
                for f in kb.q["pool"]:
                    f(e)
    return nc


_CACHE = {}


def _run(S, per_core_inputs, trace=False):
    if S not in _CACHE:
        _CACHE[S] = build_program(S)
    nc = _CACHE[S]
    return run_bass_kernel_spmd(nc, per_core_inputs, core_ids=list(range(len(per_core_inputs))), trace=trace)


def make_core_inputs(c, S, x_prompt, x_sample, cache_sb_k, cache_sb_v, cache_band_k, cache_band_v,
                     norm_mix_g, w_in, rel_bias, norm_sb_g, norm_band_g, w_out,
                     norm_ffn_g, w_up, w_down, norm_final_g):
    f = lambda a: np.ascontiguousarray(np.asarray(a, dtype=np.float32))
    return {
        "x_p": f(x_prompt[c, :S]),
        "x_s": f(x_sample[2 * c:2 * c + 2]).reshape(TS, D),
        "csk": f(cache_sb_k[0, 2 * c:2 * c + 2]).reshape(2, PAST, W),
        "csv": f(cache_sb_v[0, 2 * c:2 * c + 2]).reshape(2, PAST, W),
        "cbk": f(cache_band_k[0, 2 * c:2 * c + 2]).reshape(2, BROWS, W),
        "cbv": f(cache_band_v[0, 2 * c:2 * c + 2]).reshape(2, BROWS, W),
        "w_in": f(w_in[0]), "w_out": f(w_out[0]), "w_up": f(w_up[0]), "w_down": f(w_down[0]),
        "g_mix": f(norm_mix_g[0]), "g_ffn": f(norm_ffn_g[0]), "g_sb": f(norm_sb_g[0]), "g_bd": f(norm_band_g[0]),
        "g_fin": f(norm_final_g), "relb": f(rel_bias[0]),
    }


def assemble(results, S, nb):
    KEEP = min(512, S)
    g = lambda k: np.stack([np.asarray(r[k], dtype=np.float32) for r in results])
    y_p = g("y_p")
    y_s = g("y_s").reshape(2 * nb, 64, D)
    sbk_p = g("sbk_p").reshape(1, nb, S, 8, 64)
    sbv_p = g("sbv_p").reshape(1, nb, S, 8, 64)
    bdk_p = g("bdk_p").reshape(1, nb, KEEP, 8, 64)
    bdv_p = g("bdv_p").reshape(1, nb, KEEP, 8, 64)
    sbk_s = g("sbk_s").reshape(1, 2 * nb, 64, 8, 64)
    sbv_s = g("sbv_s").reshape(1, 2 * nb, 64, 8, 64)
    bdk_s = g("bdk_s").reshape(1, 2 * nb, 64, 8, 64)
    bdv_s = g("bdv_s").reshape(1, 2 * nb, 64, 8, 64)
    return (y_p, y_s, sbk_p, sbv_p, bdk_p, bdv_p, sbk_s, sbv_s, bdk_s, bdv_s)


def kernel(x_prompt, x_sample, cache_sb_k, cache_sb_v, cache_band_k, cache_band_v,
           norm_mix_g, w_in, rel_bias, norm_sb_g, norm_band_g, w_out,
           norm_ffn_g, w_up, w_down, norm_final_g):
    x_prompt = np.asarray(x_prompt)
    nb, S = x_prompt.shape[0], x_prompt.shape[1]
    args = (x_prompt, np.asarray(x_sample), np.asarray(cache_sb_k), np.asarray(cache_sb_v),
            np.asarray(cache_band_k), np.asarray(cache_band_v), np.asarray(norm_mix_g), np.asarray(w_in),
            np.asarray(rel_bias), np.asarray(norm_sb_g), np.asarray(norm_band_g), np.asarray(w_out),
            np.asarray(norm_ffn_g), np.asarray(w_up), np.asarray(w_down), np.asarray(norm_final_g))
    in_maps = [make_core_inputs(c, S, *args) for c in range(nb)]
    res = _run(S, in_maps)
    return assemble(res.results, S, nb)
```

```python
import contextlib
import numpy as np
import concourse.bass as bass
import concourse.mybir as mybir
from concourse.bass_utils import run_bass_kernel_spmd

F32 = mybir.dt.float32
BF16 = mybir.dt.bfloat16
U8 = mybir.dt.uint8
AF = mybir.ActivationFunctionType
ALU = mybir.AluOpType

D = 1024
W = 512
DFF = 4096
EPS = 1e-6
PAST = 2048
BROWS = 512
TS = 128
LX = 384
SB_TOTAL = 206 * 1024
SHIFT = 20.0
ESHIFT = float(np.exp(20.0))


class Buf:
    __slots__ = ("ap", "w", "r", "lsem", "ssem", "name")

    def __init__(self, ap, name=""):
        self.ap = ap
        self.w = {}
        self.r = {}
        self.lsem = None
        self.ssem = None
        self.name = name


class DSem:
    __slots__ = ("sem", "cnt", "kind")

    def __init__(self, sem):
        self.sem = sem
        self.cnt = 0
        self.kind = "pool"


ENGS = ("pe", "act", "dve", "pool", "sp")


class _Rec:
    def __init__(self):
        self.calls = []

    def __getattr__(self, name):
        def f(*a, **k):
            self.calls.append((name, a, k))
            return self
        return f


class KB:
    def __init__(self, nc, stack):
        self.nc = nc
        self.stack = stack
        self.q = {e: [] for e in ENGS}
        self.esem = {}
        self.ecnt = {}
        self.last = {}
        self.waited = {}
        self.pend = {e: [] for e in ENGS}
        self.nsem = 0
        self.dsems = []
        self.free_dsems = {}
        self.sb_off = 0
        self.sb_mark = 0

    def new_sem(self, name):
        self.nsem += 1
        return self.stack.enter_context(self.nc.semaphore(f"{name}_{self.nsem}"))

    def new_phase(self, name):
        for e in ("pe", "act", "dve", "pool"):
            self.esem[e] = self.new_sem(f"{name}_{e}")
            self.ecnt[e] = 0

    def get_dsem(self, kind="pool"):
        fl = self.free_dsems.setdefault(kind, [])
        if fl:
            return fl.pop()
        d = DSem(self.new_sem("dma" + kind))
        d.kind = kind
        self.dsems.append(d)
        return d

    def _wait(self, eng, tok):
        sem, val, peng = tok
        if peng == "pe" and eng == "pe":
            return
        key = (eng, id(sem))
        if self.waited.get(key, 0) >= val:
            return
        self.waited[key] = val
        self.q[eng].append(lambda e, sem=sem, val=val: e.wait_ge(sem, val))

    @staticmethod
    def _merge(d, tok):
        k = id(tok[0])
        if k not in d or d[k][1] < tok[1]:
            d[k] = tok

    def _deps(self, eng, reads, writes):
        for b in reads:
            for t in b.w.values():
                self._wait(eng, t)
        for b in writes:
            for t in b.w.values():
                self._wait(eng, t)
            for t in b.r.values():
                self._wait(eng, t)

    def _commit(self, tok, reads, writes):
        for b in reads:
            self._merge(b.r, tok)
        for b in writes:
            b.w = {id(tok[0]): tok}
            b.r = {}

    def op(self, eng, fn, reads=(), writes=(), signal=True):
        rec = _Rec()
        fn(rec)
        assert len(rec.calls) == 1
        mname, margs, mkw = rec.calls[0]
        fn = lambda e, mname=mname, margs=margs, mkw=mkw: getattr(e, mname)(*margs, **mkw)
        self._deps(eng, reads, writes)
        if not signal:
            self.q[eng].append(lambda e, fn=fn: fn(e))
            self.pend[eng].append((tuple(reads), tuple(writes)))
            return None
        sem = self.esem[eng]
        self.ecnt[eng] += 1
        tok = (sem, self.ecnt[eng], eng)
        self.q[eng].append(lambda e, fn=fn, sem=sem: fn(e).then_inc(sem, 1))
        for (rs, ws) in self.pend[eng]:
            self._commit(tok, rs, ws)
        self.pend[eng] = []
        self._commit(tok, reads, writes)
        self.last[eng] = tok
        return tok

    def dma(self, qeng, pairs, dsem, reads=(), writes=(), slow=False):
        self._deps(qeng, reads, writes)
        for (o, i) in pairs:
            if slow:
                self.q[qeng].append(
                    lambda e, o=o, i=i, s=dsem.sem: e.dma_start(
                        out=o, in_=i, allow_slow_non_contiguous=True).then_inc(s, 16))
            else:
                self.q[qeng].append(
                    lambda e, o=o, i=i, s=dsem.sem: e.dma_start(out=o, in_=i).then_inc(s, 16))
        dsem.cnt += 16 * len(pairs)
        tok = (dsem.sem, dsem.cnt, "dma")
        self._commit(tok, reads, writes)
        return tok

    def load(self, pairs, dst, reads=()):
        if dst.lsem is None:
            dst.lsem = self.get_dsem("sp")
        return self.dma("sp", pairs, dst.lsem, reads=reads, writes=(dst,))

    def store(self, pairs, src, writes=(), qeng="pool"):
        if src.ssem is None:
            src.ssem = self.get_dsem(qeng)
        return self.dma(qeng, pairs, src.ssem, reads=(src,), writes=writes)

    def barrier(self):
        toks = [self.last[e] for e in ("pe", "act", "dve", "pool") if e in self.last]
        toks += [(d.sem, d.cnt, "dma") for d in self.dsems if d.cnt > 0]
        for e in ENGS:
            assert not self.pend[e], e
            for t in toks:
                if t[2] == e:
                    continue
                sem, val, _ = t
                key = (e, id(sem))
                if self.waited.get(key, 0) >= val:
                    continue
                self.waited[key] = val
                self.q[e].append(lambda en, sem=sem, val=val: en.wait_ge(sem, val))

    def release_dsems(self, bufs):
        for b in bufs:
            for a in ("lsem", "ssem"):
                d = getattr(b, a)
                if d is not None:
                    self.free_dsems.setdefault(d.kind, []).append(d)
                    setattr(b, a, None)


def build_program(S, stop_after=None):
    assert S % 512 == 0
    NT = S + TS
    KEEP = min(512, S)
    NQT = S // 512
    NKB = S // 128

    nc = bass.Bass("TRN2", target_bir_lowering=False)

    def din(name, shape, dt=F32):
        return nc.dram_tensor(name, list(shape), dt, kind="ExternalInput").ap()

    def dout(name, shape, dt=F32):
        return nc.dram_tensor(name, list(shape), dt, kind="ExternalOutput").ap()

    def dscr(name, shape, dt):
        return nc.dram_tensor(name, list(shape), dt, kind="Internal").ap()

    x_p = din("x_p", [S, D])
    x_s = din("x_s", [TS, D])
    csk = din("csk", [2, PAST, W])
    csv = din("csv", [2, PAST, W])
    cbk = din("cbk", [2, BROWS, W])
    cbv = din("cbv", [2, BROWS, W])
    w_in = din("w_in", [D, 3 * D])
    w_out = din("w_out", [D, D])
    w_up = din("w_up", [D, DFF])
    w_down = din("w_down", [DFF, D])
    g_mix = din("g_mix", [D])
    g_ffn = din("g_ffn", [D])
    g_sb = din("g_sb", [W])
    g_bd = din("g_bd", [W])
    g_fin = din("g_fin", [D])
    relb = din("relb", [8, 257])

    y_p = dout("y_p", [S, D])
    y_s = dout("y_s", [TS, D])
    sbk_p = dout("sbk_p", [S, W])
    sbv_p = dout("sbv_p", [S, W])
    bdk_p = dout("bdk_p", [KEEP, W])
    bdv_p = dout("bdv_p", [KEEP, W])
    sbk_s = dout("sbk_s", [TS, W])
    sbv_s = dout("sbv_s", [TS, W])
    bdk_s = dout("bdk_s", [TS, W])
    bdv_s = dout("bdv_s", [TS, W])

    qt_sb = dscr("qt_sb", [4, 128, NT], BF16)
    kt_sb = dscr("kt_sb", [4, 128, NT], BF16)
    qt_bd = dscr("qt_bd", [4, 128, NT], BF16)
    kt_bd = dscr("kt_bd", [4, 128, NT], BF16)
    v_sb = dscr("v_sb", [NT, W], BF16)
    v_bd = dscr("v_bd", [NT, W], BF16)
    ot = dscr("ot", [8, 128, NT], BF16)
    hs = dscr("hs", [NT, D], F32)
    e_all = dscr("e_all", [8, LX], F32)
    xrep = dscr("xrep", [8, 129 * LX], F32)

    stack = contextlib.ExitStack()
    with stack:
        big = stack.enter_context(nc.sbuf_tensor("big", [128, SB_TOTAL], U8))
        psum = stack.enter_context(nc.psum_tensor("psum", [128, 4096], F32))
        kb = KB(nc, stack)

        def sb(nelem, dt, shape=None, name=""):
            size = 4 if dt == F32 else 2
            off = (kb.sb_off + 63) // 64 * 64
            nbytes = nelem * size
            assert off + nbytes <= SB_TOTAL, (name, off, nbytes)
            kb.sb_off = off + nbytes
            ap = big[:, off:off + nbytes].bitcast(dt)
            if shape is not None:
                if len(shape) == 2:
                    ap = ap.rearrange("p (a b) -> p a b", b=shape[1])
                elif len(shape) == 3:
                    ap = ap.rearrange("p (a b c) -> p a b c", b=shape[1], c=shape[2])
            return Buf(ap, name)

        def ring(n, nelem, dt, shape=None, name=""):
            return [sb(nelem, dt, shape, f"{name}{i}") for i in range(n)]

        def bank(b, nb=1):
            return psum[:, b * 512:(b + nb) * 512]

        def pbuf(ap, name=""):
            return Buf(ap, name)

        ident = sb(128, BF16, name="ident")
        tri8 = sb(128, BF16, name="tri8")
        ones8 = sb(128, BF16, name="ones8")
        ones1 = sb(64, BF16, name="ones1")
        mc = sb(128, F32, name="mc")
        mfar = sb(128, F32, name="mfar")
        mnear = sb(256, F32, name="mnear")
        wn = sb(8 * 256, F32, (8, 256), "wn")
        en = sb(8 * 256, F32, (8, 256), "en")
        gfin_t = sb(D, F32, name="gfin")
        gmix_c = sb(8, F32, name="gmixc")
        gffn_c = sb(8, F32, name="gffnc")
        gout_c = sb(8, F32, name="goutc")
        negc = sb(8, F32, name="negc")
        fs3 = sb(512, F32, (2, 256), "fs3")
        fsn = sb(512, F32, (2, 256), "fsn")
        const_end = kb.sb_off

        dram_e = Buf(None, "e_all")
        dram_x = Buf(None, "xrep")
        setup_sem = kb.get_dsem()

        class _Stop(Exception):
            pass

        def plan():
            kb.new_phase("W")
            kb.op("pool", lambda e: e.memset(ident.ap, 0.0), writes=(ident,))
            kb.op("pool", lambda e: e.affine_select(out=ident.ap, in_=ident.ap, pattern=[[-1, 128]],
                                                    compare_op=ALU.not_equal, fill=1.0, base=0,
                                                    channel_multiplier=1), writes=(ident,))
            kb.op("pool", lambda e: e.memset(tri8.ap, -8.0), writes=(tri8,))
            kb.op("pool", lambda e: e.affine_select(out=tri8.ap, in_=tri8.ap, pattern=[[-1, 128]],
                                                    compare_op=ALU.is_ge, fill=0.0, base=0,
                                                    channel_multiplier=1), writes=(tri8,))
            kb.op("pool", lambda e: e.memset(ones8.ap, -8.0), writes=(ones8,))
            kb.op("pool", lambda e: e.memset(ones1.ap, 1.0), writes=(ones1,))
            kb.op("pool", lambda e: e.memset(mc.ap, 1.0), writes=(mc,))
            kb.op("pool", lambda e: e.affine_select(out=mc.ap, in_=mc.ap, pattern=[[1, 128]],
                                                    compare_op=ALU.is_gt, fill=0.0, base=0,
                                                    channel_multiplier=-1), writes=(mc,))
            kb.op("pool", lambda e: e.memset(mfar.ap, 1.0), writes=(mfar,))
            kb.op("pool", lambda e: e.memset(mfar.ap[0:64, 64:128], 0.0), writes=(mfar,))
            kb.op("pool", lambda e: e.memset(mnear.ap, 1.0), writes=(mnear,))
            kb.op("pool", lambda e: e.memset(mnear.ap[64:128, 0:64], 0.0), writes=(mnear,))

            setup_sems = []

            def bc_load(dst, src_ap):
                ds = kb.get_dsem(); setup_sems.append(ds)
                kb.q["pool"].append(lambda e, o=dst.ap, i=src_ap, s=ds.sem:
                                    e.dma_start(out=o, in_=i, allow_slow_non_contiguous=True).then_inc(s, 16))
                ds.cnt += 16
                tok = (ds.sem, ds.cnt, "dma")
                dst.w = {id(tok[0]): tok}

            bc_load(gfin_t, g_fin.rearrange("(o n) -> o n", o=1).broadcast_to([128, D]))
            def col3(b):
                return Buf(b.ap.rearrange("p (c o) -> p c o", o=1))
            gm3 = col3(gmix_c); gf3 = col3(gffn_c)
            bc_load(gm3, g_mix.rearrange("(c p o) -> p c o", p=128, o=1))
            bc_load(gf3, g_ffn.rearrange("(c p o) -> p c o", p=128, o=1))
            gmix_c.w = dict(gm3.w); gffn_c.w = dict(gf3.w)
            gout_c_a = Buf(gout_c.ap[:, 0:4].rearrange("p (c o) -> p c o", o=1))
            gout_c_b = Buf(gout_c.ap[:, 4:8].rearrange("p (c o) -> p c o", o=1))
            bc_load(gout_c_a, g_sb.rearrange("(c p o) -> p c o", p=128, o=1))
            bc_load(gout_c_b, g_bd.rearrange("(c p o) -> p c o", p=128, o=1))
            gout_c.w = dict(gout_c_a.w); gout_c.w.update(gout_c_b.w)
            ng3 = col3(negc)
            bc_load(ng3, relb[:, 256:257].rearrange("(x h) o -> x h o", x=1).broadcast_to([128, 8, 1]))
            negc.w = dict(ng3.w)
            kb.op("dve", lambda e: e.tensor_scalar(out=negc.ap, in0=negc.ap, scalar1=-1.0, scalar2=None,
                                                   op0=ALU.mult), reads=(negc,), writes=(negc,))
            kb.dma("pool", [(e_all[:, 0:129], relb[:, 128:257]),
                          (e_all[:, 129:257].rearrange("h (n o) -> h n o", o=1),
                           relb[:, 256:257].rearrange("h (n o) -> h n o", o=1).broadcast_to([8, 128, 1])),
                          (e_all[:, 257:384], relb[:, 1:128])], setup_sem, writes=(dram_e,), slow=True)
            setup_sem2 = kb.get_dsem(); setup_sem3 = kb.get_dsem()
            kb.dma("pool", [(xrep.rearrange("h (r l) -> h r l", l=LX),
                           e_all.rearrange("h (o l) -> h o l", o=1).broadcast_to([8, 129, LX]))],
                   setup_sem2, reads=(dram_e,), writes=(dram_x,), slow=True)
            en_src = bass.AP(xrep.tensor, 0, [[LX - 1, 128], [129 * LX, 8], [1, 256]])
            kb.dma("pool", [(en.ap, en_src)], setup_sem3, reads=(dram_x,), writes=(en,), slow=True)
            for h in range(8):
                kb.op("act", lambda e, h=h: e.activation(out=en.ap[:, h, :], in_=en.ap[:, h, :], func=AF.Exp,
                                                         bias=negc.ap[:, h:h + 1], scale=1.0),
                      reads=(en, negc), writes=(en,))
            for h in range(8):
                kb.op("dve", lambda e, h=h: e.tensor_tensor(out=wn.ap[:, h, :], in0=en.ap[:, h, :],
                                                            in1=mnear.ap, op=ALU.mult),
                      reads=(en, mnear), writes=(wn,))
            kb.op("pool", lambda e: e.memset(fsn.ap, 0.0), writes=(fsn,))
            for h in range(8):
                b_, hh = h % 2, h // 2
                kb.op("dve", lambda e, h=h, b_=b_, hh=hh: e.tensor_copy(
                    out=fs3.ap[:, b_, hh * 64:(hh + 1) * 64], in_=en.ap[:, h, 128:192]),
                    reads=(en,), writes=(fs3,))
                kb.op("dve", lambda e, h=h, b_=b_, hh=hh: e.tensor_copy(
                    out=fsn.ap[0:64, b_, hh * 64:(hh + 1) * 64], in_=en.ap[0:64, h, 0:64]),
                    reads=(en,), writes=(fsn,))

            if stop_after == "W":
                raise _Stop()
            kb.sb_off = const_end
            w_in_sb = sb(8 * 3072, BF16, (8, 3072), "w_in_sb")
            wst = ring(2, 3072, F32, name="wst")
            for c in range(8):
                st = wst[c % 2]
                kb.load([(st.ap, w_in[c * 128:(c + 1) * 128, :])], st)
                kb.op("dve", lambda e, c=c, st=st: e.tensor_scalar(
                    out=w_in_sb.ap[:, c, :], in0=st.ap, scalar1=gmix_c.ap[:, c:c + 1], scalar2=None,
                    op0=ALU.mult), reads=(st, gmix_c), writes=(w_in_sb,))
            p_mark = kb.sb_off
            xin = ring(4, D, F32, name="xin")
            xn = ring(2, D, BF16, name="xn")
            ssb = ring(4, 4, F32, name="ss")
            xnT = ring(2, 8 * 512, BF16, (8, 512), "xnT")
            fmst = ring(2, 16 * 512, BF16, (16, 512), "fmst")
            tmst = ring(4, 512, F32, name="tmst")
            vst = ring(4, 512, BF16, name="vst")
            ps_fm = [pbuf(bank(0)), pbuf(bank(1)), pbuf(bank(2))]
            ps_tm = [pbuf(bank(3)), pbuf(bank(4)), pbuf(bank(5))]
            ps_T = [pbuf(bank(6).bitcast(BF16)), pbuf(bank(7).bitcast(BF16))]
            dram_q = {}

            def dbuf(key):
                if key not in dram_q:
                    dram_q[key] = Buf(None, str(key))
                return dram_q[key]

            tiles = [(i * 512, 512, False) for i in range(NQT)] + [(S, TS, True)]
            cnt = {"blk": 0, "fm": 0, "tm": 0, "tile": 0, "ev": 0, "pt": 0}

            def rms_block(src_rows, xi, xo, s4):
                kb.load([(xi.ap, src_rows)], xi)
                kb.op("act", lambda e: e.activation(out=xo.ap, in_=xi.ap, func=AF.Square,
                                                    accum_out=s4.ap[:, 0:1]),
                      reads=(xi,), writes=(xo, s4))
                kb.op("act", lambda e: e.activation(out=s4.ap[:, 1:2], in_=s4.ap[:, 0:1], func=AF.Ln,
                                                    scale=1.0 / D, bias=EPS), reads=(s4,), writes=(s4,))
                kb.op("act", lambda e: e.activation(out=s4.ap[:, 2:3], in_=s4.ap[:, 1:2], func=AF.Exp,
                                                    scale=-0.5), reads=(s4,), writes=(s4,))
                kb.op("dve", lambda e: e.tensor_scalar(out=xo.ap, in0=xi.ap, scalar1=s4.ap[:, 2:3],
                                                       scalar2=None, op0=ALU.mult),
                      reads=(xi, s4), writes=(xo,))

            def transpose_block(xo, dstT, blk, evac_eng):
                pT = ps_T[cnt["pt"] % 2]; cnt["pt"] += 1
                for c in range(8):
                    kb.op("pe", lambda e, c=c, pT=pT: e.transpose(pT.ap[:, c * 128:(c + 1) * 128],
                                                                 xo.ap[:, c * 128:(c + 1) * 128], ident.ap),
                          reads=(xo, ident), writes=(pT,), signal=(c == 7))
                src = pT.ap.rearrange("p (c n) -> p c n", n=128)
                dst = dstT.ap[:, :, blk * 128:(blk + 1) * 128]
                if evac_eng == "act":
                    kb.op("act", lambda e: e.activation(out=dst, in_=src, func=AF.Copy),
                          reads=(pT,), writes=(dstT,))
                else:
                    kb.op("dve", lambda e: e.tensor_copy(out=dst, in_=src), reads=(pT,), writes=(dstT,))

            def p_norm(ti):
                (t0, n, is_s) = tiles[ti]
                xT = xnT[ti % 2]
                for blk in range(n // 128):
                    bi = cnt["blk"]
                    xi = xin[bi % 4]; xo = xn[bi % 2]; s4 = ssb[bi % 4]
                    rows = x_s[blk * 128:(blk + 1) * 128, :] if is_s else x_p[t0 + blk * 128:t0 + (blk + 1) * 128, :]
                    rms_block(rows, xi, xo, s4)
                    transpose_block(xo, xT, blk, "dve")
                    cnt["blk"] += 1

            def p_mm(ti):
                (t0, n, is_s) = tiles[ti]
                nb = n // 128
                xT = xnT[ti % 2]
                fst = fmst[ti % 2]
                fm_cols = [0 * 512, 1 * 512, 3 * 512, 4 * 512]
                for g4 in (0, 2, 3):
                    for c4 in range(4):
                        oc = g4 * 4 + c4
                        col0 = fm_cols[g4] + c4 * 128
                        pf = ps_fm[cnt["fm"] % 3]; cnt["fm"] += 1
                        for kc in range(8):
                            kb.op("pe", lambda e, kc=kc, pf=pf, col0=col0: e.matmul(
                                pf.ap[:, 0:n], w_in_sb.ap[:, kc, col0:col0 + 128], xT.ap[:, kc, 0:n],
                                start=(kc == 0), stop=(kc == 7)),
                                reads=(w_in_sb, xT), writes=(pf,), signal=(kc == 7))
                        if cnt["ev"] % 3 != 2:
                            kb.op("act", lambda e, pf=pf, oc=oc: e.activation(out=fst.ap[:, oc, 0:n], in_=pf.ap[:, 0:n],
                                                                              func=AF.Copy),
                                  reads=(pf,), writes=(fst,))
                        else:
                            kb.op("dve", lambda e, pf=pf, oc=oc: e.tensor_copy(out=fst.ap[:, oc, 0:n], in_=pf.ap[:, 0:n]),
                                  reads=(pf,), writes=(fst,))
                        cnt["ev"] += 1
                need_bd = is_s or (t0 + n > S - KEEP)
                for blk in range(nb):
                    r0 = t0 + blk * 128
                    groups = [("k_sb", 512), ("v_sb", 1024), ("v_bd", 2560)]
                    if need_bd:
                        groups.append(("k_bd", 2048))
                    for (gname, gcol) in groups:
                        pt = ps_tm[cnt["tm"] % 3]
                        for kc in range(8):
                            kb.op("pe", lambda e, kc=kc, pt=pt, gcol=gcol, blk=blk: e.matmul(
                                pt.ap, xT.ap[:, kc, blk * 128:(blk + 1) * 128], w_in_sb.ap[:, kc, gcol:gcol + 512],
                                start=(kc == 0), stop=(kc == 7)),
                                reads=(w_in_sb, xT), writes=(pt,), signal=(kc == 7))
                        ts_ = tmst[cnt["tm"] % 4]
                        vs_ = vst[cnt["tm"] % 4]
                        cnt["tm"] += 1
                        kb.op("dve", lambda e, pt=pt, ts_=ts_: e.tensor_copy(out=ts_.ap, in_=pt.ap),
                              reads=(pt,), writes=(ts_,))
                        outs = []
                        if gname == "k_sb":
                            outs.append(sbk_s[blk * 128:(blk + 1) * 128, :] if is_s else sbk_p[r0:r0 + 128, :])
                        elif gname == "v_sb":
                            outs.append(sbv_s[blk * 128:(blk + 1) * 128, :] if is_s else sbv_p[r0:r0 + 128, :])
                        elif gname == "k_bd":
                            outs.append(bdk_s[blk * 128:(blk + 1) * 128, :] if is_s
                                        else bdk_p[r0 - (S - KEEP):r0 - (S - KEEP) + 128, :])
                        elif gname == "v_bd" and need_bd:
                            outs.append(bdv_s[blk * 128:(blk + 1) * 128, :] if is_s
                                        else bdv_p[r0 - (S - KEEP):r0 - (S - KEEP) + 128, :])
                        if outs:
                            kb.store([(o, ts_.ap) for o in outs], ts_)
                        if gname == "k_sb":
                            kbf = vs_
                            kb.op("act", lambda e, ts_=ts_, kbf=kbf: e.activation(out=kbf.ap, in_=ts_.ap, func=AF.Copy),
                                  reads=(ts_,), writes=(kbf,))
                            kdefer = kbf
                        if gname in ("v_sb", "v_bd"):
                            kb.op("act", lambda e, ts_=ts_, vs_=vs_: e.activation(out=vs_.ap, in_=ts_.ap, func=AF.Copy),
                                  reads=(ts_,), writes=(vs_,))
                            dstv = v_sb if gname == "v_sb" else v_bd
                            kb.store([(dstv[r0:r0 + 128, :], vs_.ap)], vs_, writes=(dbuf((gname, r0 // 128)),))
                    pT = ps_T[cnt["pt"] % 2]; cnt["pt"] += 1
                    for c4 in range(4):
                        kb.op("pe", lambda e, c4=c4, pT=pT, kdefer=kdefer: e.transpose(
                            pT.ap[:, c4 * 128:(c4 + 1) * 128], kdefer.ap[:, c4 * 128:(c4 + 1) * 128], ident.ap),
                            reads=(kdefer, ident), writes=(pT,), signal=(c4 == 3))
                    kb.op("dve", lambda e, pT=pT, blk=blk: e.tensor_copy(
                        out=fst.ap[:, 4:8, blk * 128:(blk + 1) * 128],
                        in_=pT.ap[:, 0:512].rearrange("p (c n) -> p c n", n=128)),
                        reads=(pT,), writes=(fst,))
                pairs = []
                for g4, dst in enumerate((qt_sb, kt_sb, qt_bd, kt_bd)):
                    pairs.append((dst[:, :, t0:t0 + n].rearrange("h p n -> p h n"), fst.ap[:, g4 * 4:(g4 + 1) * 4, 0:n]))
                kb.store(pairs, fst, writes=(dbuf(("fm", ti)),))
            p_norm(0)
            for ti in range(len(tiles)):
                if ti + 1 < len(tiles):
                    p_norm(ti + 1)
                p_mm(ti)
            kb.barrier()
            kb.release_dsems(wst + xin + fmst + tmst + vst)

            if stop_after == "P":
                raise _Stop()
            kb.new_phase("SB")
            kb.sb_off = const_end
            qkv = []
            for i in range(2):
                qkv.append((sb(S, BF16, name=f"QT{i}"), sb(S, BF16, name=f"KT{i}"),
                            sb(NKB * 128, BF16, (NKB, 128), f"V{i}")))
            l_r = ring(3, 1024, BF16, (2, 512), "L")
            w_r = ring(3, 1024, BF16, (2, 512), "w")
            ra_r = ring(3, 1024, BF16, (2, 512), "ra")
            ost = ring(2, 512, BF16, name="ost")
            zc_ps = [pbuf(bank(2 * i_, 2).rearrange("p (b n) -> p b n", n=512)) for i_ in range(3)]
            o_ps = [pbuf(bank(6)), pbuf(bank(7))]
            dram_ot = {}

            def load_qkv(hp, slot, qsrc, ksrc, vsrc, vkey):
                QT, KT, V = qkv[slot]
                rd = [dbuf(("fm", t)) for t in range(NQT)]
                kb.load([(QT.ap, qsrc[hp, :, 0:S])], QT, reads=rd)
                kb.load([(KT.ap, ksrc[hp, :, 0:S])], KT, reads=rd)
                rdv = [dbuf((vkey, b)) for b in range(NKB)]
                pairs = []
                for b0 in range(0, NKB, 16):
                    b1 = min(NKB, b0 + 16)
                    pairs.append((V.ap[:, b0:b1, :],
                                  vsrc[b0 * 128:b1 * 128, hp * 128:(hp + 1) * 128].rearrange("(b p) f -> p b f", p=128)))
                kb.load(pairs, V, reads=rdv)

            its = []
            for hp in range(4):
                for i in range(NQT):
                    js = list(range(4 * i + 3, -1, -1))
                    for n_, j in enumerate(js):
                        m = j - 4 * i
                        c0 = 128 * m if m > 0 else 0
                        its.append(dict(hp=hp, i=i, j=j, c0=c0, diag=(m >= 0), first=(n_ == 0),
                                        last=(n_ == len(js) - 1), slot=hp % 2, qt=hp * NQT + i))
            NIT = len(its)
            load_qkv(0, 0, qt_sb, kt_sb, v_sb, "v_sb")
            loaded = {0}

            def st_qk(k):
                it = its[k]
                QT, KT, V = qkv[it["slot"]]
                z = zc_ps[k % 3]; c0 = it["c0"]; i = it["i"]; j = it["j"]
                for b in range(2):
                    kb.op("pe", lambda e, b=b: e.matmul(
                        z.ap[:, b, c0:512], KT.ap[b * 64:(b + 1) * 64, j * 128:(j + 1) * 128],
                        QT.ap[b * 64:(b + 1) * 64, i * 512 + c0:(i + 1) * 512], start=True, stop=True),
                        reads=(QT, KT), writes=(z,), signal=(b == 1))

            def st_l(k):
                it = its[k]; z = zc_ps[k % 3]; lb = l_r[k % 3]; c0 = it["c0"]
                if c0 > 0:
                    kb.op("pool", lambda e: e.memset(lb.ap[:, :, 0:c0], 0.0), writes=(lb,))
                kb.op("act", lambda e: e.activation(out=lb.ap[:, :, c0:512], in_=z.ap[:, :, c0:512],
                                                    func=AF.Softplus, scale=0.125), reads=(z,), writes=(lb,))
                if it["diag"]:
                    for b in range(2):
                        kb.op("dve", lambda e, b=b: e.tensor_tensor(out=lb.ap[:, b, c0:c0 + 128],
                                                                    in0=lb.ap[:, b, c0:c0 + 128], in1=mc.ap,
                                                                    op=ALU.mult), reads=(lb, mc), writes=(lb,))

            def st_ra(k):
                it = its[k]
                if it["last"]:
                    return
                lb = l_r[k % 3]; rn = ra_r[(k + 1) % 3]; rc = ra_r[k % 3]
                if it["first"]:
                    kb.op("dve", lambda e: e.tensor_copy(out=rn.ap, in_=lb.ap), reads=(lb,), writes=(rn,))
                else:
                    kb.op("dve", lambda e: e.tensor_tensor(out=rn.ap, in0=rc.ap, in1=lb.ap, op=ALU.add),
                          reads=(rc, lb), writes=(rn,))

            def st_c(k):
                it = its[k]; lb = l_r[k % 3]; rc = ra_r[k % 3]; cp = zc_ps[k % 3]
                for b in range(2):
                    kb.op("pe", lambda e, b=b: e.matmul(cp.ap[:, b, :], tri8.ap, lb.ap[:, b, :], start=False,
                                                        stop=it["first"], skip_group_check=True),
                          reads=(tri8, lb), writes=(cp,), signal=(it["first"] and b == 1))
                    if not it["first"]:
                        kb.op("pe", lambda e, b=b: e.matmul(cp.ap[:, b, :], ones8.ap, rc.ap[:, b, :], start=False,
                                                            stop=True, skip_group_check=True),
                              reads=(ones8, rc), writes=(cp,), signal=(b == 1))

            def st_w(k):
                it = its[k]; cp = zc_ps[k % 3]; wb = w_r[k % 3]; c0 = it["c0"]
                if c0 > 0:
                    kb.op("pool", lambda e: e.memset(wb.ap[:, :, 0:c0], 0.0), writes=(wb,))
                kb.op("act", lambda e: e.activation(out=wb.ap[:, :, c0:512], in_=cp.ap[:, :, c0:512],
                                                    func=AF.Softplus, scale=0.125, bias=-SHIFT),
                      reads=(cp,), writes=(wb,))
                if it["diag"]:
                    for b in range(2):
                        kb.op("dve", lambda e, b=b: e.tensor_tensor(out=wb.ap[:, b, c0:c0 + 128],
                                                                    in0=wb.ap[:, b, c0:c0 + 128], in1=mc.ap,
                                                                    op=ALU.mult), reads=(wb, mc), writes=(wb,))

            def st_pv(k):
                it = its[k]; wb = w_r[k % 3]; QT, KT, V = qkv[it["slot"]]
                op_ = o_ps[it["qt"] % 2]; j = it["j"]
                for b in range(2):
                    kb.op("pe", lambda e, b=b: e.matmul(op_.ap[b * 64:(b + 1) * 64, :], V.ap[:, j, b * 64:(b + 1) * 64],
                                                        wb.ap[:, b, :], start=it["first"], stop=it["last"]),
                          reads=(V, wb), writes=(op_,), signal=(b == 1))
                if it["last"]:
                    os_ = ost[it["qt"] % 2]
                    kb.op("dve", lambda e: e.tensor_scalar(out=os_.ap, in0=op_.ap, scalar1=ESHIFT, scalar2=None,
                                                           op0=ALU.mult), reads=(op_,), writes=(os_,))
                    t0 = it["i"] * 512
                    d_ = Buf(None); dram_ot[(it["hp"], it["i"])] = d_
                    kb.store([(ot[it["hp"], :, t0:t0 + 512], os_.ap)], os_, writes=(d_,))

            st_qk(0)
            for r in range(NIT + 3):
                if 0 <= r - 2 < NIT:
                    it = its[r - 2]
                    if it["i"] == 0 and it["first"] and it["hp"] + 1 < 4 and (it["hp"] + 1) not in loaded:
                        load_qkv(it["hp"] + 1, (it["hp"] + 1) % 2, qt_sb, kt_sb, v_sb, "v_sb")
                        loaded.add(it["hp"] + 1)
                if r + 1 < NIT:
                    st_qk(r + 1)
                if r < NIT:
                    st_l(r)
                if 0 <= r - 1 < NIT:
                    st_w(r - 1)
                if r < NIT:
                    st_ra(r)
                    st_c(r)
                if 0 <= r - 2 < NIT:
                    st_pv(r - 2)
            kb.barrier()

            if stop_after == "SB":
                raise _Stop()
            kb.new_phase("BD")
            wb_r = ring(3, 1024, BF16, (2, 512), "wb")
            rd_r = ring(2, 512, F32, name="rden")
            zb_ps = [pbuf(bank(0, 2).rearrange("p (b n) -> p b n", n=512)),
                     pbuf(bank(2, 2).rearrange("p (b n) -> p b n", n=512))]
            ob_ps = [pbuf(bank(4)), pbuf(bank(5))]
            dn_ps = [pbuf(bank(6)), pbuf(bank(7))]
            load_qkv(0, 0, qt_bd, kt_bd, v_bd, "v_bd")
            brecs = []
            for hp in range(4):
                for i in range(NQT):
                    blocks = [(4 * i + m, 128 * m, 512, "near", m) for m in range(4)]
                    if i > 0:
                        blocks += [(4 * i - 4 + jj, 0, 128 * (jj + 1), "far", jj) for jj in range(4)]
                    for n_, (j, a, b_, kind, m) in enumerate(blocks):
                        brecs.append(dict(hp=hp, i=i, j=j, a=a, b_=b_, kind=kind, m=m, first=(n_ == 0),
                                          last=(n_ == len(blocks) - 1), qi=hp * NQT + i))
            NBR = len(brecs)

            def bd_qk(n):
                rc = brecs[n]; QT, KT, V = qkv[rc["hp"] % 2]; z = zb_ps[n % 2]
                j, a, b_, i = rc["j"], rc["a"], rc["b_"], rc["i"]
                for b in range(2):
                    kb.op("pe", lambda e, b=b: e.matmul(
                        z.ap[:, b, a:b_], KT.ap[b * 64:(b + 1) * 64, j * 128:(j + 1) * 128],
                        QT.ap[b * 64:(b + 1) * 64, i * 512 + a:i * 512 + b_], start=True, stop=True),
                        reads=(QT, KT), writes=(z,), signal=(b == 1))

            wbh = [[Buf(w_.ap[:, b]) for b in range(2)] for w_ in wb_r]

            def bd_exp(n):
                rc = brecs[n]; z = zb_ps[n % 2]; wb = wb_r[n % 3]; wh = wbh[n % 3]
                a, b_, kind, m, hp = rc["a"], rc["b_"], rc["kind"], rc["m"], rc["hp"]
                kb.op("act", lambda e: e.activation(out=wb.ap[:, :, a:b_], in_=z.ap[:, :, a:b_], func=AF.Exp,
                                                    scale=0.125), reads=(z,), writes=(wh[0], wh[1]))
                for b in range(2):
                    h = 2 * hp + b
                    if kind == "near":
                        wd = min(256, 512 - a)
                        kb.op("dve", lambda e, b=b, wd=wd, h=h: e.tensor_tensor(
                            out=wb.ap[:, b, a:a + wd], in0=wb.ap[:, b, a:a + wd], in1=wn.ap[:, h, 0:wd],
                            op=ALU.mult), reads=(wh[b], wn), writes=(wh[b],))
                    else:
                        kb.op("dve", lambda e, b=b: e.tensor_tensor(
                            out=wb.ap[:, b, b_ - 128:b_], in0=wb.ap[:, b, b_ - 128:b_], in1=mfar.ap,
                            op=ALU.mult), reads=(wh[b], mfar), writes=(wh[b],))
                        if m == 3:
                            kb.op("dve", lambda e, b=b, h=h: e.tensor_tensor(
                                out=wb.ap[:, b, 0:128], in0=wb.ap[:, b, 0:128], in1=en.ap[:, h, 128:256],
                                op=ALU.mult), reads=(wh[b], en), writes=(wh[b],))

            def bd_pv(n):
                rc = brecs[n]; QT, KT, V = qkv[rc["hp"] % 2]; wb = wb_r[n % 3]; wh = wbh[n % 3]
                j, a, b_, qi = rc["j"], rc["a"], rc["b_"], rc["qi"]
                op_ = ob_ps[qi % 2]; dn = dn_ps[qi % 2]
                for b in range(2):
                    kb.op("pe", lambda e, b=b: e.matmul(
                        op_.ap[b * 64:(b + 1) * 64, a:b_], V.ap[:, j, b * 64:(b + 1) * 64], wb.ap[:, b, a:b_],
                        start=rc["first"], stop=rc["last"]), reads=(V, wh[b]), writes=(op_,), signal=False)
                for b in range(2):
                    kb.op("pe", lambda e, b=b: e.matmul(
                        dn.ap[b * 64:(b + 1) * 64, a:b_], ones1.ap, wb.ap[:, b, a:b_],
                        start=rc["first"], stop=rc["last"]), reads=(ones1, wh[b]), writes=(dn,), signal=(b == 1))
                if rc["last"]:
                    rd = rd_r[qi % 2]; os_ = ost[qi % 2]
                    kb.op("dve", lambda e: e.reciprocal(out=rd.ap, in_=dn.ap), reads=(dn,), writes=(rd,))
                    kb.op("dve", lambda e: e.tensor_tensor(out=os_.ap, in0=op_.ap, in1=rd.ap, op=ALU.mult),
                          reads=(op_, rd), writes=(os_,))
                    d_ = Buf(None); dram_ot[(4 + rc["hp"], rc["i"])] = d_
                    kb.store([(ot[4 + rc["hp"], :, rc["i"] * 512:(rc["i"] + 1) * 512], os_.ap)], os_, writes=(d_,))

            bloaded = {0}
            bd_qk(0)
            for r in range(NBR + 1):
                if 0 <= r - 1 < NBR:
                    rc = brecs[r - 1]
                    if rc["i"] == 0 and rc["first"] and rc["hp"] + 1 < 4 and (rc["hp"] + 1) not in bloaded:
                        load_qkv(rc["hp"] + 1, (rc["hp"] + 1) % 2, qt_bd, kt_bd, v_bd, "v_bd")
                        bloaded.add(rc["hp"] + 1)
                if r + 1 < NBR:
                    bd_qk(r + 1)
                if r < NBR:
                    bd_exp(r)
                if 0 <= r - 1 < NBR:
                    bd_pv(r - 1)
            kb.barrier()
            kb.release_dsems([b for t in qkv for b in t] + ost)

            if stop_after == "BD":
                raise _Stop()
            kb.new_phase("SA")
            kb.sb_off = const_end
            ktc = sb(2 * 4 * PAST, BF16, (2, 4, PAST), "ktc")
            vc = sb(2 * 16 * W, BF16, (2, 16, W), "vc")
            ktb = sb(2 * 4 * BROWS, BF16, (2, 4, BROWS), "ktb")
            vbc = sb(2 * 4 * W, BF16, (2, 4, W), "vbc")
            cst = ring(2, 4 * W, F32, (4, W), "cst")
            cbf = ring(2, 4 * W, BF16, (4, W), "cbf")
            qs_sb = sb(4 * TS, BF16, (4, TS), "qs_sb")
            ks_sb = sb(4 * 2 * 128, BF16, (4, 2, 128), "ks_sb")
            qs_bd = sb(4 * TS, BF16, (4, TS), "qs_bd")
            ks_bd = sb(4 * 2 * 128, BF16, (4, 2, 128), "ks_bd")
            vs_sb = sb(2 * W, BF16, (2, W), "vs_sb")
            vs_bd = sb(2 * W, BF16, (2, W), "vs_bd")
            sl_r = ring(3, 512, BF16, (2, 256), "sl")
            sw_r = ring(3, 512, BF16, (2, 256), "sw")
            sra_r = ring(3, 512, BF16, (2, 256), "sra")
            sos = ring(2, 256, BF16, (4, 64), "sos")
            srd = ring(2, 256, F32, name="srd")
            fm_s = [dbuf(("fm", NQT))]
            kb.op("pool", lambda e: e.memset(ks_sb.ap, 0.0), writes=(ks_sb,))
            kb.op("pool", lambda e: e.memset(ks_bd.ap, 0.0), writes=(ks_bd,))
            kb.op("pool", lambda e: e.memset(vs_sb.ap, 0.0), writes=(vs_sb,))
            kb.op("pool", lambda e: e.memset(vs_bd.ap, 0.0), writes=(vs_bd,))
            kb.load([(qs_sb.ap, qt_sb[:, :, S:S + TS].rearrange("h p n -> p h n"))], qs_sb, reads=fm_s)
            kb.load([(qs_bd.ap, qt_bd[:, :, S:S + TS].rearrange("h p n -> p h n"))], qs_bd, reads=fm_s)
            kb.load([(ks_sb.ap[:, :, s, 0:64], kt_sb[:, :, S + s * 64:S + (s + 1) * 64].rearrange("h p n -> p h n"))
                     for s in range(2)], ks_sb, reads=fm_s)
            kb.load([(ks_bd.ap[:, :, s, 0:64], kt_bd[:, :, S + s * 64:S + (s + 1) * 64].rearrange("h p n -> p h n"))
                     for s in range(2)], ks_bd, reads=fm_s)
            kb.load([(vs_sb.ap[0:64, s, :], v_sb[S + s * 64:S + (s + 1) * 64, :]) for s in range(2)], vs_sb,
                    reads=[dbuf(("v_sb", S // 128))])
            kb.load([(vs_bd.ap[0:64, s, :], v_bd[S + s * 64:S + (s + 1) * 64, :]) for s in range(2)], vs_bd,
                    reads=[dbuf(("v_bd", S // 128))])
            sT = [pbuf(bank(7).bitcast(BF16))]
            ccnt = 0
            tcnt = 0
            for (ksrc, vsrc, nblk, kdst, vdst) in ((csk, csv, 16, ktc, vc), (cbk, cbv, 4, ktb, vbc)):
                for s in range(2):
                    for g in range(nblk // 4):
                        st = cst[ccnt % 2]; cb = cbf[ccnt % 2]; ccnt += 1
                        kb.load([(st.ap, ksrc[s, g * 512:(g + 1) * 512, :].rearrange("(b p) f -> p b f", p=128))], st)
                        kb.op("dve", lambda e, st=st, cb=cb: e.tensor_copy(out=cb.ap, in_=st.ap), reads=(st,), writes=(cb,))
                        for hp in range(4):
                            pT = sT[0]; tcnt += 1
                            for b4 in range(4):
                                kb.op("pe", lambda e, pT=pT, cb=cb, b4=b4, hp=hp: e.transpose(
                                    pT.ap[:, b4 * 128:(b4 + 1) * 128], cb.ap[:, b4, hp * 128:(hp + 1) * 128], ident.ap),
                                    reads=(cb, ident), writes=(pT,), signal=(b4 == 3))
                            kb.op("act", lambda e, pT=pT, s=s, hp=hp, g=g, kdst=kdst: e.activation(
                                out=kdst.ap[:, s, hp, g * 512:(g + 1) * 512], in_=pT.ap[:, 0:512], func=AF.Copy),
                                reads=(pT,), writes=(kdst,))
                        st2 = cst[ccnt % 2]; ccnt += 1
                        kb.load([(st2.ap, vsrc[s, g * 512:(g + 1) * 512, :].rearrange("(b p) f -> p b f", p=128))], st2)
                        kb.op("dve", lambda e, st2=st2, s=s, g=g, vdst=vdst: e.tensor_copy(
                            out=vdst.ap[:, s, g * 4:(g + 1) * 4, :], in_=st2.ap), reads=(st2,), writes=(vdst,))

            zs_ps = [pbuf(bank(2 * i_, 2).rearrange("p (b n) -> p b n", n=512)) for i_ in range(3)]
            os_ps = [pbuf(bank(6))]
            dns_ps = [pbuf(bank(6)[:, 256:512])]
            dram_ots = Buf(None)
            sits = []
            for s in range(2):
                for n_, j in enumerate([16] + list(range(15, -1, -1))):
                    sits.append(dict(s=s, j=j, first=(n_ == 0), last=(n_ == 16)))
            NS = len(sits)

            def kt_blk(it, b, hh):
                if it["j"] == 16:
                    return ks_sb.ap[b * 64:(b + 1) * 64, hh, it["s"], :]
                return ktc.ap[b * 64:(b + 1) * 64, it["s"], hh, it["j"] * 128:(it["j"] + 1) * 128]

            def v_blk(it, h):
                if it["j"] == 16:
                    return vs_sb.ap[:, it["s"], h * 64:(h + 1) * 64]
                return vc.ap[:, it["s"], it["j"], h * 64:(h + 1) * 64]

            def ss_qk(k):
                it = sits[k]; z = zs_ps[k % 3]; s = it["s"]
                for hh in range(4):
                    for b in range(2):
                        kb.op("pe", lambda e, b=b, hh=hh: e.matmul(
                            z.ap[:, b, hh * 64:(hh + 1) * 64], kt_blk(it, b, hh),
                            qs_sb.ap[b * 64:(b + 1) * 64, hh, s * 64:(s + 1) * 64], start=(hh == 0), stop=(hh == 3),
                            skip_group_check=True),
                            reads=(ktc, ks_sb, qs_sb), writes=(z,), signal=(hh == 3 and b == 1))

            def ss_l(k):
                it = sits[k]; z = zs_ps[k % 3]; lb = sl_r[k % 3]
                kb.op("act", lambda e: e.activation(out=lb.ap, in_=z.ap[:, :, 0:256], func=AF.Softplus, scale=0.125),
                      reads=(z,), writes=(lb,))
                if it["j"] == 16:
                    for b in range(2):
                        for hh in range(4):
                            kb.op("dve", lambda e, b=b, hh=hh: e.tensor_tensor(
                                out=lb.ap[:, b, hh * 64:(hh + 1) * 64], in0=lb.ap[:, b, hh * 64:(hh + 1) * 64],
                                in1=mc.ap[:, 0:64], op=ALU.mult), reads=(lb, mc), writes=(lb,))

            def ss_ra(k):
                it = sits[k]
                if it["last"]:
                    return
                lb = sl_r[k % 3]; rn = sra_r[(k + 1) % 3]; rc = sra_r[k % 3]
                if it["first"]:
                    kb.op("dve", lambda e: e.tensor_copy(out=rn.ap, in_=lb.ap), reads=(lb,), writes=(rn,))
                else:
                    kb.op("dve", lambda e: e.tensor_tensor(out=rn.ap, in0=rc.ap, in1=lb.ap, op=ALU.add),
                          reads=(rc, lb), writes=(rn,))

            def ss_c(k):
                it = sits[k]; lb = sl_r[k % 3]; rc = sra_r[k % 3]; cp = zs_ps[k % 3]
                for b in range(2):
                    kb.op("pe", lambda e, b=b: e.matmul(cp.ap[:, b, 0:256], tri8.ap, lb.ap[:, b, :], start=False,
                                                        stop=it["first"], skip_group_check=True),
                          reads=(tri8, lb), writes=(cp,), signal=(it["first"] and b == 1))
                    if not it["first"]:
                        kb.op("pe", lambda e, b=b: e.matmul(cp.ap[:, b, 0:256], ones8.ap, rc.ap[:, b, :], start=False,
                                                            stop=True, skip_group_check=True),
                              reads=(ones8, rc), writes=(cp,), signal=(b == 1))

            def ss_w(k):
                it = sits[k]; cp = zs_ps[k % 3]; wb = sw_r[k % 3]
                kb.op("act", lambda e: e.activation(out=wb.ap, in_=cp.ap[:, :, 0:256], func=AF.Softplus, scale=0.125,
                                                    bias=-SHIFT), reads=(cp,), writes=(wb,))
                if it["j"] == 16:
                    for b in range(2):
                        for hh in range(4):
                            kb.op("dve", lambda e, b=b, hh=hh: e.tensor_tensor(
                                out=wb.ap[:, b, hh * 64:(hh + 1) * 64], in0=wb.ap[:, b, hh * 64:(hh + 1) * 64],
                                in1=mc.ap[:, 0:64], op=ALU.mult), reads=(wb, mc), writes=(wb,))

            def ss_pv(k):
                it = sits[k]; wb = sw_r[k % 3]; op_ = os_ps[0]; s = it["s"]
                for hh in range(4):
                    for b in range(2):
                        h = 2 * hh + b
                        kb.op("pe", lambda e, b=b, hh=hh, h=h: e.matmul(
                            op_.ap[b * 64:(b + 1) * 64, hh * 64:(hh + 1) * 64], v_blk(it, h),
                            wb.ap[:, b, hh * 64:(hh + 1) * 64], start=(it["first"] and hh == 0),
                            stop=(it["last"] and hh == 3), skip_group_check=True),
                            reads=(vc, vs_sb, wb), writes=(op_,), signal=(hh == 3 and b == 1))
                if it["last"]:
                    os_ = sos[s % 2]
                    kb.op("dve", lambda e: e.tensor_scalar(out=os_.ap.rearrange("p h q -> p (h q)"), in0=op_.ap[:, 0:256],
                                                           scalar1=ESHIFT, scalar2=None, op0=ALU.mult),
                          reads=(op_,), writes=(os_,))
                    kb.store([(ot[0:4, :, S + s * 64:S + (s + 1) * 64].rearrange("h p n -> p h n"), os_.ap)], os_,
                             writes=(dram_ots,))

            ss_qk(0)
            for r in range(NS + 3):
                if r + 1 < NS:
                    ss_qk(r + 1)
                if r < NS:
                    ss_l(r)
                if 0 <= r - 1 < NS:
                    ss_w(r - 1)
                if r < NS:
                    ss_ra(r)
                    ss_c(r)
                if 0 <= r - 2 < NS:
                    ss_pv(r - 2)
            scnt = 0
            for s in range(2):
                op_ = os_ps[0]; dn = dns_ps[0]
                order = [4, 3, 2, 1, 0]
                for n_, j in enumerate(order):
                    z = zs_ps[scnt % 2]; wb = sw_r[scnt % 3]; scnt += 1
                    for hh in range(4):
                        for b in range(2):
                            kt_ap = (ks_bd.ap[b * 64:(b + 1) * 64, hh, s, :] if j == 4
                                     else ktb.ap[b * 64:(b + 1) * 64, s, hh, j * 128:(j + 1) * 128])
                            kb.op("pe", lambda e, b=b, hh=hh, z=z, kt_ap=kt_ap: e.matmul(
                                z.ap[:, b, hh * 64:(hh + 1) * 64], kt_ap,
                                qs_bd.ap[b * 64:(b + 1) * 64, hh, s * 64:(s + 1) * 64], start=True, stop=True),
                                reads=(ktb, ks_bd, qs_bd), writes=(z,), signal=(hh == 3 and b == 1))
                    kb.op("act", lambda e, z=z, wb=wb: e.activation(out=wb.ap, in_=z.ap[:, :, 0:256], func=AF.Exp,
                                                                    scale=0.125), reads=(z,), writes=(wb,))
                    if j == 4:
                        kb.op("dve", lambda e, wb=wb: e.tensor_tensor(out=wb.ap, in0=wb.ap, in1=fsn.ap, op=ALU.mult),
                              reads=(wb, fsn), writes=(wb,))
                    elif j == 3:
                        kb.op("dve", lambda e, wb=wb: e.tensor_tensor(out=wb.ap, in0=wb.ap, in1=fs3.ap, op=ALU.mult),
                              reads=(wb, fs3), writes=(wb,))
                    fst_, lst_ = (n_ == 0), (n_ == 4)
                    for hh in range(4):
                        for b in range(2):
                            h = 2 * hh + b
                            v_ap = vs_bd.ap[:, s, h * 64:(h + 1) * 64] if j == 4 else vbc.ap[:, s, j, h * 64:(h + 1) * 64]
                            kb.op("pe", lambda e, b=b, hh=hh, wb=wb, v_ap=v_ap, fst_=fst_, lst_=lst_: e.matmul(
                                op_.ap[b * 64:(b + 1) * 64, hh * 64:(hh + 1) * 64], v_ap,
                                wb.ap[:, b, hh * 64:(hh + 1) * 64], start=(fst_ and hh == 0),
                                stop=(lst_ and hh == 3), skip_group_check=True),
                                reads=(vbc, vs_bd, wb), writes=(op_,), signal=False)
                    for b in range(2):
                        kb.op("pe", lambda e, b=b, wb=wb, fst_=fst_, lst_=lst_: e.matmul(
                            dn.ap[b * 64:(b + 1) * 64, :], ones1.ap, wb.ap[:, b, :], start=False, stop=lst_,
                            skip_group_check=True), reads=(ones1, wb), writes=(dn, op_), signal=(b == 1))
                rd = srd[s % 2]; os_ = sos[s % 2]
                kb.op("dve", lambda e, rd=rd: e.reciprocal(out=rd.ap, in_=dn.ap), reads=(dn, op_), writes=(rd,))
                kb.op("dve", lambda e, rd=rd, os_=os_: e.tensor_tensor(out=os_.ap.rearrange("p h q -> p (h q)"),
                                                                       in0=op_.ap[:, 0:256], in1=rd.ap, op=ALU.mult),
                      reads=(op_, rd), writes=(os_,))
                kb.store([(ot[4:8, :, S + s * 64:S + (s + 1) * 64].rearrange("h p n -> p h n"), os_.ap)], os_,
                         writes=(dram_ots,))
            kb.barrier()
            kb.release_dsems(cst + sos + [qs_sb, ks_sb, qs_bd, ks_bd, vs_sb, vs_bd])

            if stop_after == "SA":
                raise _Stop()
            kb.new_phase("O")
            kb.sb_off = const_end
            w_out_sb = sb(8 * D, BF16, (8, D), "w_out_sb")
            wst2 = ring(2, 4 * D, F32, (4, D), "wst2")
            for g in range(2):
                st = wst2[g % 2]
                kb.load([(st.ap, w_out[g * 512:(g + 1) * 512, :].rearrange("(c p) n -> p c n", p=128))], st)
                for c in range(4):
                    kb.op("dve", lambda e, st=st, c=c, g=g: e.tensor_scalar(
                        out=w_out_sb.ap[:, g * 4 + c, :], in0=st.ap[:, c, :], scalar1=gout_c.ap[:, g * 4 + c:g * 4 + c + 1],
                        scalar2=None, op0=ALU.mult), reads=(st, gout_c), writes=(w_out_sb,))
            oT_r = ring(2, 8 * 512, BF16, (8, 512), "oT")
            osq_r = ring(2, 8 * 512, BF16, (8, 512), "osq")
            xo_r = ring(3, D, F32, name="xo")
            h_r = ring(3, D, F32, name="h")
            rs_r = ring(4, 8, F32, name="rs")
            st_ps = [pbuf(bank(0)[:, 0:2]), pbuf(bank(1)[:, 0:2])]
            oo_ps = [(pbuf(bank(2, 2)), pbuf(bank(4, 2)))]
            dram_h = {}
            ocnt = 0
            for ti, (t0, n, is_s) in enumerate(tiles):
                oT = oT_r[ti % 2]; osq = osq_r[ti % 2]
                if is_s:
                    rd = [dram_ots]
                else:
                    rd = [dram_ot[(c, ti)] for c in range(8)]
                kb.load([(oT.ap[:, :, 0:n], ot[:, :, t0:t0 + n].rearrange("h p n -> p h n"))], oT, reads=rd)
                kb.op("act", lambda e, oT=oT, osq=osq: e.activation(out=osq.ap[:, :, 0:n], in_=oT.ap[:, :, 0:n],
                                                                    func=AF.Square), reads=(oT,), writes=(osq,))
                for blk in range(n // 128):
                    r0 = t0 + blk * 128
                    xi = xo_r[ocnt % 3]; hb = h_r[ocnt % 3]; rs = rs_r[ocnt % 4]; sp_ = st_ps[ocnt % 2]
                    pa, pb_ = oo_ps[0]
                    ocnt += 1
                    rows = x_s[blk * 128:(blk + 1) * 128, :] if is_s else x_p[r0:r0 + 128, :]
                    kb.load([(xi.ap, rows)], xi)
                    for g in range(2):
                        for c in range(4):
                            kb.op("pe", lambda e, g=g, c=c, sp_=sp_, osq=osq, blk=blk: e.matmul(
                                sp_.ap[:, g:g + 1], osq.ap[:, g * 4 + c, blk * 128:(blk + 1) * 128], ones1.ap[:, 0:1],
                                start=(c == 0), stop=(c == 3), skip_group_check=True),
                                reads=(osq, ones1), writes=(sp_,), signal=(g == 1 and c == 3))
                    kb.op("act", lambda e, rs=rs, sp_=sp_: e.activation(out=rs.ap[:, 0:2], in_=sp_.ap, func=AF.Ln,
                                                                        scale=1.0 / W, bias=EPS),
                          reads=(sp_,), writes=(rs,))
                    kb.op("act", lambda e, rs=rs: e.activation(out=rs.ap[:, 2:4], in_=rs.ap[:, 0:2], func=AF.Exp,
                                                               scale=-0.5), reads=(rs,), writes=(rs,))
                    for g, pg in ((0, pa), (1, pb_)):
                        for nh in range(2):
                            for c in range(4):
                                kb.op("pe", lambda e, g=g, nh=nh, c=c, pg=pg, oT=oT, blk=blk: e.matmul(
                                    pg.ap[:, nh * 512:(nh + 1) * 512], oT.ap[:, g * 4 + c, blk * 128:(blk + 1) * 128],
                                    w_out_sb.ap[:, g * 4 + c, nh * 512:(nh + 1) * 512], start=(c == 0), stop=(c == 3)),
                                    reads=(oT, w_out_sb), writes=(pg,), signal=(nh == 1 and c == 3))
                    kb.op("dve", lambda e, hb=hb, pa=pa, rs=rs, xi=xi: e.scalar_tensor_tensor(
                        out=hb.ap, in0=pa.ap, scalar=rs.ap[:, 2:3], in1=xi.ap, op0=ALU.mult, op1=ALU.add),
                        reads=(pa, rs, xi), writes=(hb,))
                    kb.op("dve", lambda e, hb=hb, pb_=pb_, rs=rs: e.scalar_tensor_tensor(
                        out=hb.ap, in0=pb_.ap, scalar=rs.ap[:, 3:4], in1=hb.ap, op0=ALU.mult, op1=ALU.add),
                        reads=(pb_, rs, hb), writes=(hb,))
                    d_ = Buf(None); dram_h[r0 // 128] = d_
                    kb.store([(hs[r0:r0 + 128, :], hb.ap)], hb, writes=(d_,))
            kb.barrier()
            kb.release_dsems(wst2 + oT_r + xo_r + h_r)

            if stop_after == "O":
                raise _Stop()
            kb.new_phase("F")
            kb.sb_off = const_end
            w_up_sb = sb(8 * DFF, BF16, (8, DFF), "w_up_sb")
            w_dn_sb = sb(32 * D, BF16, (32, D), "w_dn_sb")
            s4f = ring(4, 4, F32, name="s4f")
            s4g = ring(4, 4, F32, name="s4g")
            wst3_off = kb.sb_off
            wst3 = ring(2, DFF, F32, name="wst3")
            for c in range(8):
                st = wst3[c % 2]
                kb.load([(st.ap, w_up[c * 128:(c + 1) * 128, :])], st)
                kb.op("dve", lambda e, st=st, c=c: e.tensor_scalar(
                    out=w_up_sb.ap[:, c, :], in0=st.ap, scalar1=gffn_c.ap[:, c:c + 1], scalar2=None, op0=ALU.mult),
                    reads=(st, gffn_c), writes=(w_up_sb,))
            for g in range(8):
                st = wst3[g % 2]
                stv = st.ap.rearrange("p (c n) -> p c n", n=D)
                kb.load([(stv, w_down[g * 512:(g + 1) * 512, :].rearrange("(c p) n -> p c n", p=128))], st)
                kb.op("act" if g % 2 else "dve",
                      (lambda e, stv=stv, g=g: e.activation(out=w_dn_sb.ap[:, g * 4:(g + 1) * 4, :], in_=stv, func=AF.Copy))
                      if g % 2 else
                      (lambda e, stv=stv, g=g: e.tensor_copy(out=w_dn_sb.ap[:, g * 4:(g + 1) * 4, :], in_=stv)),
                      reads=(st,), writes=(w_dn_sb,))
            FT = 256
            kb.barrier()
            kb.release_dsems(wst3)
            kb.sb_off = wst3_off
            hin = ring(4, D, F32, name="hin")
            junk = ring(1, D, BF16, name="junk")
            hn_r = ring(2, D, BF16, name="hn")
            hnT = ring(2, 8 * FT, BF16, (8, FT), "hnT")
            aT = ring(1, 32 * FT, BF16, (32, FT), "aT")
            rl = ring(3, FT, BF16, name="rl")
            up_ps = [pbuf(bank(0)), pbuf(bank(1))]
            dn_ps2 = [pbuf(bank(2, 2)), pbuf(bank(4, 2))]
            ps_T = [pbuf(bank(6).bitcast(BF16)), pbuf(bank(7).bitcast(BF16))]
            ftiles = []
            for (t0, n, is_s) in tiles:
                for a in range(0, n, FT):
                    ftiles.append((t0 + a, min(FT, n - a), is_s, a))
            fstate = {"fcnt": 0, "ucnt": 0}
            fh = {}

            def f_norm(fi):
                (t0, n, is_s, a0) = ftiles[fi]
                hT = hnT[fi % 2]
                hbufs = []
                for blk in range(n // 128):
                    fcnt = fstate["fcnt"]
                    r0 = t0 + blk * 128
                    hi = hin[fcnt % 4]; ho = hn_r[fcnt % 2]; s4 = s4f[fcnt % 4]
                    hbufs.append((hi, r0))
                    kb.load([(hi.ap, hs[r0:r0 + 128, :])], hi, reads=[dram_h[r0 // 128]])
                    kb.op("act", lambda e: e.activation(out=ho.ap, in_=hi.ap, func=AF.Square, accum_out=s4.ap[:, 0:1]),
                          reads=(hi,), writes=(ho, s4))
                    kb.op("act", lambda e: e.activation(out=s4.ap[:, 1:2], in_=s4.ap[:, 0:1], func=AF.Ln,
                                                        scale=1.0 / D, bias=EPS), reads=(s4,), writes=(s4,))
                    kb.op("act", lambda e: e.activation(out=s4.ap[:, 2:3], in_=s4.ap[:, 1:2], func=AF.Exp,
                                                        scale=-0.5), reads=(s4,), writes=(s4,))
                    kb.op("dve", lambda e: e.tensor_scalar(out=ho.ap, in0=hi.ap, scalar1=s4.ap[:, 2:3],
                                                           scalar2=None, op0=ALU.mult),
                          reads=(hi, s4), writes=(ho,))
                    pT = ps_T[fcnt % 2]
                    for c in range(8):
                        kb.op("pe", lambda e, c=c: e.transpose(pT.ap[:, c * 128:(c + 1) * 128],
                                                               ho.ap[:, c * 128:(c + 1) * 128], ident.ap),
                              reads=(ho, ident), writes=(pT,), signal=(c == 7))
                    kb.op("dve", lambda e: e.tensor_copy(
                        out=hT.ap[:, :, blk * 128:(blk + 1) * 128], in_=pT.ap.rearrange("p (c n) -> p c n", n=128)),
                        reads=(pT,), writes=(hT,))
                    fstate["fcnt"] += 1
                fh[fi] = hbufs

            def f_up(fi):
                (t0, n, is_s, a0) = ftiles[fi]
                hT = hnT[fi % 2]; at = aT[0]
                for fc in range(32):
                    ucnt = fstate["ucnt"]
                    pu = up_ps[ucnt % 2]; rb = rl[ucnt % 3]; fstate["ucnt"] += 1
                    for kc in range(8):
                        kb.op("pe", lambda e, kc=kc: e.matmul(
                            pu.ap[:, 0:n], w_up_sb.ap[:, kc, fc * 128:(fc + 1) * 128], hT.ap[:, kc, 0:n],
                            start=(kc == 0), stop=(kc == 7)), reads=(w_up_sb, hT), writes=(pu,), signal=(kc == 7))
                    kb.op("act", lambda e: e.activation(out=rb.ap[:, 0:n], in_=pu.ap[:, 0:n], func=AF.Relu),
                          reads=(pu,), writes=(rb,))
                    kb.op("dve", lambda e: e.tensor_tensor(out=at.ap[:, fc, 0:n], in0=rb.ap[:, 0:n],
                                                           in1=rb.ap[:, 0:n], op=ALU.mult),
                          reads=(rb,), writes=(at,))

            def f_down(fi):
                (t0, n, is_s, a0) = ftiles[fi]
                at = aT[0]
                for blk in range(n // 128):
                    hi, r0 = fh[fi][blk]
                    pd = dn_ps2[blk % 2]
                    for nh in range(2):
                        for fc in range(32):
                            kb.op("pe", lambda e, nh=nh, fc=fc: e.matmul(
                                pd.ap[:, nh * 512:(nh + 1) * 512], at.ap[:, fc, blk * 128:(blk + 1) * 128],
                                w_dn_sb.ap[:, fc, nh * 512:(nh + 1) * 512], start=(fc == 0), stop=(fc == 31)),
                                reads=(at, w_dn_sb), writes=(pd,), signal=(nh == 1 and fc == 31))
                    s4 = s4g[(fi * 2 + blk) % 4]; jk = junk[0]
                    kb.op("dve", lambda e: e.tensor_tensor(out=hi.ap, in0=pd.ap, in1=hi.ap, op=ALU.add),
                          reads=(pd, hi), writes=(hi,))
                    kb.op("act", lambda e: e.activation(out=jk.ap, in_=hi.ap, func=AF.Square, accum_out=s4.ap[:, 0:1]),
                          reads=(hi,), writes=(jk, s4))
                    kb.op("act", lambda e: e.activation(out=s4.ap[:, 1:2], in_=s4.ap[:, 0:1], func=AF.Ln,
                                                        scale=1.0 / D, bias=EPS), reads=(s4,), writes=(s4,))
                    kb.op("act", lambda e: e.activation(out=s4.ap[:, 2:3], in_=s4.ap[:, 1:2], func=AF.Exp,
                                                        scale=-0.5), reads=(s4,), writes=(s4,))
                    kb.op("dve", lambda e: e.scalar_tensor_tensor(
                        out=hi.ap, in0=hi.ap, scalar=s4.ap[:, 2:3], in1=gfin_t.ap, op0=ALU.mult, op1=ALU.mult),
                        reads=(hi, s4, gfin_t), writes=(hi,))
                    dst = y_s[r0 - S:r0 - S + 128, :] if is_s else y_p[r0:r0 + 128, :]
                    kb.store([(dst, hi.ap)], hi)

            f_norm(0)
            for fi in range(len(ftiles)):
                f_up(fi)
                if fi + 1 < len(ftiles):
                    f_norm(fi + 1)
                f_down(fi)
            kb.barrier()

        try:
            plan()
        except _Stop:
            kb.barrier()
        with nc.Block() as block:
            @block.sync
            def _(e):
                for f in kb.q["sp"]:
                    f(e)

            @block.tensor
            def _(e):
                for f in kb.q["pe"]:
                    f(e)

            @block.scalar
            def _(e):
                for f in kb.q["act"]:
                    f(e)

            @block.vector
            def _(e):
                for f in kb.q["dve"]:
                    f(e)

            @block.gpsimd
            def _(e):
                for f in kb.q["pool"]:
                    f(e)
    return nc


_CACHE = {}


def _run(S, per_core_inputs, trace=False):
    if S not in _CACHE:
        _CACHE[S] = build_program(S)
    nc = _CACHE[S]
    return run_bass_kernel_spmd(nc, per_core_inputs, core_ids=list(range(len(per_core_inputs))), trace=trace)


def make_core_inputs(c, S, x_prompt, x_sample, cache_sb_k, cache_sb_v, cache_band_k, cache_band_v,
                     norm_mix_g, w_in, rel_bias, norm_sb_g, norm_band_g, w_out,
                     norm_ffn_g, w_up, w_down, norm_final_g):
    f = lambda a: np.ascontiguousarray(np.asarray(a, dtype=np.float32))
    return {
        "x_p": f(x_prompt[c, :S]),
        "x_s": f(x_sample[2 * c:2 * c + 2]).reshape(TS, D),
        "csk": f(cache_sb_k[0, 2 * c:2 * c + 2]).reshape(2, PAST, W),
        "csv": f(cache_sb_v[0, 2 * c:2 * c + 2]).reshape(2, PAST, W),
        "cbk": f(cache_band_k[0, 2 * c:2 * c + 2]).reshape(2, BROWS, W),
        "cbv": f(cache_band_v[0, 2 * c:2 * c + 2]).reshape(2, BROWS, W),
        "w_in": f(w_in[0]), "w_out": f(w_out[0]), "w_up": f(w_up[0]), "w_down": f(w_down[0]),
        "g_mix": f(norm_mix_g[0]), "g_ffn": f(norm_ffn_g[0]), "g_sb": f(norm_sb_g[0]), "g_bd": f(norm_band_g[0]),
        "g_fin": f(norm_final_g), "relb": f(rel_bias[0]),
    }


def assemble(results, S, nb):
    KEEP = min(512, S)
    g = lambda k: np.stack([np.asarray(r[k], dtype=np.float32) for r in results])
    y_p = g("y_p")
    y_s = g("y_s").reshape(2 * nb, 64, D)
    sbk_p = g("sbk_p").reshape(1, nb, S, 8, 64)
    sbv_p = g("sbv_p").reshape(1, nb, S, 8, 64)
    bdk_p = g("bdk_p").reshape(1, nb, KEEP, 8, 64)
    bdv_p = g("bdv_p").reshape(1, nb, KEEP, 8, 64)
    sbk_s = g("sbk_s").reshape(1, 2 * nb, 64, 8, 64)
    sbv_s = g("sbv_s").reshape(1, 2 * nb, 64, 8, 64)
    bdk_s = g("bdk_s").reshape(1, 2 * nb, 64, 8, 64)
    bdv_s = g("bdv_s").reshape(1, 2 * nb, 64, 8, 64)
    return (y_p, y_s, sbk_p, sbv_p, bdk_p, bdv_p, sbk_s, sbv_s, bdk_s, bdv_s)


def kernel(x_prompt, x_sample, cache_sb_k, cache_sb_v, cache_band_k, cache_band_v,
           norm_mix_g, w_in, rel_bias, norm_sb_g, norm_band_g, w_out,
           norm_ffn_g, w_up, w_down, norm_final_g):
    x_prompt = np.asarray(x_prompt)
    nb, S = x_prompt.shape[0], x_prompt.shape[1]
    args = (x_prompt, np.asarray(x_sample), np.asarray(cache_sb_k), np.asarray(cache_sb_v),
            np.asarray(cache_band_k), np.asarray(cache_band_v), np.asarray(norm_mix_g), np.asarray(w_in),
            np.asarray(rel_bias), np.asarray(norm_sb_g), np.asarray(norm_band_g), np.asarray(w_out),
            np.asarray(norm_ffn_g), np.asarray(w_up), np.asarray(w_down), np.asarray(norm_final_g))
    in_maps = [make_core_inputs(c, S, *args) for c in range(nb)]
    res = _run(S, in_maps)
    return assemble(res.results, S, nb)
```

```python
import contextlib
import numpy as np
import concourse.bass as bass
import concourse.mybir as mybir
from concourse.bass_utils import run_bass_kernel_spmd

F32 = mybir.dt.float32
BF16 = mybir.dt.bfloat16
U8 = mybir.dt.uint8
AF = mybir.ActivationFunctionType
ALU = mybir.AluOpType

D = 1024
W = 512
DFF = 4096
EPS = 1e-6
PAST = 2048
BROWS = 512
TS = 128
LX = 384
SB_TOTAL = 206 * 1024
SHIFT = 20.0
ESHIFT = float(np.exp(20.0))


class Buf:
    __slots__ = ("ap", "w", "r", "lsem", "ssem", "name")

    def __init__(self, ap, name=""):
        self.ap = ap
        self.w = {}
        self.r = {}
        self.lsem = None
        self.ssem = None
        self.name = name


class DSem:
    __slots__ = ("sem", "cnt", "kind")

    def __init__(self, sem):
        self.sem = sem
        self.cnt = 0
        self.kind = "pool"


ENGS = ("pe", "act", "dve", "pool", "sp")


class _Rec:
    def __init__(self):
        self.calls = []

    def __getattr__(self, name):
        def f(*a, **k):
            self.calls.append((name, a, k))
            return self
        return f


class KB:
    def __init__(self, nc, stack):
        self.nc = nc
        self.stack = stack
        self.q = {e: [] for e in ENGS}
        self.esem = {}
        self.ecnt = {}
        self.last = {}
        self.waited = {}
        self.pend = {e: [] for e in ENGS}
        self.nsem = 0
        self.dsems = []
        self.free_dsems = {}
        self.sb_off = 0
        self.sb_mark = 0

    def new_sem(self, name):
        self.nsem += 1
        return self.stack.enter_context(self.nc.semaphore(f"{name}_{self.nsem}"))

    def new_phase(self, name):
        for e in ("pe", "act", "dve", "pool"):
            self.esem[e] = self.new_sem(f"{name}_{e}")
            self.ecnt[e] = 0

    def get_dsem(self, kind="pool"):
        fl = self.free_dsems.setdefault(kind, [])
        if fl:
            return fl.pop()
        d = DSem(self.new_sem("dma" + kind))
        d.kind = kind
        self.dsems.append(d)
        return d

    def _wait(self, eng, tok):
        sem, val, peng = tok
        if peng == "pe" and eng == "pe":
            return
        key = (eng, id(sem))
        if self.waited.get(key, 0) >= val:
            return
        self.waited[key] = val
        self.q[eng].append(lambda e, sem=sem, val=val: e.wait_ge(sem, val))

    @staticmethod
    def _merge(d, tok):
        k = id(tok[0])
        if k not in d or d[k][1] < tok[1]:
            d[k] = tok

    def _deps(self, eng, reads, writes):
        for b in reads:
            for t in b.w.values():
                self._wait(eng, t)
        for b in writes:
            for t in b.w.values():
                self._wait(eng, t)
            for t in b.r.values():
                self._wait(eng, t)

    def _commit(self, tok, reads, writes):
        for b in reads:
            self._merge(b.r, tok)
        for b in writes:
            b.w = {id(tok[0]): tok}
            b.r = {}

    def op(self, eng, fn, reads=(), writes=(), signal=True):
        rec = _Rec()
        fn(rec)
        assert len(rec.calls) == 1
        mname, margs, mkw = rec.calls[0]
        fn = lambda e, mname=mname, margs=margs, mkw=mkw: getattr(e, mname)(*margs, **mkw)
        self._deps(eng, reads, writes)
        if not signal:
            self.q[eng].append(lambda e, fn=fn: fn(e))
            self.pend[eng].append((tuple(reads), tuple(writes)))
            return None
        sem = self.esem[eng]
        self.ecnt[eng] += 1
        tok = (sem, self.ecnt[eng], eng)
        self.q[eng].append(lambda e, fn=fn, sem=sem: fn(e).then_inc(sem, 1))
        for (rs, ws) in self.pend[eng]:
            self._commit(tok, rs, ws)
        self.pend[eng] = []
        self._commit(tok, reads, writes)
        self.last[eng] = tok
        return tok

    def dma(self, qeng, pairs, dsem, reads=(), writes=(), slow=False):
        self._deps(qeng, reads, writes)
        for (o, i) in pairs:
            if slow:
                self.q[qeng].append(
                    lambda e, o=o, i=i, s=dsem.sem: e.dma_start(
                        out=o, in_=i, allow_slow_non_contiguous=True).then_inc(s, 16))
            else:
                self.q[qeng].append(
                    lambda e, o=o, i=i, s=dsem.sem: e.dma_start(out=o, in_=i).then_inc(s, 16))
        dsem.cnt += 16 * len(pairs)
        tok = (dsem.sem, dsem.cnt, "dma")
        self._commit(tok, reads, writes)
        return tok

    def load(self, pairs, dst, reads=()):
        if dst.lsem is None:
            dst.lsem = self.get_dsem("sp")
        return self.dma("sp", pairs, dst.lsem, reads=reads, writes=(dst,))

    def store(self, pairs, src, writes=(), qeng="pool"):
        if src.ssem is None:
            src.ssem = self.get_dsem(qeng)
        return self.dma(qeng, pairs, src.ssem, reads=(src,), writes=writes)

    def barrier(self):
        toks = [self.last[e] for e in ("pe", "act", "dve", "pool") if e in self.last]
        toks += [(d.sem, d.cnt, "dma") for d in self.dsems if d.cnt > 0]
        for e in ENGS:
            assert not self.pend[e], e
            for t in toks:
                if t[2] == e:
                    continue
                sem, val, _ = t
                key = (e, id(sem))
                if self.waited.get(key, 0) >= val:
                    continue
                self.waited[key] = val
                self.q[e].append(lambda en, sem=sem, val=val: en.wait_ge(sem, val))

    def release_dsems(self, bufs):
        for b in bufs:
            for a in ("lsem", "ssem"):
                d = getattr(b, a)
                if d is not None:
                    self.free_dsems.setdefault(d.kind, []).append(d)
                    setattr(b, a, None)


def build_program(S, stop_after=None):
    assert S % 512 == 0
    NT = S + TS
    KEEP = min(512, S)
    NQT = S // 512
    NKB = S // 128

    nc = bass.Bass("TRN2", target_bir_lowering=False)

    def din(name, shape, dt=F32):
        return nc.dram_tensor(name, list(shape), dt, kind="ExternalInput").ap()

    def dout(name, shape, dt=F32):
        return nc.dram_tensor(name, list(shape), dt, kind="ExternalOutput").ap()

    def dscr(name, shape, dt):
        return nc.dram_tensor(name, list(shape), dt, kind="Internal").ap()

    x_p = din("x_p", [S, D])
    x_s = din("x_s", [TS, D])
    csk = din("csk", [2, PAST, W])
    csv = din("csv", [2, PAST, W])
    cbk = din("cbk", [2, BROWS, W])
    cbv = din("cbv", [2, BROWS, W])
    w_in = din("w_in", [D, 3 * D])
    w_out = din("w_out", [D, D])
    w_up = din("w_up", [D, DFF])
    w_down = din("w_down", [DFF, D])
    g_mix = din("g_mix", [D])
    g_ffn = din("g_ffn", [D])
    g_sb = din("g_sb", [W])
    g_bd = din("g_bd", [W])
    g_fin = din("g_fin", [D])
    relb = din("relb", [8, 257])

    y_p = dout("y_p", [S, D])
    y_s = dout("y_s", [TS, D])
    sbk_p = dout("sbk_p", [S, W])
    sbv_p = dout("sbv_p", [S, W])
    bdk_p = dout("bdk_p", [KEEP, W])
    bdv_p = dout("bdv_p", [KEEP, W])
    sbk_s = dout("sbk_s", [TS, W])
    sbv_s = dout("sbv_s", [TS, W])
    bdk_s = dout("bdk_s", [TS, W])
    bdv_s = dout("bdv_s", [TS, W])

    qt_sb = dscr("qt_sb", [4, 128, NT], BF16)
    kt_sb = dscr("kt_sb", [4, 128, NT], BF16)
    qt_bd = dscr("qt_bd", [4, 128, NT], BF16)
    kt_bd = dscr("kt_bd", [4, 128, NT], BF16)
    v_sb = dscr("v_sb", [NT, W], BF16)
    v_bd = dscr("v_bd", [NT, W], BF16)
    ot = dscr("ot", [8, 128, NT], BF16)
    hs = dscr("hs", [NT, D], F32)
    e_all = dscr("e_all", [8, LX], F32)
    xrep = dscr("xrep", [8, 129 * LX], F32)

    stack = contextlib.ExitStack()
    with stack:
        big = stack.enter_context(nc.sbuf_tensor("big", [128, SB_TOTAL], U8))
        psum = stack.enter_context(nc.psum_tensor("psum", [128, 4096], F32))
        kb = KB(nc, stack)

        def sb(nelem, dt, shape=None, name=""):
            size = 4 if dt == F32 else 2
            off = (kb.sb_off + 63) // 64 * 64
            nbytes = nelem * size
            assert off + nbytes <= SB_TOTAL, (name, off, nbytes)
            kb.sb_off = off + nbytes
            ap = big[:, off:off + nbytes].bitcast(dt)
            if shape is not None:
                if len(shape) == 2:
                    ap = ap.rearrange("p (a b) -> p a b", b=shape[1])
                elif len(shape) == 3:
                    ap = ap.rearrange("p (a b c) -> p a b c", b=shape[1], c=shape[2])
            return Buf(ap, name)

        def ring(n, nelem, dt, shape=None, name=""):
            return [sb(nelem, dt, shape, f"{name}{i}") for i in range(n)]

        def bank(b, nb=1):
            return psum[:, b * 512:(b + nb) * 512]

        def pbuf(ap, name=""):
            return Buf(ap, name)

        ident = sb(128, BF16, name="ident")
        tri8 = sb(128, BF16, name="tri8")
        ones8 = sb(128, BF16, name="ones8")
        ones1 = sb(64, BF16, name="ones1")
        mc = sb(128, F32, name="mc")
        mfar = sb(128, F32, name="mfar")
        wn_b = sb(8 * 256, BF16, (8, 256), "wn_b")
        en_b = sb(8 * 128, BF16, (8, 128), "en_b")
        mfar_b = sb(128, BF16, name="mfar_b")
        gfin_t = sb(D, F32, name="gfin")
        gmix_c = sb(8, F32, name="gmixc")
        gffn_c = sb(8, F32, name="gffnc")
        gout_c = sb(8, F32, name="goutc")
        negc = sb(8, F32, name="negc")
        fs3 = sb(512, F32, (2, 256), "fs3")
        fsn = sb(512, F32, (2, 256), "fsn")
        const_end = kb.sb_off
        kb.sb_off = SB_TOTAL - 18 * 1024
        mnear = sb(256, F32, name="mnear")
        wn = sb(8 * 256, F32, (8, 256), "wn")
        en = sb(8 * 256, F32, (8, 256), "en")
        w_tmp_lo = SB_TOTAL - 18 * 1024
        kb.sb_off = const_end

        dram_e = Buf(None, "e_all")
        dram_x = Buf(None, "xrep")
        setup_sem = kb.get_dsem()

        class _Stop(Exception):
            pass

        def plan():
            kb.new_phase("W")
            kb.op("pool", lambda e: e.memset(ident.ap, 0.0), writes=(ident,))
            kb.op("pool", lambda e: e.affine_select(out=ident.ap, in_=ident.ap, pattern=[[-1, 128]],
                                                    compare_op=ALU.not_equal, fill=1.0, base=0,
                                                    channel_multiplier=1), writes=(ident,))
            kb.op("pool", lambda e: e.memset(tri8.ap, -8.0), writes=(tri8,))
            kb.op("pool", lambda e: e.affine_select(out=tri8.ap, in_=tri8.ap, pattern=[[-1, 128]],
                                                    compare_op=ALU.is_ge, fill=0.0, base=0,
                                                    channel_multiplier=1), writes=(tri8,))
            kb.op("pool", lambda e: e.memset(ones8.ap, -8.0), writes=(ones8,))
            kb.op("pool", lambda e: e.memset(ones1.ap, 1.0), writes=(ones1,))
            kb.op("pool", lambda e: e.memset(mc.ap, 1.0), writes=(mc,))
            kb.op("pool", lambda e: e.affine_select(out=mc.ap, in_=mc.ap, pattern=[[1, 128]],
                                                    compare_op=ALU.is_gt, fill=0.0, base=0,
                                                    channel_multiplier=-1), writes=(mc,))
            kb.op("pool", lambda e: e.memset(mfar.ap, 1.0), writes=(mfar,))
            kb.op("pool", lambda e: e.memset(mfar.ap[0:64, 64:128], 0.0), writes=(mfar,))
            kb.op("pool", lambda e: e.memset(mnear.ap, 1.0), writes=(mnear,))
            kb.op("pool", lambda e: e.memset(mnear.ap[64:128, 0:64], 0.0), writes=(mnear,))

            setup_sems = []

            def bc_load(dst, src_ap):
                ds = kb.get_dsem(); setup_sems.append(ds)
                kb.q["pool"].append(lambda e, o=dst.ap, i=src_ap, s=ds.sem:
                                    e.dma_start(out=o, in_=i, allow_slow_non_contiguous=True).then_inc(s, 16))
                ds.cnt += 16
                tok = (ds.sem, ds.cnt, "dma")
                dst.w = {id(tok[0]): tok}

            bc_load(gfin_t, g_fin.rearrange("(o n) -> o n", o=1).broadcast_to([128, D]))
            def col3(b):
                return Buf(b.ap.rearrange("p (c o) -> p c o", o=1))
            gm3 = col3(gmix_c); gf3 = col3(gffn_c)
            bc_load(gm3, g_mix.rearrange("(c p o) -> p c o", p=128, o=1))
            bc_load(gf3, g_ffn.rearrange("(c p o) -> p c o", p=128, o=1))
            gmix_c.w = dict(gm3.w); gffn_c.w = dict(gf3.w)
            gout_c_a = Buf(gout_c.ap[:, 0:4].rearrange("p (c o) -> p c o", o=1))
            gout_c_b = Buf(gout_c.ap[:, 4:8].rearrange("p (c o) -> p c o", o=1))
            bc_load(gout_c_a, g_sb.rearrange("(c p o) -> p c o", p=128, o=1))
            bc_load(gout_c_b, g_bd.rearrange("(c p o) -> p c o", p=128, o=1))
            gout_c.w = dict(gout_c_a.w); gout_c.w.update(gout_c_b.w)
            ng3 = col3(negc)
            bc_load(ng3, relb[:, 256:257].rearrange("(x h) o -> x h o", x=1).broadcast_to([128, 8, 1]))
            negc.w = dict(ng3.w)
            kb.op("dve", lambda e: e.tensor_scalar(out=negc.ap, in0=negc.ap, scalar1=-1.0, scalar2=None,
                                                   op0=ALU.mult), reads=(negc,), writes=(negc,))
            kb.dma("pool", [(e_all[:, 0:129], relb[:, 128:257]),
                          (e_all[:, 129:257].rearrange("h (n o) -> h n o", o=1),
                           relb[:, 256:257].rearrange("h (n o) -> h n o", o=1).broadcast_to([8, 128, 1])),
                          (e_all[:, 257:384], relb[:, 1:128])], setup_sem, writes=(dram_e,), slow=True)
            setup_sem2 = kb.get_dsem(); setup_sem3 = kb.get_dsem()
            kb.dma("pool", [(xrep.rearrange("h (r l) -> h r l", l=LX),
                           e_all.rearrange("h (o l) -> h o l", o=1).broadcast_to([8, 129, LX]))],
                   setup_sem2, reads=(dram_e,), writes=(dram_x,), slow=True)
            en_src = bass.AP(xrep.tensor, 0, [[LX - 1, 128], [129 * LX, 8], [1, 256]])
            kb.dma("pool", [(en.ap, en_src)], setup_sem3, reads=(dram_x,), writes=(en,), slow=True)
            for h in range(8):
                kb.op("act", lambda e, h=h: e.activation(out=en.ap[:, h, :], in_=en.ap[:, h, :], func=AF.Exp,
                                                         bias=negc.ap[:, h:h + 1], scale=1.0),
                      reads=(en, negc), writes=(en,))
            for h in range(8):
                kb.op("dve", lambda e, h=h: e.tensor_tensor(out=wn.ap[:, h, :], in0=en.ap[:, h, :],
                                                            in1=mnear.ap, op=ALU.mult),
                      reads=(en, mnear), writes=(wn,))
            kb.op("dve", lambda e: e.tensor_copy(out=wn_b.ap, in_=wn.ap), reads=(wn,), writes=(wn_b,))
            kb.op("dve", lambda e: e.tensor_copy(out=en_b.ap, in_=en.ap[:, :, 128:256]), reads=(en,), writes=(en_b,))
            kb.op("dve", lambda e: e.tensor_copy(out=mfar_b.ap, in_=mfar.ap), reads=(mfar,), writes=(mfar_b,))
            kb.op("pool", lambda e: e.memset(fsn.ap, 0.0), writes=(fsn,))
            for h in range(8):
                b_, hh = h % 2, h // 2
                kb.op("dve", lambda e, h=h, b_=b_, hh=hh: e.tensor_copy(
                    out=fs3.ap[:, b_, hh * 64:(hh + 1) * 64], in_=en.ap[:, h, 128:192]),
                    reads=(en,), writes=(fs3,))
                kb.op("dve", lambda e, h=h, b_=b_, hh=hh: e.tensor_copy(
                    out=fsn.ap[0:64, b_, hh * 64:(hh + 1) * 64], in_=en.ap[0:64, h, 0:64]),
                    reads=(en,), writes=(fsn,))

            if stop_after == "W":
                raise _Stop()
            kb.sb_off = const_end
            w_in_sb = sb(8 * 3072, BF16, (8, 3072), "w_in_sb")
            wst = ring(2, 3072, F32, name="wst")
            for c in range(8):
                st = wst[c % 2]
                kb.load([(st.ap, w_in[c * 128:(c + 1) * 128, :])], st)
                kb.op("dve", lambda e, c=c, st=st: e.tensor_scalar(
                    out=w_in_sb.ap[:, c, :], in0=st.ap, scalar1=gmix_c.ap[:, c:c + 1], scalar2=None,
                    op0=ALU.mult), reads=(st, gmix_c), writes=(w_in_sb,))
            p_mark = kb.sb_off
            xin = ring(4, D, F32, name="xin")
            xn = ring(2, D, BF16, name="xn")
            ssb = ring(4, 4, F32, name="ss")
            xnT = ring(2, 8 * 512, BF16, (8, 512), "xnT")
            fmst = ring(2, 16 * 512, BF16, (16, 512), "fmst")
            tmst = ring(4, 512, F32, name="tmst")
            vst = ring(4, 512, BF16, name="vst")
            assert kb.sb_off <= w_tmp_lo, kb.sb_off
            ps_fm = [pbuf(bank(0)), pbuf(bank(1)), pbuf(bank(2))]
            ps_tm = [pbuf(bank(3)), pbuf(bank(4)), pbuf(bank(5))]
            ps_T = [pbuf(bank(6).bitcast(BF16)), pbuf(bank(7).bitcast(BF16))]
            dram_q = {}

            def dbuf(key):
                if key not in dram_q:
                    dram_q[key] = Buf(None, str(key))
                return dram_q[key]

            tiles = [(i * 512, 512, False) for i in range(NQT)] + [(S, TS, True)]
            cnt = {"blk": 0, "fm": 0, "tm": 0, "tile": 0, "ev": 0, "pt": 0}

            def rms_block(src_rows, xi, xo, s4):
                kb.load([(xi.ap, src_rows)], xi)
                kb.op("act", lambda e: e.activation(out=xo.ap, in_=xi.ap, func=AF.Square,
                                                    accum_out=s4.ap[:, 0:1]),
                      reads=(xi,), writes=(xo, s4))
                kb.op("act", lambda e: e.activation(out=s4.ap[:, 1:2], in_=s4.ap[:, 0:1], func=AF.Ln,
                                                    scale=1.0 / D, bias=EPS), reads=(s4,), writes=(s4,))
                kb.op("act", lambda e: e.activation(out=s4.ap[:, 2:3], in_=s4.ap[:, 1:2], func=AF.Exp,
                                                    scale=-0.5), reads=(s4,), writes=(s4,))
                kb.op("dve", lambda e: e.tensor_scalar(out=xo.ap, in0=xi.ap, scalar1=s4.ap[:, 2:3],
                                                       scalar2=None, op0=ALU.mult),
                      reads=(xi, s4), writes=(xo,))

            def transpose_block(xo, dstT, blk, evac_eng):
                pT = ps_T[cnt["pt"] % 2]; cnt["pt"] += 1
                for c in range(8):
                    kb.op("pe", lambda e, c=c, pT=pT: e.transpose(pT.ap[:, c * 128:(c + 1) * 128],
                                                                 xo.ap[:, c * 128:(c + 1) * 128], ident.ap),
                          reads=(xo, ident), writes=(pT,), signal=(c == 7))
                src = pT.ap.rearrange("p (c n) -> p c n", n=128)
                dst = dstT.ap[:, :, blk * 128:(blk + 1) * 128]
                if evac_eng == "act":
                    kb.op("act", lambda e: e.activation(out=dst, in_=src, func=AF.Copy),
                          reads=(pT,), writes=(dstT,))
                else:
                    kb.op("dve", lambda e: e.tensor_copy(out=dst, in_=src), reads=(pT,), writes=(dstT,))

            def p_norm(ti):
                (t0, n, is_s) = tiles[ti]
                xT = xnT[ti % 2]
                for blk in range(n // 128):
                    bi = cnt["blk"]
                    xi = xin[bi % 4]; xo = xn[bi % 2]; s4 = ssb[bi % 4]
                    rows = x_s[blk * 128:(blk + 1) * 128, :] if is_s else x_p[t0 + blk * 128:t0 + (blk + 1) * 128, :]
                    rms_block(rows, xi, xo, s4)
                    transpose_block(xo, xT, blk, "dve")
                    cnt["blk"] += 1

            def p_mm(ti):
                (t0, n, is_s) = tiles[ti]
                nb = n // 128
                xT = xnT[ti % 2]
                fst = fmst[ti % 2]
                fm_cols = [0 * 512, 1 * 512, 3 * 512, 4 * 512]
                for g4 in (0, 2, 3):
                    for c4 in range(4):
                        oc = g4 * 4 + c4
                        col0 = fm_cols[g4] + c4 * 128
                        pf = ps_fm[cnt["fm"] % 3]; cnt["fm"] += 1
                        for kc in range(8):
                            kb.op("pe", lambda e, kc=kc, pf=pf, col0=col0: e.matmul(
                                pf.ap[:, 0:n], w_in_sb.ap[:, kc, col0:col0 + 128], xT.ap[:, kc, 0:n],
                                start=(kc == 0), stop=(kc == 7)),
                                reads=(w_in_sb, xT), writes=(pf,), signal=(kc == 7))
                        if cnt["ev"] % 3 != 2:
                            kb.op("act", lambda e, pf=pf, oc=oc: e.activation(out=fst.ap[:, oc, 0:n], in_=pf.ap[:, 0:n],
                                                                              func=AF.Copy),
                                  reads=(pf,), writes=(fst,))
                        else:
                            kb.op("dve", lambda e, pf=pf, oc=oc: e.tensor_copy(out=fst.ap[:, oc, 0:n], in_=pf.ap[:, 0:n]),
                                  reads=(pf,), writes=(fst,))
                        cnt["ev"] += 1
                need_bd = is_s or (t0 + n > S - KEEP)
                for blk in range(nb):
                    r0 = t0 + blk * 128
                    groups = [("k_sb", 512), ("v_sb", 1024), ("v_bd", 2560)]
                    if need_bd:
                        groups.append(("k_bd", 2048))
                    for (gname, gcol) in groups:
                        pt = ps_tm[cnt["tm"] % 3]
                        for kc in range(8):
                            kb.op("pe", lambda e, kc=kc, pt=pt, gcol=gcol, blk=blk: e.matmul(
                                pt.ap, xT.ap[:, kc, blk * 128:(blk + 1) * 128], w_in_sb.ap[:, kc, gcol:gcol + 512],
                                start=(kc == 0), stop=(kc == 7)),
                                reads=(w_in_sb, xT), writes=(pt,), signal=(kc == 7))
                        ts_ = tmst[cnt["tm"] % 4]
                        vs_ = vst[cnt["tm"] % 4]
                        cnt["tm"] += 1
                        kb.op("dve", lambda e, pt=pt, ts_=ts_: e.tensor_copy(out=ts_.ap, in_=pt.ap),
                              reads=(pt,), writes=(ts_,))
                        outs = []
                        if gname == "k_sb":
                            outs.append(sbk_s[blk * 128:(blk + 1) * 128, :] if is_s else sbk_p[r0:r0 + 128, :])
                        elif gname == "v_sb":
                            outs.append(sbv_s[blk * 128:(blk + 1) * 128, :] if is_s else sbv_p[r0:r0 + 128, :])
                        elif gname == "k_bd":
                            outs.append(bdk_s[blk * 128:(blk + 1) * 128, :] if is_s
                                        else bdk_p[r0 - (S - KEEP):r0 - (S - KEEP) + 128, :])
                        elif gname == "v_bd" and need_bd:
                            outs.append(bdv_s[blk * 128:(blk + 1) * 128, :] if is_s
                                        else bdv_p[r0 - (S - KEEP):r0 - (S - KEEP) + 128, :])
                        if outs:
                            kb.store([(o, ts_.ap) for o in outs], ts_)
                        if gname == "k_sb":
                            kbf = vs_
                            kb.op("act", lambda e, ts_=ts_, kbf=kbf: e.activation(out=kbf.ap, in_=ts_.ap, func=AF.Copy),
                                  reads=(ts_,), writes=(kbf,))
                            kdefer = kbf
                        if gname in ("v_sb", "v_bd"):
                            kb.op("act", lambda e, ts_=ts_, vs_=vs_: e.activation(out=vs_.ap, in_=ts_.ap, func=AF.Copy),
                                  reads=(ts_,), writes=(vs_,))
                            dstv = v_sb if gname == "v_sb" else v_bd
                            kb.store([(dstv[r0:r0 + 128, :], vs_.ap)], vs_, writes=(dbuf((gname, r0 // 128)),))
                    pT = ps_T[cnt["pt"] % 2]; cnt["pt"] += 1
                    for c4 in range(4):
                        kb.op("pe", lambda e, c4=c4, pT=pT, kdefer=kdefer: e.transpose(
                            pT.ap[:, c4 * 128:(c4 + 1) * 128], kdefer.ap[:, c4 * 128:(c4 + 1) * 128], ident.ap),
                            reads=(kdefer, ident), writes=(pT,), signal=(c4 == 3))
                    kb.op("dve", lambda e, pT=pT, blk=blk: e.tensor_copy(
                        out=fst.ap[:, 4:8, blk * 128:(blk + 1) * 128],
                        in_=pT.ap[:, 0:512].rearrange("p (c n) -> p c n", n=128)),
                        reads=(pT,), writes=(fst,))
                pairs = []
                for g4, dst in enumerate((qt_sb, kt_sb, qt_bd, kt_bd)):
                    pairs.append((dst[:, :, t0:t0 + n].rearrange("h p n -> p h n"), fst.ap[:, g4 * 4:(g4 + 1) * 4, 0:n]))
                kb.store(pairs, fst, writes=(dbuf(("fm", ti)),))
            p_norm(0)
            for ti in range(len(tiles)):
                if ti + 1 < len(tiles):
                    p_norm(ti + 1)
                p_mm(ti)
            kb.barrier()
            kb.release_dsems(wst + xin + fmst + tmst + vst)

            if stop_after == "P":
                raise _Stop()
            kb.new_phase("SB")
            kb.sb_off = const_end
            qkv = []
            for i in range(2):
                qkv.append((sb(S, BF16, name=f"QT{i}"), sb(S, BF16, name=f"KT{i}"),
                            sb(NKB * 128, BF16, (NKB, 128), f"V{i}")))
            l_r = ring(3, 1024, BF16, (2, 512), "L")
            w_r = ring(3, 1024, BF16, (2, 512), "w")
            ra_r = ring(3, 1024, BF16, (2, 512), "ra")
            ost = ring(2, 512, BF16, name="ost")
            zc_ps = [pbuf(bank(2 * i_, 2).rearrange("p (b n) -> p b n", n=512)) for i_ in range(3)]
            o_ps = [pbuf(bank(6)), pbuf(bank(7))]
            dram_ot = {}

            def load_qkv(hp, slot, qsrc, ksrc, vsrc, vkey):
                QT, KT, V = qkv[slot]
                rd = [dbuf(("fm", t)) for t in range(NQT)]
                kb.load([(QT.ap, qsrc[hp, :, 0:S])], QT, reads=rd)
                kb.load([(KT.ap, ksrc[hp, :, 0:S])], KT, reads=rd)
                rdv = [dbuf((vkey, b)) for b in range(NKB)]
                pairs = []
                for b0 in range(0, NKB, 16):
                    b1 = min(NKB, b0 + 16)
                    pairs.append((V.ap[:, b0:b1, :],
                                  vsrc[b0 * 128:b1 * 128, hp * 128:(hp + 1) * 128].rearrange("(b p) f -> p b f", p=128)))
                kb.load(pairs, V, reads=rdv)

            its = []
            for hp in range(4):
                for i in range(NQT):
                    js = list(range(4 * i + 3, -1, -1))
                    for n_, j in enumerate(js):
                        m = j - 4 * i
                        c0 = 128 * m if m > 0 else 0
                        its.append(dict(hp=hp, i=i, j=j, c0=c0, diag=(m >= 0), first=(n_ == 0),
                                        last=(n_ == len(js) - 1), slot=hp % 2, qt=hp * NQT + i))
            NIT = len(its)
            load_qkv(0, 0, qt_sb, kt_sb, v_sb, "v_sb")
            loaded = {0}

            def st_qk(k):
                it = its[k]
                QT, KT, V = qkv[it["slot"]]
                z = zc_ps[k % 3]; c0 = it["c0"]; i = it["i"]; j = it["j"]
                for b in range(2):
                    kb.op("pe", lambda e, b=b: e.matmul(
                        z.ap[:, b, c0:512], KT.ap[b * 64:(b + 1) * 64, j * 128:(j + 1) * 128],
                        QT.ap[b * 64:(b + 1) * 64, i * 512 + c0:(i + 1) * 512], start=True, stop=True),
                        reads=(QT, KT), writes=(z,), signal=(b == 1))

            def st_l(k):
                it = its[k]; z = zc_ps[k % 3]; lb = l_r[k % 3]; c0 = it["c0"]
                if c0 > 0:
                    kb.op("pool", lambda e: e.memset(lb.ap[:, :, 0:c0], 0.0), writes=(lb,))
                kb.op("act", lambda e: e.activation(out=lb.ap[:, :, c0:512], in_=z.ap[:, :, c0:512],
                                                    func=AF.Softplus, scale=0.125), reads=(z,), writes=(lb,))
                if it["diag"]:
                    for b in range(2):
                        kb.op("dve", lambda e, b=b: e.tensor_tensor(out=lb.ap[:, b, c0:c0 + 128],
                                                                    in0=lb.ap[:, b, c0:c0 + 128], in1=mc.ap,
                                                                    op=ALU.mult), reads=(lb, mc), writes=(lb,))

            def st_ra(k):
                it = its[k]
                if it["last"]:
                    return
                lb = l_r[k % 3]; rn = ra_r[(k + 1) % 3]; rc = ra_r[k % 3]
                if it["first"]:
                    kb.op("dve", lambda e: e.tensor_copy(out=rn.ap, in_=lb.ap), reads=(lb,), writes=(rn,))
                else:
                    kb.op("dve", lambda e: e.tensor_tensor(out=rn.ap, in0=rc.ap, in1=lb.ap, op=ALU.add),
                          reads=(rc, lb), writes=(rn,))

            def st_c(k):
                it = its[k]; lb = l_r[k % 3]; rc = ra_r[k % 3]; cp = zc_ps[k % 3]
                for b in range(2):
                    kb.op("pe", lambda e, b=b: e.matmul(cp.ap[:, b, :], tri8.ap, lb.ap[:, b, :], start=False,
                                                        stop=it["first"], skip_group_check=True),
                          reads=(tri8, lb), writes=(cp,), signal=(it["first"] and b == 1))
                    if not it["first"]:
                        kb.op("pe", lambda e, b=b: e.matmul(cp.ap[:, b, :], ones8.ap, rc.ap[:, b, :], start=False,
                                                            stop=True, skip_group_check=True),
                              reads=(ones8, rc), writes=(cp,), signal=(b == 1))

            def st_w(k):
                it = its[k]; cp = zc_ps[k % 3]; wb = w_r[k % 3]; c0 = it["c0"]
                if c0 > 0:
                    kb.op("pool", lambda e: e.memset(wb.ap[:, :, 0:c0], 0.0), writes=(wb,))
                kb.op("act", lambda e: e.activation(out=wb.ap[:, :, c0:512], in_=cp.ap[:, :, c0:512],
                                                    func=AF.Softplus, scale=0.125, bias=-SHIFT),
                      reads=(cp,), writes=(wb,))
                if it["diag"]:
                    for b in range(2):
                        kb.op("dve", lambda e, b=b: e.tensor_tensor(out=wb.ap[:, b, c0:c0 + 128],
                                                                    in0=wb.ap[:, b, c0:c0 + 128], in1=mc.ap,
                                                                    op=ALU.mult), reads=(wb, mc), writes=(wb,))

            def st_pv(k):
                it = its[k]; wb = w_r[k % 3]; QT, KT, V = qkv[it["slot"]]
                op_ = o_ps[it["qt"] % 2]; j = it["j"]
                for b in range(2):
                    kb.op("pe", lambda e, b=b: e.matmul(op_.ap[b * 64:(b + 1) * 64, :], V.ap[:, j, b * 64:(b + 1) * 64],
                                                        wb.ap[:, b, :], start=it["first"], stop=it["last"]),
                          reads=(V, wb), writes=(op_,), signal=(b == 1))
                if it["last"]:
                    os_ = ost[it["qt"] % 2]
                    kb.op("dve", lambda e: e.tensor_scalar(out=os_.ap, in0=op_.ap, scalar1=ESHIFT, scalar2=None,
                                                           op0=ALU.mult), reads=(op_,), writes=(os_,))
                    t0 = it["i"] * 512
                    d_ = Buf(None); dram_ot[(it["hp"], it["i"])] = d_
                    kb.store([(ot[it["hp"], :, t0:t0 + 512], os_.ap)], os_, writes=(d_,))

            st_qk(0)
            for r in range(NIT + 3):
                if 0 <= r - 2 < NIT:
                    it = its[r - 2]
                    if it["i"] == 0 and it["first"] and it["hp"] + 1 < 4 and (it["hp"] + 1) not in loaded:
                        load_qkv(it["hp"] + 1, (it["hp"] + 1) % 2, qt_sb, kt_sb, v_sb, "v_sb")
                        loaded.add(it["hp"] + 1)
                if r + 1 < NIT:
                    st_qk(r + 1)
                if r < NIT:
                    st_l(r)
                if 0 <= r - 1 < NIT:
                    st_w(r - 1)
                if r < NIT:
                    st_ra(r)
                    st_c(r)
                if 0 <= r - 2 < NIT:
                    st_pv(r - 2)
            kb.barrier()

            if stop_after == "SB":
                raise _Stop()
            kb.new_phase("BD")
            wb_r = ring(3, 1024, BF16, (2, 512), "wb")
            rd_r = ring(2, 512, F32, name="rden")
            zb_ps = [pbuf(bank(2 * i_, 2).rearrange("p (b n) -> p b n", n=512)) for i_ in range(3)]
            ob_ps = [pbuf(bank(6))]
            dn_ps = [pbuf(bank(7))]
            load_qkv(0, 0, qt_bd, kt_bd, v_bd, "v_bd")
            brecs = []
            for hp in range(4):
                for i in range(NQT):
                    blocks = [(4 * i + m, 128 * m, 512, "near", m) for m in range(4)]
                    if i > 0:
                        blocks += [(4 * i - 4 + jj, 0, 128 * (jj + 1), "far", jj) for jj in range(4)]
                    for n_, (j, a, b_, kind, m) in enumerate(blocks):
                        brecs.append(dict(hp=hp, i=i, j=j, a=a, b_=b_, kind=kind, m=m, first=(n_ == 0),
                                          last=(n_ == len(blocks) - 1), qi=hp * NQT + i))
            NBR = len(brecs)

            def bd_qk(n):
                rc = brecs[n]; QT, KT, V = qkv[rc["hp"] % 2]; z = zb_ps[n % 3]
                j, a, b_, i = rc["j"], rc["a"], rc["b_"], rc["i"]
                for b in range(2):
                    kb.op("pe", lambda e, b=b: e.matmul(
                        z.ap[:, b, a:b_], KT.ap[b * 64:(b + 1) * 64, j * 128:(j + 1) * 128],
                        QT.ap[b * 64:(b + 1) * 64, i * 512 + a:i * 512 + b_], start=True, stop=True),
                        reads=(QT, KT), writes=(z,), signal=(b == 1))

            wbh = [[Buf(w_.ap[:, b]) for b in range(2)] for w_ in wb_r]

            def bd_exp(n):
                rc = brecs[n]; z = zb_ps[n % 3]; wb = wb_r[n % 3]; wh = wbh[n % 3]
                a, b_, kind, m, hp = rc["a"], rc["b_"], rc["kind"], rc["m"], rc["hp"]
                kb.op("act", lambda e: e.activation(out=wb.ap[:, :, a:b_], in_=z.ap[:, :, a:b_], func=AF.Exp,
                                                    scale=0.125), reads=(z,), writes=(wh[0], wh[1]))
                for b in range(2):
                    h = 2 * hp + b
                    if kind == "near":
                        wd = min(256, 512 - a)
                        kb.op("dve", lambda e, b=b, wd=wd, h=h: e.tensor_tensor(
                            out=wb.ap[:, b, a:a + wd], in0=wb.ap[:, b, a:a + wd], in1=wn_b.ap[:, h, 0:wd],
                            op=ALU.mult), reads=(wh[b], wn_b), writes=(wh[b],))
                    else:
                        kb.op("dve", lambda e, b=b: e.tensor_tensor(
                            out=wb.ap[:, b, b_ - 128:b_], in0=wb.ap[:, b, b_ - 128:b_], in1=mfar_b.ap,
                            op=ALU.mult), reads=(wh[b], mfar_b), writes=(wh[b],))
                        if m == 3:
                            kb.op("dve", lambda e, b=b, h=h: e.tensor_tensor(
                                out=wb.ap[:, b, 0:128], in0=wb.ap[:, b, 0:128], in1=en_b.ap[:, h, :],
                                op=ALU.mult), reads=(wh[b], en_b), writes=(wh[b],))

            def bd_pv(n):
                rc = brecs[n]; QT, KT, V = qkv[rc["hp"] % 2]; wb = wb_r[n % 3]; wh = wbh[n % 3]
                j, a, b_, qi = rc["j"], rc["a"], rc["b_"], rc["qi"]
                op_ = ob_ps[0]; dn = dn_ps[0]
                for b in range(2):
                    kb.op("pe", lambda e, b=b: e.matmul(
                        op_.ap[b * 64:(b + 1) * 64, a:b_], V.ap[:, j, b * 64:(b + 1) * 64], wb.ap[:, b, a:b_],
                        start=rc["first"], stop=rc["last"]), reads=(V, wh[b]), writes=(op_,), signal=False)
                for b in range(2):
                    kb.op("pe", lambda e, b=b: e.matmul(
                        dn.ap[b * 64:(b + 1) * 64, a:b_], ones1.ap, wb.ap[:, b, a:b_],
                        start=rc["first"], stop=rc["last"]), reads=(ones1, wh[b]), writes=(dn,), signal=(b == 1))
                if rc["last"]:
                    rd = rd_r[qi % 2]; os_ = ost[qi % 2]
                    kb.op("dve", lambda e: e.reciprocal(out=rd.ap, in_=dn.ap), reads=(dn,), writes=(rd,))
                    kb.op("dve", lambda e: e.tensor_tensor(out=os_.ap, in0=op_.ap, in1=rd.ap, op=ALU.mult),
                          reads=(op_, rd), writes=(os_,))
                    d_ = Buf(None); dram_ot[(4 + rc["hp"], rc["i"])] = d_
                    kb.store([(ot[4 + rc["hp"], :, rc["i"] * 512:(rc["i"] + 1) * 512], os_.ap)], os_, writes=(d_,))

            bloaded = {0}
            bd_qk(0)
            bd_qk(1)
            for r in range(NBR + 1):
                if 0 <= r - 1 < NBR:
                    rc = brecs[r - 1]
                    if rc["i"] == 0 and rc["first"] and rc["hp"] + 1 < 4 and (rc["hp"] + 1) not in bloaded:
                        load_qkv(rc["hp"] + 1, (rc["hp"] + 1) % 2, qt_bd, kt_bd, v_bd, "v_bd")
                        bloaded.add(rc["hp"] + 1)
                if r + 2 < NBR:
                    bd_qk(r + 2)
                if r < NBR:
                    bd_exp(r)
                if 0 <= r - 1 < NBR:
                    bd_pv(r - 1)
            kb.barrier()
            kb.release_dsems([b for t in qkv for b in t] + ost)

            if stop_after == "BD":
                raise _Stop()
            kb.new_phase("SA")
            kb.sb_off = const_end
            ktc = sb(2 * 4 * PAST, BF16, (2, 4, PAST), "ktc")
            vc = sb(2 * 16 * W, BF16, (2, 16, W), "vc")
            ktb = sb(2 * 4 * BROWS, BF16, (2, 4, BROWS), "ktb")
            vbc = sb(2 * 4 * W, BF16, (2, 4, W), "vbc")
            cst = ring(2, 4 * W, F32, (4, W), "cst")
            cbf = ring(2, 4 * W, BF16, (4, W), "cbf")
            qs_sb = sb(4 * TS, BF16, (4, TS), "qs_sb")
            ks_sb = sb(4 * 2 * 128, BF16, (4, 2, 128), "ks_sb")
            qs_bd = sb(4 * TS, BF16, (4, TS), "qs_bd")
            ks_bd = sb(4 * 2 * 128, BF16, (4, 2, 128), "ks_bd")
            vs_sb = sb(2 * W, BF16, (2, W), "vs_sb")
            vs_bd = sb(2 * W, BF16, (2, W), "vs_bd")
            sl_r = ring(3, 512, BF16, (2, 256), "sl")
            sw_r = ring(3, 512, BF16, (2, 256), "sw")
            sra_r = ring(3, 512, BF16, (2, 256), "sra")
            sos = ring(2, 256, BF16, (4, 64), "sos")
            srd = ring(2, 256, F32, name="srd")
            fm_s = [dbuf(("fm", NQT))]
            kb.op("pool", lambda e: e.memset(ks_sb.ap, 0.0), writes=(ks_sb,))
            kb.op("pool", lambda e: e.memset(ks_bd.ap, 0.0), writes=(ks_bd,))
            kb.op("pool", lambda e: e.memset(vs_sb.ap, 0.0), writes=(vs_sb,))
            kb.op("pool", lambda e: e.memset(vs_bd.ap, 0.0), writes=(vs_bd,))
            kb.load([(qs_sb.ap, qt_sb[:, :, S:S + TS].rearrange("h p n -> p h n"))], qs_sb, reads=fm_s)
            kb.load([(qs_bd.ap, qt_bd[:, :, S:S + TS].rearrange("h p n -> p h n"))], qs_bd, reads=fm_s)
            kb.load([(ks_sb.ap[:, :, s, 0:64], kt_sb[:, :, S + s * 64:S + (s + 1) * 64].rearrange("h p n -> p h n"))
                     for s in range(2)], ks_sb, reads=fm_s)
            kb.load([(ks_bd.ap[:, :, s, 0:64], kt_bd[:, :, S + s * 64:S + (s + 1) * 64].rearrange("h p n -> p h n"))
                     for s in range(2)], ks_bd, reads=fm_s)
            kb.load([(vs_sb.ap[0:64, s, :], v_sb[S + s * 64:S + (s + 1) * 64, :]) for s in range(2)], vs_sb,
                    reads=[dbuf(("v_sb", S // 128))])
            kb.load([(vs_bd.ap[0:64, s, :], v_bd[S + s * 64:S + (s + 1) * 64, :]) for s in range(2)], vs_bd,
                    reads=[dbuf(("v_bd", S // 128))])
            sT = [pbuf(bank(7).bitcast(BF16))]
            ccnt = 0
            tcnt = 0
            for (ksrc, vsrc, nblk, kdst, vdst) in ((csk, csv, 16, ktc, vc), (cbk, cbv, 4, ktb, vbc)):
                for s in range(2):
                    for g in range(nblk // 4):
                        st = cst[ccnt % 2]; cb = cbf[ccnt % 2]; ccnt += 1
                        kb.load([(st.ap, ksrc[s, g * 512:(g + 1) * 512, :].rearrange("(b p) f -> p b f", p=128))], st)
                        kb.op("dve", lambda e, st=st, cb=cb: e.tensor_copy(out=cb.ap, in_=st.ap), reads=(st,), writes=(cb,))
                        for hp in range(4):
                            pT = sT[0]; tcnt += 1
                            for b4 in range(4):
                                kb.op("pe", lambda e, pT=pT, cb=cb, b4=b4, hp=hp: e.transpose(
                                    pT.ap[:, b4 * 128:(b4 + 1) * 128], cb.ap[:, b4, hp * 128:(hp + 1) * 128], ident.ap),
                                    reads=(cb, ident), writes=(pT,), signal=(b4 == 3))
                            kb.op("act", lambda e, pT=pT, s=s, hp=hp, g=g, kdst=kdst: e.activation(
                                out=kdst.ap[:, s, hp, g * 512:(g + 1) * 512], in_=pT.ap[:, 0:512], func=AF.Copy),
                                reads=(pT,), writes=(kdst,))
                        st2 = cst[ccnt % 2]; ccnt += 1
                        kb.load([(st2.ap, vsrc[s, g * 512:(g + 1) * 512, :].rearrange("(b p) f -> p b f", p=128))], st2)
                        kb.op("dve", lambda e, st2=st2, s=s, g=g, vdst=vdst: e.tensor_copy(
                            out=vdst.ap[:, s, g * 4:(g + 1) * 4, :], in_=st2.ap), reads=(st2,), writes=(vdst,))

            zs_ps = [pbuf(bank(2 * i_, 2).rearrange("p (b n) -> p b n", n=512)) for i_ in range(3)]
            os_ps = [pbuf(bank(6))]
            dns_ps = [pbuf(bank(6)[:, 256:512])]
            dram_ots = Buf(None)
            sits = []
            for s in range(2):
                for n_, j in enumerate([16] + list(range(15, -1, -1))):
                    sits.append(dict(s=s, j=j, first=(n_ == 0), last=(n_ == 16)))
            NS = len(sits)

            def kt_blk(it, b, hh):
                if it["j"] == 16:
                    return ks_sb.ap[b * 64:(b + 1) * 64, hh, it["s"], :]
                return ktc.ap[b * 64:(b + 1) * 64, it["s"], hh, it["j"] * 128:(it["j"] + 1) * 128]

            def v_blk(it, h):
                if it["j"] == 16:
                    return vs_sb.ap[:, it["s"], h * 64:(h + 1) * 64]
                return vc.ap[:, it["s"], it["j"], h * 64:(h + 1) * 64]

            def ss_qk(k):
                it = sits[k]; z = zs_ps[k % 3]; s = it["s"]
                for hh in range(4):
                    for b in range(2):
                        kb.op("pe", lambda e, b=b, hh=hh: e.matmul(
                            z.ap[:, b, hh * 64:(hh + 1) * 64], kt_blk(it, b, hh),
                            qs_sb.ap[b * 64:(b + 1) * 64, hh, s * 64:(s + 1) * 64], start=(hh == 0), stop=(hh == 3),
                            skip_group_check=True),
                            reads=(ktc, ks_sb, qs_sb), writes=(z,), signal=(hh == 3 and b == 1))

            def ss_l(k):
                it = sits[k]; z = zs_ps[k % 3]; lb = sl_r[k % 3]
                kb.op("act", lambda e: e.activation(out=lb.ap, in_=z.ap[:, :, 0:256], func=AF.Softplus, scale=0.125),
                      reads=(z,), writes=(lb,))
                if it["j"] == 16:
                    for b in range(2):
                        for hh in range(4):
                            kb.op("dve", lambda e, b=b, hh=hh: e.tensor_tensor(
                                out=lb.ap[:, b, hh * 64:(hh + 1) * 64], in0=lb.ap[:, b, hh * 64:(hh + 1) * 64],
                                in1=mc.ap[:, 0:64], op=ALU.mult), reads=(lb, mc), writes=(lb,))

            def ss_ra(k):
                it = sits[k]
                if it["last"]:
                    return
                lb = sl_r[k % 3]; rn = sra_r[(k + 1) % 3]; rc = sra_r[k % 3]
                if it["first"]:
                    kb.op("dve", lambda e: e.tensor_copy(out=rn.ap, in_=lb.ap), reads=(lb,), writes=(rn,))
                else:
                    kb.op("dve", lambda e: e.tensor_tensor(out=rn.ap, in0=rc.ap, in1=lb.ap, op=ALU.add),
                          reads=(rc, lb), writes=(rn,))

            def ss_c(k):
                it = sits[k]; lb = sl_r[k % 3]; rc = sra_r[k % 3]; cp = zs_ps[k % 3]
                for b in range(2):
                    kb.op("pe", lambda e, b=b: e.matmul(cp.ap[:, b, 0:256], tri8.ap, lb.ap[:, b, :], start=False,
                                                        stop=it["first"], skip_group_check=True),
                          reads=(tri8, lb), writes=(cp,), signal=(it["first"] and b == 1))
                    if not it["first"]:
                        kb.op("pe", lambda e, b=b: e.matmul(cp.ap[:, b, 0:256], ones8.ap, rc.ap[:, b, :], start=False,
                                                            stop=True, skip_group_check=True),
                              reads=(ones8, rc), writes=(cp,), signal=(b == 1))

            def ss_w(k):
                it = sits[k]; cp = zs_ps[k % 3]; wb = sw_r[k % 3]
                kb.op("act", lambda e: e.activation(out=wb.ap, in_=cp.ap[:, :, 0:256], func=AF.Softplus, scale=0.125,
                                                    bias=-SHIFT), reads=(cp,), writes=(wb,))
                if it["j"] == 16:
                    for b in range(2):
                        for hh in range(4):
                            kb.op("dve", lambda e, b=b, hh=hh: e.tensor_tensor(
                                out=wb.ap[:, b, hh * 64:(hh + 1) * 64], in0=wb.ap[:, b, hh * 64:(hh + 1) * 64],
                                in1=mc.ap[:, 0:64], op=ALU.mult), reads=(wb, mc), writes=(wb,))

            def ss_pv(k):
                it = sits[k]; wb = sw_r[k % 3]; op_ = os_ps[0]; s = it["s"]
                for hh in range(4):
                    for b in range(2):
                        h = 2 * hh + b
                        kb.op("pe", lambda e, b=b, hh=hh, h=h: e.matmul(
                            op_.ap[b * 64:(b + 1) * 64, hh * 64:(hh + 1) * 64], v_blk(it, h),
                            wb.ap[:, b, hh * 64:(hh + 1) * 64], start=(it["first"] and hh == 0),
                            stop=(it["last"] and hh == 3), skip_group_check=True),
                            reads=(vc, vs_sb, wb), writes=(op_,), signal=(hh == 3 and b == 1))
                if it["last"]:
                    os_ = sos[s % 2]
                    kb.op("dve", lambda e: e.tensor_scalar(out=os_.ap.rearrange("p h q -> p (h q)"), in0=op_.ap[:, 0:256],
                                                           scalar1=ESHIFT, scalar2=None, op0=ALU.mult),
                          reads=(op_,), writes=(os_,))
                    kb.store([(ot[0:4, :, S + s * 64:S + (s + 1) * 64].rearrange("h p n -> p h n"), os_.ap)], os_,
                             writes=(dram_ots,))

            ss_qk(0)
            for r in range(NS + 3):
                if r + 1 < NS:
                    ss_qk(r + 1)
                if r < NS:
                    ss_l(r)
                if 0 <= r - 1 < NS:
                    ss_w(r - 1)
                if r < NS:
                    ss_ra(r)
                    ss_c(r)
                if 0 <= r - 2 < NS:
                    ss_pv(r - 2)
            scnt = 0
            for s in range(2):
                op_ = os_ps[0]; dn = dns_ps[0]
                order = [4, 3, 2, 1, 0]
                for n_, j in enumerate(order):
                    z = zs_ps[scnt % 2]; wb = sw_r[scnt % 3]; scnt += 1
                    for hh in range(4):
                        for b in range(2):
                            kt_ap = (ks_bd.ap[b * 64:(b + 1) * 64, hh, s, :] if j == 4
                                     else ktb.ap[b * 64:(b + 1) * 64, s, hh, j * 128:(j + 1) * 128])
                            kb.op("pe", lambda e, b=b, hh=hh, z=z, kt_ap=kt_ap: e.matmul(
                                z.ap[:, b, hh * 64:(hh + 1) * 64], kt_ap,
                                qs_bd.ap[b * 64:(b + 1) * 64, hh, s * 64:(s + 1) * 64], start=True, stop=True),
                                reads=(ktb, ks_bd, qs_bd), writes=(z,), signal=(hh == 3 and b == 1))
                    kb.op("act", lambda e, z=z, wb=wb: e.activation(out=wb.ap, in_=z.ap[:, :, 0:256], func=AF.Exp,
                                                                    scale=0.125), reads=(z,), writes=(wb,))
                    if j == 4:
                        kb.op("dve", lambda e, wb=wb: e.tensor_tensor(out=wb.ap, in0=wb.ap, in1=fsn.ap, op=ALU.mult),
                              reads=(wb, fsn), writes=(wb,))
                    elif j == 3:
                        kb.op("dve", lambda e, wb=wb: e.tensor_tensor(out=wb.ap, in0=wb.ap, in1=fs3.ap, op=ALU.mult),
                              reads=(wb, fs3), writes=(wb,))
                    fst_, lst_ = (n_ == 0), (n_ == 4)
                    for hh in range(4):
                        for b in range(2):
                            h = 2 * hh + b
                            v_ap = vs_bd.ap[:, s, h * 64:(h + 1) * 64] if j == 4 else vbc.ap[:, s, j, h * 64:(h + 1) * 64]
                            kb.op("pe", lambda e, b=b, hh=hh, wb=wb, v_ap=v_ap, fst_=fst_, lst_=lst_: e.matmul(
                                op_.ap[b * 64:(b + 1) * 64, hh * 64:(hh + 1) * 64], v_ap,
                                wb.ap[:, b, hh * 64:(hh + 1) * 64], start=(fst_ and hh == 0),
                                stop=(lst_ and hh == 3), skip_group_check=True),
                                reads=(vbc, vs_bd, wb), writes=(op_,), signal=False)
                    for b in range(2):
                        kb.op("pe", lambda e, b=b, wb=wb, fst_=fst_, lst_=lst_: e.matmul(
                            dn.ap[b * 64:(b + 1) * 64, :], ones1.ap, wb.ap[:, b, :], start=False, stop=lst_,
                            skip_group_check=True), reads=(ones1, wb), writes=(dn, op_), signal=(b == 1))
                rd = srd[s % 2]; os_ = sos[s % 2]
                kb.op("dve", lambda e, rd=rd: e.reciprocal(out=rd.ap, in_=dn.ap), reads=(dn, op_), writes=(rd,))
                kb.op("dve", lambda e, rd=rd, os_=os_: e.tensor_tensor(out=os_.ap.rearrange("p h q -> p (h q)"),
                                                                       in0=op_.ap[:, 0:256], in1=rd.ap, op=ALU.mult),
                      reads=(op_, rd), writes=(os_,))
                kb.store([(ot[4:8, :, S + s * 64:S + (s + 1) * 64].rearrange("h p n -> p h n"), os_.ap)], os_,
                         writes=(dram_ots,))
            kb.barrier()
            kb.release_dsems(cst + sos + [qs_sb, ks_sb, qs_bd, ks_bd, vs_sb, vs_bd])

            if stop_after == "SA":
                raise _Stop()
            kb.new_phase("O")
            kb.sb_off = const_end
            w_out_sb = sb(8 * D, BF16, (8, D), "w_out_sb")
            wst2 = ring(2, 4 * D, F32, (4, D), "wst2")
            for g in range(2):
                st = wst2[g % 2]
                kb.load([(st.ap, w_out[g * 512:(g + 1) * 512, :].rearrange("(c p) n -> p c n", p=128))], st)
                for c in range(4):
                    kb.op("dve", lambda e, st=st, c=c, g=g: e.tensor_scalar(
                        out=w_out_sb.ap[:, g * 4 + c, :], in0=st.ap[:, c, :], scalar1=gout_c.ap[:, g * 4 + c:g * 4 + c + 1],
                        scalar2=None, op0=ALU.mult), reads=(st, gout_c), writes=(w_out_sb,))
            oT_r = ring(2, 8 * 512, BF16, (8, 512), "oT")
            osq_r = ring(2, 8 * 512, BF16, (8, 512), "osq")
            xo_r = ring(3, D, F32, name="xo")
            h_r = ring(3, D, F32, name="h")
            rs_r = ring(4, 8, F32, name="rs")
            st_ps = [pbuf(bank(0)[:, 0:2]), pbuf(bank(1)[:, 0:2])]
            oo_ps = [(pbuf(bank(2, 2)), pbuf(bank(4, 2)))]
            dram_h = {}
            ocnt = 0
            for ti, (t0, n, is_s) in enumerate(tiles):
                oT = oT_r[ti % 2]; osq = osq_r[ti % 2]
                if is_s:
                    rd = [dram_ots]
                else:
                    rd = [dram_ot[(c, ti)] for c in range(8)]
                kb.load([(oT.ap[:, :, 0:n], ot[:, :, t0:t0 + n].rearrange("h p n -> p h n"))], oT, reads=rd)
                kb.op("act", lambda e, oT=oT, osq=osq: e.activation(out=osq.ap[:, :, 0:n], in_=oT.ap[:, :, 0:n],
                                                                    func=AF.Square), reads=(oT,), writes=(osq,))
                for blk in range(n // 128):
                    r0 = t0 + blk * 128
                    xi = xo_r[ocnt % 3]; hb = h_r[ocnt % 3]; rs = rs_r[ocnt % 4]; sp_ = st_ps[ocnt % 2]
                    pa, pb_ = oo_ps[0]
                    ocnt += 1
                    rows = x_s[blk * 128:(blk + 1) * 128, :] if is_s else x_p[r0:r0 + 128, :]
                    kb.load([(xi.ap, rows)], xi)
                    for g in range(2):
                        for c in range(4):
                            kb.op("pe", lambda e, g=g, c=c, sp_=sp_, osq=osq, blk=blk: e.matmul(
                                sp_.ap[:, g:g + 1], osq.ap[:, g * 4 + c, blk * 128:(blk + 1) * 128], ones1.ap[:, 0:1],
                                start=(c == 0), stop=(c == 3), skip_group_check=True),
                                reads=(osq, ones1), writes=(sp_,), signal=(g == 1 and c == 3))
                    kb.op("act", lambda e, rs=rs, sp_=sp_: e.activation(out=rs.ap[:, 0:2], in_=sp_.ap, func=AF.Ln,
                                                                        scale=1.0 / W, bias=EPS),
                          reads=(sp_,), writes=(rs,))
                    kb.op("act", lambda e, rs=rs: e.activation(out=rs.ap[:, 2:4], in_=rs.ap[:, 0:2], func=AF.Exp,
                                                               scale=-0.5), reads=(rs,), writes=(rs,))
                    for g, pg in ((0, pa), (1, pb_)):
                        for nh in range(2):
                            for c in range(4):
                                kb.op("pe", lambda e, g=g, nh=nh, c=c, pg=pg, oT=oT, blk=blk: e.matmul(
                                    pg.ap[:, nh * 512:(nh + 1) * 512], oT.ap[:, g * 4 + c, blk * 128:(blk + 1) * 128],
                                    w_out_sb.ap[:, g * 4 + c, nh * 512:(nh + 1) * 512], start=(c == 0), stop=(c == 3)),
                                    reads=(oT, w_out_sb), writes=(pg,), signal=(nh == 1 and c == 3))
                    kb.op("dve", lambda e, hb=hb, pa=pa, rs=rs, xi=xi: e.scalar_tensor_tensor(
                        out=hb.ap, in0=pa.ap, scalar=rs.ap[:, 2:3], in1=xi.ap, op0=ALU.mult, op1=ALU.add),
                        reads=(pa, rs, xi), writes=(hb,))
                    kb.op("dve", lambda e, hb=hb, pb_=pb_, rs=rs: e.scalar_tensor_tensor(
                        out=hb.ap, in0=pb_.ap, scalar=rs.ap[:, 3:4], in1=hb.ap, op0=ALU.mult, op1=ALU.add),
                        reads=(pb_, rs, hb), writes=(hb,))
                    d_ = Buf(None); dram_h[r0 // 128] = d_
                    kb.store([(hs[r0:r0 + 128, :], hb.ap)], hb, writes=(d_,))
            kb.barrier()
            kb.release_dsems(wst2 + oT_r + xo_r + h_r)

            if stop_after == "O":
                raise _Stop()
            kb.new_phase("F")
            kb.sb_off = const_end
            w_up_sb = sb(8 * DFF, BF16, (8, DFF), "w_up_sb")
            w_dn_sb = sb(32 * D, BF16, (32, D), "w_dn_sb")
            s4f = ring(4, 4, F32, name="s4f")
            s4g = ring(4, 4, F32, name="s4g")
            wst3_off = kb.sb_off
            wst3 = ring(2, DFF, F32, name="wst3")
            for c in range(8):
                st = wst3[c % 2]
                kb.load([(st.ap, w_up[c * 128:(c + 1) * 128, :])], st)
                kb.op("dve", lambda e, st=st, c=c: e.tensor_scalar(
                    out=w_up_sb.ap[:, c, :], in0=st.ap, scalar1=gffn_c.ap[:, c:c + 1], scalar2=None, op0=ALU.mult),
                    reads=(st, gffn_c), writes=(w_up_sb,))
            for g in range(8):
                st = wst3[g % 2]
                stv = st.ap.rearrange("p (c n) -> p c n", n=D)
                kb.load([(stv, w_down[g * 512:(g + 1) * 512, :].rearrange("(c p) n -> p c n", p=128))], st)
                kb.op("act" if g % 2 else "dve",
                      (lambda e, stv=stv, g=g: e.activation(out=w_dn_sb.ap[:, g * 4:(g + 1) * 4, :], in_=stv, func=AF.Copy))
                      if g % 2 else
                      (lambda e, stv=stv, g=g: e.tensor_copy(out=w_dn_sb.ap[:, g * 4:(g + 1) * 4, :], in_=stv)),
                      reads=(st,), writes=(w_dn_sb,))
            FT = 256
            kb.barrier()
            kb.release_dsems(wst3)
            kb.sb_off = wst3_off
            hin = ring(4, D, F32, name="hin")
            junk = ring(1, D, BF16, name="junk")
            hn_r = ring(2, D, BF16, name="hn")
            hnT = ring(2, 8 * FT, BF16, (8, FT), "hnT")
            aT = ring(1, 32 * FT, BF16, (32, FT), "aT")
            rl = ring(3, FT, BF16, name="rl")
            up_ps = [pbuf(bank(0)), pbuf(bank(1))]
            dn_ps2 = [pbuf(bank(2, 2)), pbuf(bank(4, 2))]
            ps_T = [pbuf(bank(6).bitcast(BF16)), pbuf(bank(7).bitcast(BF16))]
            ftiles = []
            for (t0, n, is_s) in tiles:
                for a in range(0, n, FT):
                    ftiles.append((t0 + a, min(FT, n - a), is_s, a))
            fstate = {"fcnt": 0, "ucnt": 0}
            fh = {}

            def f_norm(fi):
                (t0, n, is_s, a0) = ftiles[fi]
                hT = hnT[fi % 2]
                hbufs = []
                for blk in range(n // 128):
                    fcnt = fstate["fcnt"]
                    r0 = t0 + blk * 128
                    hi = hin[fcnt % 4]; ho = hn_r[fcnt % 2]; s4 = s4f[fcnt % 4]
                    hbufs.append((hi, r0))
                    kb.load([(hi.ap, hs[r0:r0 + 128, :])], hi, reads=[dram_h[r0 // 128]])
                    kb.op("act", lambda e: e.activation(out=ho.ap, in_=hi.ap, func=AF.Square, accum_out=s4.ap[:, 0:1]),
                          reads=(hi,), writes=(ho, s4))
                    kb.op("act", lambda e: e.activation(out=s4.ap[:, 1:2], in_=s4.ap[:, 0:1], func=AF.Ln,
                                                        scale=1.0 / D, bias=EPS), reads=(s4,), writes=(s4,))
                    kb.op("act", lambda e: e.activation(out=s4.ap[:, 2:3], in_=s4.ap[:, 1:2], func=AF.Exp,
                                                        scale=-0.5), reads=(s4,), writes=(s4,))
                    kb.op("dve", lambda e: e.tensor_scalar(out=ho.ap, in0=hi.ap, scalar1=s4.ap[:, 2:3],
                                                           scalar2=None, op0=ALU.mult),
                          reads=(hi, s4), writes=(ho,))
                    pT = ps_T[fcnt % 2]
                    for c in range(8):
                        kb.op("pe", lambda e, c=c: e.transpose(pT.ap[:, c * 128:(c + 1) * 128],
                                                               ho.ap[:, c * 128:(c + 1) * 128], ident.ap),
                              reads=(ho, ident), writes=(pT,), signal=(c == 7))
                    kb.op("dve", lambda e: e.tensor_copy(
                        out=hT.ap[:, :, blk * 128:(blk + 1) * 128], in_=pT.ap.rearrange("p (c n) -> p c n", n=128)),
                        reads=(pT,), writes=(hT,))
                    fstate["fcnt"] += 1
                fh[fi] = hbufs

            def f_up(fi):
                (t0, n, is_s, a0) = ftiles[fi]
                hT = hnT[fi % 2]; at = aT[0]
                for fc in range(32):
                    ucnt = fstate["ucnt"]
                    pu = up_ps[ucnt % 2]; rb = rl[ucnt % 3]; fstate["ucnt"] += 1
                    for kc in range(8):
                        kb.op("pe", lambda e, kc=kc: e.matmul(
                            pu.ap[:, 0:n], w_up_sb.ap[:, kc, fc * 128:(fc + 1) * 128], hT.ap[:, kc, 0:n],
                            start=(kc == 0), stop=(kc == 7)), reads=(w_up_sb, hT), writes=(pu,), signal=(kc == 7))
                    kb.op("act", lambda e: e.activation(out=rb.ap[:, 0:n], in_=pu.ap[:, 0:n], func=AF.Relu),
                          reads=(pu,), writes=(rb,))
                    kb.op("dve", lambda e: e.tensor_tensor(out=at.ap[:, fc, 0:n], in0=rb.ap[:, 0:n],
                                                           in1=rb.ap[:, 0:n], op=ALU.mult),
                          reads=(rb,), writes=(at,))

            def f_down(fi):
                (t0, n, is_s, a0) = ftiles[fi]
                at = aT[0]
                for blk in range(n // 128):
                    hi, r0 = fh[fi][blk]
                    pd = dn_ps2[blk % 2]
                    for nh in range(2):
                        for fc in range(32):
                            kb.op("pe", lambda e, nh=nh, fc=fc: e.matmul(
                                pd.ap[:, nh * 512:(nh + 1) * 512], at.ap[:, fc, blk * 128:(blk + 1) * 128],
                                w_dn_sb.ap[:, fc, nh * 512:(nh + 1) * 512], start=(fc == 0), stop=(fc == 31)),
                                reads=(at, w_dn_sb), writes=(pd,), signal=(nh == 1 and fc == 31))
                    s4 = s4g[(fi * 2 + blk) % 4]; jk = junk[0]
                    kb.op("dve", lambda e: e.tensor_tensor(out=hi.ap, in0=pd.ap, in1=hi.ap, op=ALU.add),
                          reads=(pd, hi), writes=(hi,))
                    kb.op("act", lambda e: e.activation(out=jk.ap, in_=hi.ap, func=AF.Square, accum_out=s4.ap[:, 0:1]),
                          reads=(hi,), writes=(jk, s4))
                    kb.op("act", lambda e: e.activation(out=s4.ap[:, 1:2], in_=s4.ap[:, 0:1], func=AF.Ln,
                                                        scale=1.0 / D, bias=EPS), reads=(s4,), writes=(s4,))
                    kb.op("act", lambda e: e.activation(out=s4.ap[:, 2:3], in_=s4.ap[:, 1:2], func=AF.Exp,
                                                        scale=-0.5), reads=(s4,), writes=(s4,))
                    kb.op("dve", lambda e: e.scalar_tensor_tensor(
                        out=hi.ap, in0=hi.ap, scalar=s4.ap[:, 2:3], in1=gfin_t.ap, op0=ALU.mult, op1=ALU.mult),
                        reads=(hi, s4, gfin_t), writes=(hi,))
                    dst = y_s[r0 - S:r0 - S + 128, :] if is_s else y_p[r0:r0 + 128, :]
                    kb.store([(dst, hi.ap)], hi)

            f_norm(0)
            for fi in range(len(ftiles)):
                f_up(fi)
                if fi + 1 < len(ftiles):
                    f_norm(fi + 1)
                f_down(fi)
            kb.barrier()

        try:
            plan()
        except _Stop:
            kb.barrier()
        with nc.Block() as block:
            @block.sync
            def _(e):
                for f in kb.q["sp"]:
                    f(e)

            @block.tensor
            def _(e):
                for f in kb.q["pe"]:
                    f(e)

            @block.scalar
            def _(e):
                for f in kb.q["act"]:
                    f(e)

            @block.vector
            def _(e):
                for f in kb.q["dve"]:
                    f(e)

            @block.gpsimd
            def _(e):
                for f in kb.q["pool"]:
                    f(e)
    return nc


_CACHE = {}


def _run(S, per_core_inputs, trace=False):
    if S not in _CACHE:
        _CACHE[S] = build_program(S)
    nc = _CACHE[S]
    return run_bass_kernel_spmd(nc, per_core_inputs, core_ids=list(range(len(per_core_inputs))), trace=trace)


def make_core_inputs(c, S, x_prompt, x_sample, cache_sb_k, cache_sb_v, cache_band_k, cache_band_v,
                     norm_mix_g, w_in, rel_bias, norm_sb_g, norm_band_g, w_out,
                     norm_ffn_g, w_up, w_down, norm_final_g):
    f = lambda a: np.ascontiguousarray(np.asarray(a, dtype=np.float32))
    return {
        "x_p": f(x_prompt[c, :S]),
        "x_s": f(x_sample[2 * c:2 * c + 2]).reshape(TS, D),
        "csk": f(cache_sb_k[0, 2 * c:2 * c + 2]).reshape(2, PAST, W),
        "csv": f(cache_sb_v[0, 2 * c:2 * c + 2]).reshape(2, PAST, W),
        "cbk": f(cache_band_k[0, 2 * c:2 * c + 2]).reshape(2, BROWS, W),
        "cbv": f(cache_band_v[0, 2 * c:2 * c + 2]).reshape(2, BROWS, W),
        "w_in": f(w_in[0]), "w_out": f(w_out[0]), "w_up": f(w_up[0]), "w_down": f(w_down[0]),
        "g_mix": f(norm_mix_g[0]), "g_ffn": f(norm_ffn_g[0]), "g_sb": f(norm_sb_g[0]), "g_bd": f(norm_band_g[0]),
        "g_fin": f(norm_final_g), "relb": f(rel_bias[0]),
    }


def assemble(results, S, nb):
    KEEP = min(512, S)
    g = lambda k: np.stack([np.asarray(r[k], dtype=np.float32) for r in results])
    y_p = g("y_p")
    y_s = g("y_s").reshape(2 * nb, 64, D)
    sbk_p = g("sbk_p").reshape(1, nb, S, 8, 64)
    sbv_p = g("sbv_p").reshape(1, nb, S, 8, 64)
    bdk_p = g("bdk_p").reshape(1, nb, KEEP, 8, 64)
    bdv_p = g("bdv_p").reshape(1, nb, KEEP, 8, 64)
    sbk_s = g("sbk_s").reshape(1, 2 * nb, 64, 8, 64)
    sbv_s = g("sbv_s").reshape(1, 2 * nb, 64, 8, 64)
    bdk_s = g("bdk_s").reshape(1, 2 * nb, 64, 8, 64)
    bdv_s = g("bdv_s").reshape(1, 2 * nb, 64, 8, 64)
    return (y_p, y_s, sbk_p, sbv_p, bdk_p, bdv_p, sbk_s, sbv_s, bdk_s, bdv_s)


def kernel(x_prompt, x_sample, cache_sb_k, cache_sb_v, cache_band_k, cache_band_v,
           norm_mix_g, w_in, rel_bias, norm_sb_g, norm_band_g, w_out,
           norm_ffn_g, w_up, w_down, norm_final_g):
    x_prompt = np.asarray(x_prompt)
    nb, S = x_prompt.shape[0], x_prompt.shape[1]
    args = (x_prompt, np.asarray(x_sample), np.asarray(cache_sb_k), np.asarray(cache_sb_v),
            np.asarray(cache_band_k), np.asarray(cache_band_v), np.asarray(norm_mix_g), np.asarray(w_in),
            np.asarray(rel_bias), np.asarray(norm_sb_g), np.asarray(norm_band_g), np.asarray(w_out),
            np.asarray(norm_ffn_g), np.asarray(w_up), np.asarray(w_down), np.asarray(norm_final_g))
    in_maps = [make_core_inputs(c, S, *args) for c in range(nb)]
    res = _run(S, in_maps)
    return assemble(res.results, S, nb)
```

```python
import contextlib
import numpy as np
import concourse.bass as bass
import concourse.mybir as mybir
from concourse.bass_utils import run_bass_kernel_spmd

F32 = mybir.dt.float32
BF16 = mybir.dt.bfloat16
U8 = mybir.dt.uint8
AF = mybir.ActivationFunctionType
ALU = mybir.AluOpType

D = 1024
W = 512
DFF = 4096
EPS = 1e-6
PAST = 2048
BROWS = 512
TS = 128
LX = 384
SB_TOTAL = 206 * 1024
SHIFT = 20.0
ESHIFT = float(np.exp(20.0))


class Buf:
    __slots__ = ("ap", "w", "r", "lsem", "ssem", "name")

    def __init__(self, ap, name=""):
        self.ap = ap
        self.w = {}
        self.r = {}
        self.lsem = None
        self.ssem = None
        self.name = name


class DSem:
    __slots__ = ("sem", "cnt", "kind")

    def __init__(self, sem):
        self.sem = sem
        self.cnt = 0
        self.kind = "pool"


ENGS = ("pe", "act", "dve", "pool", "sp")


class _Rec:
    def __init__(self):
        self.calls = []

    def __getattr__(self, name):
        def f(*a, **k):
            self.calls.append((name, a, k))
            return self
        return f


class KB:
    def __init__(self, nc, stack):
        self.nc = nc
        self.stack = stack
        self.q = {e: [] for e in ENGS}
        self.esem = {}
        self.ecnt = {}
        self.last = {}
        self.waited = {}
        self.pend = {e: [] for e in ENGS}
        self.nsem = 0
        self.dsems = []
        self.free_dsems = {}
        self.sb_off = 0
        self.sb_mark = 0

    def new_sem(self, name):
        self.nsem += 1
        return self.stack.enter_context(self.nc.semaphore(f"{name}_{self.nsem}"))

    def new_phase(self, name):
        for e in ("pe", "act", "dve", "pool"):
            self.esem[e] = self.new_sem(f"{name}_{e}")
            self.ecnt[e] = 0

    def get_dsem(self, kind="pool"):
        fl = self.free_dsems.setdefault(kind, [])
        if fl:
            return fl.pop()
        d = DSem(self.new_sem("dma" + kind))
        d.kind = kind
        self.dsems.append(d)
        return d

    def _wait(self, eng, tok):
        sem, val, peng = tok
        if peng == "pe" and eng == "pe":
            return
        key = (eng, id(sem))
        if self.waited.get(key, 0) >= val:
            return
        self.waited[key] = val
        self.q[eng].append(lambda e, sem=sem, val=val: e.wait_ge(sem, val))

    @staticmethod
    def _merge(d, tok):
        k = id(tok[0])
        if k not in d or d[k][1] < tok[1]:
            d[k] = tok

    def _deps(self, eng, reads, writes):
        for b in reads:
            for t in b.w.values():
                self._wait(eng, t)
        for b in writes:
            for t in b.w.values():
                self._wait(eng, t)
            for t in b.r.values():
                self._wait(eng, t)

    def _commit(self, tok, reads, writes):
        for b in reads:
            self._merge(b.r, tok)
        for b in writes:
            b.w = {id(tok[0]): tok}
            b.r = {}

    def op(self, eng, fn, reads=(), writes=(), signal=True):
        rec = _Rec()
        fn(rec)
        assert len(rec.calls) == 1
        mname, margs, mkw = rec.calls[0]
        fn = lambda e, mname=mname, margs=margs, mkw=mkw: getattr(e, mname)(*margs, **mkw)
        self._deps(eng, reads, writes)
        if not signal:
            self.q[eng].append(lambda e, fn=fn: fn(e))
            self.pend[eng].append((tuple(reads), tuple(writes)))
            return None
        sem = self.esem[eng]
        self.ecnt[eng] += 1
        tok = (sem, self.ecnt[eng], eng)
        self.q[eng].append(lambda e, fn=fn, sem=sem: fn(e).then_inc(sem, 1))
        for (rs, ws) in self.pend[eng]:
            self._commit(tok, rs, ws)
        self.pend[eng] = []
        self._commit(tok, reads, writes)
        self.last[eng] = tok
        return tok

    def dma(self, qeng, pairs, dsem, reads=(), writes=(), slow=False):
        self._deps(qeng, reads, writes)
        for (o, i) in pairs:
            if slow:
                self.q[qeng].append(
                    lambda e, o=o, i=i, s=dsem.sem: e.dma_start(
                        out=o, in_=i, allow_slow_non_contiguous=True).then_inc(s, 16))
            else:
                self.q[qeng].append(
                    lambda e, o=o, i=i, s=dsem.sem: e.dma_start(out=o, in_=i).then_inc(s, 16))
        dsem.cnt += 16 * len(pairs)
        tok = (dsem.sem, dsem.cnt, "dma")
        self._commit(tok, reads, writes)
        return tok

    def load(self, pairs, dst, reads=()):
        if dst.lsem is None:
            dst.lsem = self.get_dsem("sp")
        return self.dma("sp", pairs, dst.lsem, reads=reads, writes=(dst,))

    def store(self, pairs, src, writes=(), qeng="pool"):
        if src.ssem is None:
            src.ssem = self.get_dsem(qeng)
        return self.dma(qeng, pairs, src.ssem, reads=(src,), writes=writes)

    def barrier(self):
        toks = [self.last[e] for e in ("pe", "act", "dve", "pool") if e in self.last]
        toks += [(d.sem, d.cnt, "dma") for d in self.dsems if d.cnt > 0]
        for e in ENGS:
            assert not self.pend[e], e
            for t in toks:
                if t[2] == e:
                    continue
                sem, val, _ = t
                key = (e, id(sem))
                if self.waited.get(key, 0) >= val:
                    continue
                self.waited[key] = val
                self.q[e].append(lambda en, sem=sem, val=val: en.wait_ge(sem, val))

    def release_dsems(self, bufs):
        for b in bufs:
            for a in ("lsem", "ssem"):
                d = getattr(b, a)
                if d is not None:
                    self.free_dsems.setdefault(d.kind, []).append(d)
                    setattr(b, a, None)


def build_program(S, stop_after=None):
    assert S % 512 == 0
    NT = S + TS
    KEEP = min(512, S)
    NQT = S // 512
    NKB = S // 128

    nc = bass.Bass("TRN2", target_bir_lowering=False)

    def din(name, shape, dt=F32):
        return nc.dram_tensor(name, list(shape), dt, kind="ExternalInput").ap()

    def dout(name, shape, dt=F32):
        return nc.dram_tensor(name, list(shape), dt, kind="ExternalOutput").ap()

    def dscr(name, shape, dt):
        return nc.dram_tensor(name, list(shape), dt, kind="Internal").ap()

    x_p = din("x_p", [S, D])
    x_s = din("x_s", [TS, D])
    csk = din("csk", [2, PAST, W])
    csv = din("csv", [2, PAST, W])
    cbk = din("cbk", [2, BROWS, W])
    cbv = din("cbv", [2, BROWS, W])
    w_in = din("w_in", [D, 3 * D])
    w_out = din("w_out", [D, D])
    w_up = din("w_up", [D, DFF])
    w_down = din("w_down", [DFF, D])
    g_mix = din("g_mix", [D])
    g_ffn = din("g_ffn", [D])
    g_sb = din("g_sb", [W])
    g_bd = din("g_bd", [W])
    g_fin = din("g_fin", [D])
    relb = din("relb", [8, 257])

    y_p = dout("y_p", [S, D])
    y_s = dout("y_s", [TS, D])
    sbk_p = dout("sbk_p", [S, W])
    sbv_p = dout("sbv_p", [S, W])
    bdk_p = dout("bdk_p", [KEEP, W])
    bdv_p = dout("bdv_p", [KEEP, W])
    sbk_s = dout("sbk_s", [TS, W])
    sbv_s = dout("sbv_s", [TS, W])
    bdk_s = dout("bdk_s", [TS, W])
    bdv_s = dout("bdv_s", [TS, W])

    qt_sb = dscr("qt_sb", [4, 128, NT], BF16)
    kt_sb = dscr("kt_sb", [4, 128, NT], BF16)
    qt_bd = dscr("qt_bd", [4, 128, NT], BF16)
    kt_bd = dscr("kt_bd", [4, 128, NT], BF16)
    v_sb = dscr("v_sb", [NT, W], BF16)
    v_bd = dscr("v_bd", [NT, W], BF16)
    ot = dscr("ot", [8, 128, NT], BF16)
    hs = dscr("hs", [NT, D], F32)
    e_all = dscr("e_all", [8, LX], F32)
    xrep = dscr("xrep", [8, 129 * LX], F32)

    stack = contextlib.ExitStack()
    with stack:
        big = stack.enter_context(nc.sbuf_tensor("big", [128, SB_TOTAL], U8))
        psum = stack.enter_context(nc.psum_tensor("psum", [128, 4096], F32))
        kb = KB(nc, stack)

        def sb(nelem, dt, shape=None, name=""):
            size = 4 if dt == F32 else 2
            off = (kb.sb_off + 63) // 64 * 64
            nbytes = nelem * size
            assert off + nbytes <= SB_TOTAL, (name, off, nbytes)
            kb.sb_off = off + nbytes
            ap = big[:, off:off + nbytes].bitcast(dt)
            if shape is not None:
                if len(shape) == 2:
                    ap = ap.rearrange("p (a b) -> p a b", b=shape[1])
                elif len(shape) == 3:
                    ap = ap.rearrange("p (a b c) -> p a b c", b=shape[1], c=shape[2])
            return Buf(ap, name)

        def ring(n, nelem, dt, shape=None, name=""):
            return [sb(nelem, dt, shape, f"{name}{i}") for i in range(n)]

        def bank(b, nb=1):
            return psum[:, b * 512:(b + nb) * 512]

        def pbuf(ap, name=""):
            return Buf(ap, name)

        ident = sb(128, BF16, name="ident")
        tri8 = sb(128, BF16, name="tri8")
        ones8 = sb(128, BF16, name="ones8")
        ones1 = sb(64, BF16, name="ones1")
        mc = sb(128, F32, name="mc")
        mfar = sb(128, F32, name="mfar")
        wn_b = sb(8 * 256, BF16, (8, 256), "wn_b")
        en_b = sb(8 * 128, BF16, (8, 128), "en_b")
        mfar_b = sb(128, BF16, name="mfar_b")
        gfin_t = sb(D, F32, name="gfin")
        gmix_c = sb(8, F32, name="gmixc")
        gffn_c = sb(8, F32, name="gffnc")
        gout_c = sb(8, F32, name="goutc")
        negc = sb(8, F32, name="negc")
        fs3 = sb(512, F32, (2, 256), "fs3")
        fsn = sb(512, F32, (2, 256), "fsn")
        const_end = kb.sb_off
        kb.sb_off = SB_TOTAL - 18 * 1024
        mnear = sb(256, F32, name="mnear")
        wn = sb(8 * 256, F32, (8, 256), "wn")
        en = sb(8 * 256, F32, (8, 256), "en")
        w_tmp_lo = SB_TOTAL - 18 * 1024
        kb.sb_off = const_end

        dram_e = Buf(None, "e_all")
        dram_x = Buf(None, "xrep")
        setup_sem = kb.get_dsem()

        class _Stop(Exception):
            pass

        def plan():
            kb.new_phase("W")
            kb.op("pool", lambda e: e.memset(ident.ap, 0.0), writes=(ident,))
            kb.op("pool", lambda e: e.affine_select(out=ident.ap, in_=ident.ap, pattern=[[-1, 128]],
                                                    compare_op=ALU.not_equal, fill=1.0, base=0,
                                                    channel_multiplier=1), writes=(ident,))
            kb.op("pool", lambda e: e.memset(tri8.ap, -8.0), writes=(tri8,))
            kb.op("pool", lambda e: e.affine_select(out=tri8.ap, in_=tri8.ap, pattern=[[-1, 128]],
                                                    compare_op=ALU.is_ge, fill=0.0, base=0,
                                                    channel_multiplier=1), writes=(tri8,))
            kb.op("pool", lambda e: e.memset(ones8.ap, -8.0), writes=(ones8,))
            kb.op("pool", lambda e: e.memset(ones1.ap, 1.0), writes=(ones1,))
            kb.op("pool", lambda e: e.memset(mc.ap, 1.0), writes=(mc,))
            kb.op("pool", lambda e: e.affine_select(out=mc.ap, in_=mc.ap, pattern=[[1, 128]],
                                                    compare_op=ALU.is_gt, fill=0.0, base=0,
                                                    channel_multiplier=-1), writes=(mc,))
            kb.op("pool", lambda e: e.memset(mfar.ap, 1.0), writes=(mfar,))
            kb.op("pool", lambda e: e.memset(mfar.ap[0:64, 64:128], 0.0), writes=(mfar,))
            kb.op("pool", lambda e: e.memset(mnear.ap, 1.0), writes=(mnear,))
            kb.op("pool", lambda e: e.memset(mnear.ap[64:128, 0:64], 0.0), writes=(mnear,))

            setup_sems = []

            def bc_load(dst, src_ap):
                ds = kb.get_dsem(); setup_sems.append(ds)
                kb.q["pool"].append(lambda e, o=dst.ap, i=src_ap, s=ds.sem:
                                    e.dma_start(out=o, in_=i, allow_slow_non_contiguous=True).then_inc(s, 16))
                ds.cnt += 16
                tok = (ds.sem, ds.cnt, "dma")
                dst.w = {id(tok[0]): tok}

            bc_load(gfin_t, g_fin.rearrange("(o n) -> o n", o=1).broadcast_to([128, D]))
            def col3(b):
                return Buf(b.ap.rearrange("p (c o) -> p c o", o=1))
            gm3 = col3(gmix_c); gf3 = col3(gffn_c)
            bc_load(gm3, g_mix.rearrange("(c p o) -> p c o", p=128, o=1))
            bc_load(gf3, g_ffn.rearrange("(c p o) -> p c o", p=128, o=1))
            gmix_c.w = dict(gm3.w); gffn_c.w = dict(gf3.w)
            gout_c_a = Buf(gout_c.ap[:, 0:4].rearrange("p (c o) -> p c o", o=1))
            gout_c_b = Buf(gout_c.ap[:, 4:8].rearrange("p (c o) -> p c o", o=1))
            bc_load(gout_c_a, g_sb.rearrange("(c p o) -> p c o", p=128, o=1))
            bc_load(gout_c_b, g_bd.rearrange("(c p o) -> p c o", p=128, o=1))
            gout_c.w = dict(gout_c_a.w); gout_c.w.update(gout_c_b.w)
            ng3 = col3(negc)
            bc_load(ng3, relb[:, 256:257].rearrange("(x h) o -> x h o", x=1).broadcast_to([128, 8, 1]))
            negc.w = dict(ng3.w)
            kb.op("dve", lambda e: e.tensor_scalar(out=negc.ap, in0=negc.ap, scalar1=-1.0, scalar2=None,
                                                   op0=ALU.mult), reads=(negc,), writes=(negc,))
            kb.dma("pool", [(e_all[:, 0:129], relb[:, 128:257]),
                          (e_all[:, 129:257].rearrange("h (n o) -> h n o", o=1),
                           relb[:, 256:257].rearrange("h (n o) -> h n o", o=1).broadcast_to([8, 128, 1])),
                          (e_all[:, 257:384], relb[:, 1:128])], setup_sem, writes=(dram_e,), slow=True)
            setup_sem2 = kb.get_dsem(); setup_sem3 = kb.get_dsem()
            kb.dma("pool", [(xrep.rearrange("h (r l) -> h r l", l=LX),
                           e_all.rearrange("h (o l) -> h o l", o=1).broadcast_to([8, 129, LX]))],
                   setup_sem2, reads=(dram_e,), writes=(dram_x,), slow=True)
            en_src = bass.AP(xrep.tensor, 0, [[LX - 1, 128], [129 * LX, 8], [1, 256]])
            kb.dma("pool", [(en.ap, en_src)], setup_sem3, reads=(dram_x,), writes=(en,), slow=True)
            for h in range(8):
                kb.op("act", lambda e, h=h: e.activation(out=en.ap[:, h, :], in_=en.ap[:, h, :], func=AF.Exp,
                                                         bias=negc.ap[:, h:h + 1], scale=1.0),
                      reads=(en, negc), writes=(en,))
            for h in range(8):
                kb.op("dve", lambda e, h=h: e.tensor_tensor(out=wn.ap[:, h, :], in0=en.ap[:, h, :],
                                                            in1=mnear.ap, op=ALU.mult),
                      reads=(en, mnear), writes=(wn,))
            kb.op("dve", lambda e: e.tensor_copy(out=wn_b.ap, in_=wn.ap), reads=(wn,), writes=(wn_b,))
            kb.op("dve", lambda e: e.tensor_copy(out=en_b.ap, in_=en.ap[:, :, 128:256]), reads=(en,), writes=(en_b,))
            kb.op("dve", lambda e: e.tensor_copy(out=mfar_b.ap, in_=mfar.ap), reads=(mfar,), writes=(mfar_b,))
            kb.op("pool", lambda e: e.memset(fsn.ap, 0.0), writes=(fsn,))
            for h in range(8):
                b_, hh = h % 2, h // 2
                kb.op("dve", lambda e, h=h, b_=b_, hh=hh: e.tensor_copy(
                    out=fs3.ap[:, b_, hh * 64:(hh + 1) * 64], in_=en.ap[:, h, 128:192]),
                    reads=(en,), writes=(fs3,))
                kb.op("dve", lambda e, h=h, b_=b_, hh=hh: e.tensor_copy(
                    out=fsn.ap[0:64, b_, hh * 64:(hh + 1) * 64], in_=en.ap[0:64, h, 0:64]),
                    reads=(en,), writes=(fsn,))

            if stop_after == "W":
                raise _Stop()
            kb.sb_off = const_end
            w_in_sb = sb(8 * 3072, BF16, (8, 3072), "w_in_sb")
            wst = ring(2, 3072, F32, name="wst")
            for c in range(8):
                st = wst[c % 2]
                kb.load([(st.ap, w_in[c * 128:(c + 1) * 128, :])], st)
                kb.op("dve", lambda e, c=c, st=st: e.tensor_scalar(
                    out=w_in_sb.ap[:, c, :], in0=st.ap, scalar1=gmix_c.ap[:, c:c + 1], scalar2=None,
                    op0=ALU.mult), reads=(st, gmix_c), writes=(w_in_sb,))
            p_mark = kb.sb_off
            xin = ring(4, D, F32, name="xin")
            xn = ring(2, D, BF16, name="xn")
            ssb = ring(4, 4, F32, name="ss")
            xnT = ring(2, 8 * 512, BF16, (8, 512), "xnT")
            fmst = ring(2, 16 * 512, BF16, (16, 512), "fmst")
            tmst = ring(4, 512, F32, name="tmst")
            vst = ring(4, 512, BF16, name="vst")
            assert kb.sb_off <= w_tmp_lo, kb.sb_off
            ps_fm = [pbuf(bank(0)), pbuf(bank(1)), pbuf(bank(2))]
            ps_tm = [pbuf(bank(3)), pbuf(bank(4)), pbuf(bank(5))]
            ps_T = [pbuf(bank(6).bitcast(BF16)), pbuf(bank(7).bitcast(BF16))]
            dram_q = {}

            def dbuf(key):
                if key not in dram_q:
                    dram_q[key] = Buf(None, str(key))
                return dram_q[key]

            tiles = [(i * 512, 512, False) for i in range(NQT)] + [(S, TS, True)]
            cnt = {"blk": 0, "fm": 0, "tm": 0, "tile": 0, "ev": 0, "pt": 0}

            def rms_block(src_rows, xi, xo, s4):
                kb.load([(xi.ap, src_rows)], xi)
                kb.op("act", lambda e: e.activation(out=xo.ap, in_=xi.ap, func=AF.Square,
                                                    accum_out=s4.ap[:, 0:1]),
                      reads=(xi,), writes=(xo, s4))
                kb.op("act", lambda e: e.activation(out=s4.ap[:, 1:2], in_=s4.ap[:, 0:1], func=AF.Ln,
                                                    scale=1.0 / D, bias=EPS), reads=(s4,), writes=(s4,))
                kb.op("act", lambda e: e.activation(out=s4.ap[:, 2:3], in_=s4.ap[:, 1:2], func=AF.Exp,
                                                    scale=-0.5), reads=(s4,), writes=(s4,))
                kb.op("dve", lambda e: e.tensor_scalar(out=xo.ap, in0=xi.ap, scalar1=s4.ap[:, 2:3],
                                                       scalar2=None, op0=ALU.mult),
                      reads=(xi, s4), writes=(xo,))

            def transpose_block(xo, dstT, blk, evac_eng):
                pT = ps_T[cnt["pt"] % 2]; cnt["pt"] += 1
                for c in range(8):
                    kb.op("pe", lambda e, c=c, pT=pT: e.transpose(pT.ap[:, c * 128:(c + 1) * 128],
                                                                 xo.ap[:, c * 128:(c + 1) * 128], ident.ap),
                          reads=(xo, ident), writes=(pT,), signal=(c == 7))
                src = pT.ap.rearrange("p (c n) -> p c n", n=128)
                dst = dstT.ap[:, :, blk * 128:(blk + 1) * 128]
                if evac_eng == "act":
                    kb.op("act", lambda e: e.activation(out=dst, in_=src, func=AF.Copy),
                          reads=(pT,), writes=(dstT,))
                else:
                    kb.op("dve", lambda e: e.tensor_copy(out=dst, in_=src), reads=(pT,), writes=(dstT,))

            def p_norm(ti):
                (t0, n, is_s) = tiles[ti]
                xT = xnT[ti % 2]
                for blk in range(n // 128):
                    bi = cnt["blk"]
                    xi = xin[bi % 4]; xo = xn[bi % 2]; s4 = ssb[bi % 4]
                    rows = x_s[blk * 128:(blk + 1) * 128, :] if is_s else x_p[t0 + blk * 128:t0 + (blk + 1) * 128, :]
                    rms_block(rows, xi, xo, s4)
                    transpose_block(xo, xT, blk, "dve")
                    cnt["blk"] += 1

            def p_mm(ti):
                (t0, n, is_s) = tiles[ti]
                nb = n // 128
                xT = xnT[ti % 2]
                fst = fmst[ti % 2]
                fm_cols = [0 * 512, 1 * 512, 3 * 512, 4 * 512]
                for g4 in (0, 2, 3):
                    for c4 in range(4):
                        oc = g4 * 4 + c4
                        col0 = fm_cols[g4] + c4 * 128
                        pf = ps_fm[cnt["fm"] % 3]; cnt["fm"] += 1
                        for kc in range(8):
                            kb.op("pe", lambda e, kc=kc, pf=pf, col0=col0: e.matmul(
                                pf.ap[:, 0:n], w_in_sb.ap[:, kc, col0:col0 + 128], xT.ap[:, kc, 0:n],
                                start=(kc == 0), stop=(kc == 7)),
                                reads=(w_in_sb, xT), writes=(pf,), signal=(kc == 7))
                        if cnt["ev"] % 3 != 2:
                            kb.op("act", lambda e, pf=pf, oc=oc: e.activation(out=fst.ap[:, oc, 0:n], in_=pf.ap[:, 0:n],
                                                                              func=AF.Copy),
                                  reads=(pf,), writes=(fst,))
                        else:
                            kb.op("dve", lambda e, pf=pf, oc=oc: e.tensor_copy(out=fst.ap[:, oc, 0:n], in_=pf.ap[:, 0:n]),
                                  reads=(pf,), writes=(fst,))
                        cnt["ev"] += 1
                need_bd = is_s or (t0 + n > S - KEEP)
                for blk in range(nb):
                    r0 = t0 + blk * 128
                    groups = [("k_sb", 512), ("v_sb", 1024), ("v_bd", 2560)]
                    if need_bd:
                        groups.append(("k_bd", 2048))
                    for (gname, gcol) in groups:
                        pt = ps_tm[cnt["tm"] % 3]
                        for kc in range(8):
                            kb.op("pe", lambda e, kc=kc, pt=pt, gcol=gcol, blk=blk: e.matmul(
                                pt.ap, xT.ap[:, kc, blk * 128:(blk + 1) * 128], w_in_sb.ap[:, kc, gcol:gcol + 512],
                                start=(kc == 0), stop=(kc == 7)),
                                reads=(w_in_sb, xT), writes=(pt,), signal=(kc == 7))
                        ts_ = tmst[cnt["tm"] % 4]
                        vs_ = vst[cnt["tm"] % 4]
                        cnt["tm"] += 1
                        kb.op("dve", lambda e, pt=pt, ts_=ts_: e.tensor_copy(out=ts_.ap, in_=pt.ap),
                              reads=(pt,), writes=(ts_,))
                        outs = []
                        if gname == "k_sb":
                            outs.append(sbk_s[blk * 128:(blk + 1) * 128, :] if is_s else sbk_p[r0:r0 + 128, :])
                        elif gname == "v_sb":
                            outs.append(sbv_s[blk * 128:(blk + 1) * 128, :] if is_s else sbv_p[r0:r0 + 128, :])
                        elif gname == "k_bd":
                            outs.append(bdk_s[blk * 128:(blk + 1) * 128, :] if is_s
                                        else bdk_p[r0 - (S - KEEP):r0 - (S - KEEP) + 128, :])
                        elif gname == "v_bd" and need_bd:
                            outs.append(bdv_s[blk * 128:(blk + 1) * 128, :] if is_s
                                        else bdv_p[r0 - (S - KEEP):r0 - (S - KEEP) + 128, :])
                        if outs:
                            kb.store([(o, ts_.ap) for o in outs], ts_)
                        if gname == "k_sb":
                            kbf = vs_
                            kb.op("act", lambda e, ts_=ts_, kbf=kbf: e.activation(out=kbf.ap, in_=ts_.ap, func=AF.Copy),
                                  reads=(ts_,), writes=(kbf,))
                            kdefer = kbf
                        if gname in ("v_sb", "v_bd"):
                            kb.op("act", lambda e, ts_=ts_, vs_=vs_: e.activation(out=vs_.ap, in_=ts_.ap, func=AF.Copy),
                                  reads=(ts_,), writes=(vs_,))
                            dstv = v_sb if gname == "v_sb" else v_bd
                            kb.store([(dstv[r0:r0 + 128, :], vs_.ap)], vs_, writes=(dbuf((gname, r0 // 128)),))
                    pT = ps_T[cnt["pt"] % 2]; cnt["pt"] += 1
                    for c4 in range(4):
                        kb.op("pe", lambda e, c4=c4, pT=pT, kdefer=kdefer: e.transpose(
                            pT.ap[:, c4 * 128:(c4 + 1) * 128], kdefer.ap[:, c4 * 128:(c4 + 1) * 128], ident.ap),
                            reads=(kdefer, ident), writes=(pT,), signal=(c4 == 3))
                    kb.op("dve", lambda e, pT=pT, blk=blk: e.tensor_copy(
                        out=fst.ap[:, 4:8, blk * 128:(blk + 1) * 128],
                        in_=pT.ap[:, 0:512].rearrange("p (c n) -> p c n", n=128)),
                        reads=(pT,), writes=(fst,))
                pairs = []
                for g4, dst in enumerate((qt_sb, kt_sb, qt_bd, kt_bd)):
                    pairs.append((dst[:, :, t0:t0 + n].rearrange("h p n -> p h n"), fst.ap[:, g4 * 4:(g4 + 1) * 4, 0:n]))
                kb.store(pairs, fst, writes=(dbuf(("fm", ti)),))
            p_norm(0)
            for ti in range(len(tiles)):
                if ti + 1 < len(tiles):
                    p_norm(ti + 1)
                p_mm(ti)
            kb.barrier()
            kb.release_dsems(wst + xin + fmst + tmst + vst)

            if stop_after == "P":
                raise _Stop()
            kb.new_phase("SB")
            kb.sb_off = const_end
            qkv = []
            for i in range(2):
                qkv.append((sb(S, BF16, name=f"QT{i}"), sb(S, BF16, name=f"KT{i}"),
                            sb(NKB * 128, BF16, (NKB, 128), f"V{i}")))
            l_r = ring(3, 1024, BF16, (2, 512), "L")
            w_r = ring(3, 1024, BF16, (2, 512), "w")
            ra_r = ring(3, 1024, BF16, (2, 512), "ra")
            ost = ring(2, 512, BF16, name="ost")
            zc_ps = [pbuf(bank(2 * i_, 2).rearrange("p (b n) -> p b n", n=512)) for i_ in range(3)]
            o_ps = [pbuf(bank(6)), pbuf(bank(7))]
            dram_ot = {}

            def load_qkv(hp, slot, qsrc, ksrc, vsrc, vkey):
                QT, KT, V = qkv[slot]
                rd = [dbuf(("fm", t)) for t in range(NQT)]
                kb.load([(QT.ap, qsrc[hp, :, 0:S])], QT, reads=rd)
                kb.load([(KT.ap, ksrc[hp, :, 0:S])], KT, reads=rd)
                rdv = [dbuf((vkey, b)) for b in range(NKB)]
                pairs = []
                for b0 in range(0, NKB, 16):
                    b1 = min(NKB, b0 + 16)
                    pairs.append((V.ap[:, b0:b1, :],
                                  vsrc[b0 * 128:b1 * 128, hp * 128:(hp + 1) * 128].rearrange("(b p) f -> p b f", p=128)))
                kb.load(pairs, V, reads=rdv)

            its = []
            for hp in range(4):
                for i in range(NQT):
                    js = list(range(4 * i + 3, -1, -1))
                    for n_, j in enumerate(js):
                        m = j - 4 * i
                        c0 = 128 * m if m > 0 else 0
                        its.append(dict(hp=hp, i=i, j=j, c0=c0, diag=(m >= 0), first=(n_ == 0),
                                        last=(n_ == len(js) - 1), slot=hp % 2, qt=hp * NQT + i))
            NIT = len(its)
            load_qkv(0, 0, qt_sb, kt_sb, v_sb, "v_sb")
            loaded = {0}

            def st_qk(k):
                it = its[k]
                QT, KT, V = qkv[it["slot"]]
                z = zc_ps[k % 3]; c0 = it["c0"]; i = it["i"]; j = it["j"]
                for b in range(2):
                    kb.op("pe", lambda e, b=b: e.matmul(
                        z.ap[:, b, c0:512], KT.ap[b * 64:(b + 1) * 64, j * 128:(j + 1) * 128],
                        QT.ap[b * 64:(b + 1) * 64, i * 512 + c0:(i + 1) * 512], start=True, stop=True),
                        reads=(QT, KT), writes=(z,), signal=(b == 1))

            def st_l(k):
                it = its[k]; z = zc_ps[k % 3]; lb = l_r[k % 3]; c0 = it["c0"]
                if c0 > 0:
                    kb.op("pool", lambda e: e.memset(lb.ap[:, :, 0:c0], 0.0), writes=(lb,))
                kb.op("act", lambda e: e.activation(out=lb.ap[:, :, c0:512], in_=z.ap[:, :, c0:512],
                                                    func=AF.Softplus, scale=0.125), reads=(z,), writes=(lb,))
                if it["diag"]:
                    for b in range(2):
                        kb.op("dve", lambda e, b=b: e.tensor_tensor(out=lb.ap[:, b, c0:c0 + 128],
                                                                    in0=lb.ap[:, b, c0:c0 + 128], in1=mc.ap,
                                                                    op=ALU.mult), reads=(lb, mc), writes=(lb,))

            def st_ra(k):
                it = its[k]
                if it["last"]:
                    return
                lb = l_r[k % 3]; rn = ra_r[(k + 1) % 3]; rc = ra_r[k % 3]
                if it["first"]:
                    kb.op("dve", lambda e: e.tensor_copy(out=rn.ap, in_=lb.ap), reads=(lb,), writes=(rn,))
                else:
                    kb.op("dve", lambda e: e.tensor_tensor(out=rn.ap, in0=rc.ap, in1=lb.ap, op=ALU.add),
                          reads=(rc, lb), writes=(rn,))

            def st_c(k):
                it = its[k]; lb = l_r[k % 3]; rc = ra_r[k % 3]; cp = zc_ps[k % 3]
                for b in range(2):
                    kb.op("pe", lambda e, b=b: e.matmul(cp.ap[:, b, :], tri8.ap, lb.ap[:, b, :], start=False,
                                                        stop=it["first"], skip_group_check=True),
                          reads=(tri8, lb), writes=(cp,), signal=(it["first"] and b == 1))
                    if not it["first"]:
                        kb.op("pe", lambda e, b=b: e.matmul(cp.ap[:, b, :], ones8.ap, rc.ap[:, b, :], start=False,
                                                            stop=True, skip_group_check=True),
                              reads=(ones8, rc), writes=(cp,), signal=(b == 1))

            def st_w(k):
                it = its[k]; cp = zc_ps[k % 3]; wb = w_r[k % 3]; c0 = it["c0"]
                if c0 > 0:
                    kb.op("pool", lambda e: e.memset(wb.ap[:, :, 0:c0], 0.0), writes=(wb,))
                kb.op("act", lambda e: e.activation(out=wb.ap[:, :, c0:512], in_=cp.ap[:, :, c0:512],
                                                    func=AF.Softplus, scale=0.125, bias=-SHIFT),
                      reads=(cp,), writes=(wb,))
                if it["diag"]:
                    for b in range(2):
                        kb.op("dve", lambda e, b=b: e.tensor_tensor(out=wb.ap[:, b, c0:c0 + 128],
                                                                    in0=wb.ap[:, b, c0:c0 + 128], in1=mc.ap,
                                                                    op=ALU.mult), reads=(wb, mc), writes=(wb,))

            def st_pv(k):
                it = its[k]; wb = w_r[k % 3]; QT, KT, V = qkv[it["slot"]]
                op_ = o_ps[it["qt"] % 2]; j = it["j"]
                for b in range(2):
                    kb.op("pe", lambda e, b=b: e.matmul(op_.ap[b * 64:(b + 1) * 64, :], V.ap[:, j, b * 64:(b + 1) * 64],
                                                        wb.ap[:, b, :], start=it["first"], stop=it["last"]),
                          reads=(V, wb), writes=(op_,), signal=(b == 1))
                if it["last"]:
                    os_ = ost[it["qt"] % 2]
                    kb.op("dve", lambda e: e.tensor_scalar(out=os_.ap, in0=op_.ap, scalar1=ESHIFT, scalar2=None,
                                                           op0=ALU.mult), reads=(op_,), writes=(os_,))
                    t0 = it["i"] * 512
                    d_ = Buf(None); dram_ot[(it["hp"], it["i"])] = d_
                    kb.store([(ot[it["hp"], :, t0:t0 + 512], os_.ap)], os_, writes=(d_,))

            st_qk(0)
            for r in range(NIT + 3):
                if 0 <= r - 2 < NIT:
                    it = its[r - 2]
                    if it["i"] == 0 and it["first"] and it["hp"] + 1 < 4 and (it["hp"] + 1) not in loaded:
                        load_qkv(it["hp"] + 1, (it["hp"] + 1) % 2, qt_sb, kt_sb, v_sb, "v_sb")
                        loaded.add(it["hp"] + 1)
                if r + 1 < NIT:
                    st_qk(r + 1)
                if r < NIT:
                    st_l(r)
                if 0 <= r - 1 < NIT:
                    st_w(r - 1)
                if r < NIT:
                    st_ra(r)
                    st_c(r)
                if 0 <= r - 2 < NIT:
                    st_pv(r - 2)
            kb.barrier()

            if stop_after == "SB":
                raise _Stop()
            kb.new_phase("BD")
            wb_r = ring(3, 1024, BF16, (2, 512), "wb")
            rd_r = ring(2, 512, F32, name="rden")
            zb_ps = [pbuf(bank(2 * i_, 2).rearrange("p (b n) -> p b n", n=512)) for i_ in range(3)]
            ob_ps = [pbuf(bank(6))]
            dn_ps = [pbuf(bank(7))]
            load_qkv(0, 0, qt_bd, kt_bd, v_bd, "v_bd")
            brecs = []
            for hp in range(4):
                for i in range(NQT):
                    blocks = [(4 * i + m, 128 * m, 512, "near", m) for m in range(4)]
                    if i > 0:
                        blocks += [(4 * i - 4 + jj, 0, 128 * (jj + 1), "far", jj) for jj in range(4)]
                    for n_, (j, a, b_, kind, m) in enumerate(blocks):
                        brecs.append(dict(hp=hp, i=i, j=j, a=a, b_=b_, kind=kind, m=m, first=(n_ == 0),
                                          last=(n_ == len(blocks) - 1), qi=hp * NQT + i))
            NBR = len(brecs)

            def bd_qk(n):
                rc = brecs[n]; QT, KT, V = qkv[rc["hp"] % 2]; z = zb_ps[n % 3]
                j, a, b_, i = rc["j"], rc["a"], rc["b_"], rc["i"]
                for b in range(2):
                    kb.op("pe", lambda e, b=b: e.matmul(
                        z.ap[:, b, a:b_], KT.ap[b * 64:(b + 1) * 64, j * 128:(j + 1) * 128],
                        QT.ap[b * 64:(b + 1) * 64, i * 512 + a:i * 512 + b_], start=True, stop=True),
                        reads=(QT, KT), writes=(z,), signal=(b == 1))

            wbh = [[Buf(w_.ap[:, b]) for b in range(2)] for w_ in wb_r]

            def bd_exp(n):
                rc = brecs[n]; z = zb_ps[n % 3]; wb = wb_r[n % 3]; wh = wbh[n % 3]
                a, b_, kind, m, hp = rc["a"], rc["b_"], rc["kind"], rc["m"], rc["hp"]
                kb.op("act", lambda e: e.activation(out=wb.ap[:, :, a:b_], in_=z.ap[:, :, a:b_], func=AF.Exp,
                                                    scale=0.125), reads=(z,), writes=(wh[0], wh[1]))
                for b in range(2):
                    h = 2 * hp + b
                    if kind == "near":
                        wd = min(256, 512 - a)
                        kb.op("dve", lambda e, b=b, wd=wd, h=h: e.tensor_tensor(
                            out=wb.ap[:, b, a:a + wd], in0=wb.ap[:, b, a:a + wd], in1=wn_b.ap[:, h, 0:wd],
                            op=ALU.mult), reads=(wh[b], wn_b), writes=(wh[b],))
                    else:
                        kb.op("dve", lambda e, b=b: e.tensor_tensor(
                            out=wb.ap[:, b, b_ - 128:b_], in0=wb.ap[:, b, b_ - 128:b_], in1=mfar_b.ap,
                            op=ALU.mult), reads=(wh[b], mfar_b), writes=(wh[b],))
                        if m == 3:
                            kb.op("dve", lambda e, b=b, h=h: e.tensor_tensor(
                                out=wb.ap[:, b, 0:128], in0=wb.ap[:, b, 0:128], in1=en_b.ap[:, h, :],
                                op=ALU.mult), reads=(wh[b], en_b), writes=(wh[b],))

            def bd_pv(n):
                rc = brecs[n]; QT, KT, V = qkv[rc["hp"] % 2]; wb = wb_r[n % 3]; wh = wbh[n % 3]
                j, a, b_, qi = rc["j"], rc["a"], rc["b_"], rc["qi"]
                op_ = ob_ps[0]; dn = dn_ps[0]
                for b in range(2):
                    kb.op("pe", lambda e, b=b: e.matmul(
                        op_.ap[b * 64:(b + 1) * 64, a:b_], V.ap[:, j, b * 64:(b + 1) * 64], wb.ap[:, b, a:b_],
                        start=rc["first"], stop=rc["last"]), reads=(V, wh[b]), writes=(op_,), signal=False)
                for b in range(2):
                    kb.op("pe", lambda e, b=b: e.matmul(
                        dn.ap[b * 64:(b + 1) * 64, a:b_], ones1.ap, wb.ap[:, b, a:b_],
                        start=rc["first"], stop=rc["last"]), reads=(ones1, wh[b]), writes=(dn,), signal=(b == 1))
                if rc["last"]:
                    rd = rd_r[qi % 2]; os_ = ost[qi % 2]
                    kb.op("act", lambda e: e.activation(out=rd.ap, in_=dn.ap, func=AF.Ln), reads=(dn,), writes=(rd,))
                    kb.op("act", lambda e: e.activation(out=rd.ap, in_=rd.ap, func=AF.Exp, scale=-1.0),
                          reads=(rd,), writes=(rd,))
                    kb.op("dve", lambda e: e.tensor_tensor(out=os_.ap, in0=op_.ap, in1=rd.ap, op=ALU.mult),
                          reads=(op_, rd), writes=(os_,))
                    d_ = Buf(None); dram_ot[(4 + rc["hp"], rc["i"])] = d_
                    kb.store([(ot[4 + rc["hp"], :, rc["i"] * 512:(rc["i"] + 1) * 512], os_.ap)], os_, writes=(d_,))

            bloaded = {0}
            bd_qk(0)
            bd_qk(1)
            for r in range(NBR + 1):
                if 0 <= r - 1 < NBR:
                    rc = brecs[r - 1]
                    if rc["i"] == 0 and rc["first"] and rc["hp"] + 1 < 4 and (rc["hp"] + 1) not in bloaded:
                        load_qkv(rc["hp"] + 1, (rc["hp"] + 1) % 2, qt_bd, kt_bd, v_bd, "v_bd")
                        bloaded.add(rc["hp"] + 1)
                if r + 2 < NBR:
                    bd_qk(r + 2)
                if r < NBR:
                    bd_exp(r)
                if 0 <= r - 1 < NBR:
                    bd_pv(r - 1)
            kb.barrier()
            kb.release_dsems([b for t in qkv for b in t] + ost)

            if stop_after == "BD":
                raise _Stop()
            kb.new_phase("SA")
            kb.sb_off = const_end
            ktc = sb(2 * 4 * PAST, BF16, (2, 4, PAST), "ktc")
            vc = sb(2 * 16 * W, BF16, (2, 16, W), "vc")
            ktb = sb(2 * 4 * BROWS, BF16, (2, 4, BROWS), "ktb")
            vbc = sb(2 * 4 * W, BF16, (2, 4, W), "vbc")
            cst = ring(2, 4 * W, F32, (4, W), "cst")
            cbf = ring(2, 4 * W, BF16, (4, W), "cbf")
            qs_sb = sb(4 * TS, BF16, (4, TS), "qs_sb")
            ks_sb = sb(4 * 2 * 128, BF16, (4, 2, 128), "ks_sb")
            qs_bd = sb(4 * TS, BF16, (4, TS), "qs_bd")
            ks_bd = sb(4 * 2 * 128, BF16, (4, 2, 128), "ks_bd")
            vs_sb = sb(2 * W, BF16, (2, W), "vs_sb")
            vs_bd = sb(2 * W, BF16, (2, W), "vs_bd")
            sl_r = ring(3, 512, BF16, (2, 256), "sl")
            sw_r = ring(3, 512, BF16, (2, 256), "sw")
            sra_r = ring(3, 512, BF16, (2, 256), "sra")
            sos = ring(2, 256, BF16, (4, 64), "sos")
            srd = ring(2, 256, F32, name="srd")
            fm_s = [dbuf(("fm", NQT))]
            kb.op("pool", lambda e: e.memset(ks_sb.ap, 0.0), writes=(ks_sb,))
            kb.op("pool", lambda e: e.memset(ks_bd.ap, 0.0), writes=(ks_bd,))
            kb.op("pool", lambda e: e.memset(vs_sb.ap, 0.0), writes=(vs_sb,))
            kb.op("pool", lambda e: e.memset(vs_bd.ap, 0.0), writes=(vs_bd,))
            kb.load([(qs_sb.ap, qt_sb[:, :, S:S + TS].rearrange("h p n -> p h n"))], qs_sb, reads=fm_s)
            kb.load([(qs_bd.ap, qt_bd[:, :, S:S + TS].rearrange("h p n -> p h n"))], qs_bd, reads=fm_s)
            kb.load([(ks_sb.ap[:, :, s, 0:64], kt_sb[:, :, S + s * 64:S + (s + 1) * 64].rearrange("h p n -> p h n"))
                     for s in range(2)], ks_sb, reads=fm_s)
            kb.load([(ks_bd.ap[:, :, s, 0:64], kt_bd[:, :, S + s * 64:S + (s + 1) * 64].rearrange("h p n -> p h n"))
                     for s in range(2)], ks_bd, reads=fm_s)
            kb.load([(vs_sb.ap[0:64, s, :], v_sb[S + s * 64:S + (s + 1) * 64, :]) for s in range(2)], vs_sb,
                    reads=[dbuf(("v_sb", S // 128))])
            kb.load([(vs_bd.ap[0:64, s, :], v_bd[S + s * 64:S + (s + 1) * 64, :]) for s in range(2)], vs_bd,
                    reads=[dbuf(("v_bd", S // 128))])
            sT = [pbuf(bank(7).bitcast(BF16))]
            ccnt = 0
            tcnt = 0
            for (ksrc, vsrc, nblk, kdst, vdst) in ((csk, csv, 16, ktc, vc), (cbk, cbv, 4, ktb, vbc)):
                for s in range(2):
                    for g in range(nblk // 4):
                        st = cst[ccnt % 2]; cb = cbf[ccnt % 2]; ccnt += 1
                        kb.load([(st.ap, ksrc[s, g * 512:(g + 1) * 512, :].rearrange("(b p) f -> p b f", p=128))], st)
                        kb.op("dve", lambda e, st=st, cb=cb: e.tensor_copy(out=cb.ap, in_=st.ap), reads=(st,), writes=(cb,))
                        for hp in range(4):
                            pT = sT[0]; tcnt += 1
                            for b4 in range(4):
                                kb.op("pe", lambda e, pT=pT, cb=cb, b4=b4, hp=hp: e.transpose(
                                    pT.ap[:, b4 * 128:(b4 + 1) * 128], cb.ap[:, b4, hp * 128:(hp + 1) * 128], ident.ap),
                                    reads=(cb, ident), writes=(pT,), signal=(b4 == 3))
                            kb.op("act", lambda e, pT=pT, s=s, hp=hp, g=g, kdst=kdst: e.activation(
                                out=kdst.ap[:, s, hp, g * 512:(g + 1) * 512], in_=pT.ap[:, 0:512], func=AF.Copy),
                                reads=(pT,), writes=(kdst,))
                        st2 = cst[ccnt % 2]; ccnt += 1
                        kb.load([(st2.ap, vsrc[s, g * 512:(g + 1) * 512, :].rearrange("(b p) f -> p b f", p=128))], st2)
                        kb.op("dve", lambda e, st2=st2, s=s, g=g, vdst=vdst: e.tensor_copy(
                            out=vdst.ap[:, s, g * 4:(g + 1) * 4, :], in_=st2.ap), reads=(st2,), writes=(vdst,))

            zs_ps = [pbuf(bank(2 * i_, 2).rearrange("p (b n) -> p b n", n=512)) for i_ in range(3)]
            os_ps = [pbuf(bank(6))]
            dns_ps = [pbuf(bank(6)[:, 256:512])]
            dram_ots = Buf(None)
            sits = []
            for s in range(2):
                for n_, j in enumerate([16] + list(range(15, -1, -1))):
                    sits.append(dict(s=s, j=j, first=(n_ == 0), last=(n_ == 16)))
            NS = len(sits)

            def kt_blk(it, b, hh):
                if it["j"] == 16:
                    return ks_sb.ap[b * 64:(b + 1) * 64, hh, it["s"], :]
                return ktc.ap[b * 64:(b + 1) * 64, it["s"], hh, it["j"] * 128:(it["j"] + 1) * 128]

            def v_blk(it, h):
                if it["j"] == 16:
                    return vs_sb.ap[:, it["s"], h * 64:(h + 1) * 64]
                return vc.ap[:, it["s"], it["j"], h * 64:(h + 1) * 64]

            def ss_qk(k):
                it = sits[k]; z = zs_ps[k % 3]; s = it["s"]
                for hh in range(4):
                    for b in range(2):
                        kb.op("pe", lambda e, b=b, hh=hh: e.matmul(
                            z.ap[:, b, hh * 64:(hh + 1) * 64], kt_blk(it, b, hh),
                            qs_sb.ap[b * 64:(b + 1) * 64, hh, s * 64:(s + 1) * 64], start=(hh == 0), stop=(hh == 3),
                            skip_group_check=True),
                            reads=(ktc, ks_sb, qs_sb), writes=(z,), signal=(hh == 3 and b == 1))

            def ss_l(k):
                it = sits[k]; z = zs_ps[k % 3]; lb = sl_r[k % 3]
                kb.op("act", lambda e: e.activation(out=lb.ap, in_=z.ap[:, :, 0:256], func=AF.Softplus, scale=0.125),
                      reads=(z,), writes=(lb,))
                if it["j"] == 16:
                    for b in range(2):
                        for hh in range(4):
                            kb.op("dve", lambda e, b=b, hh=hh: e.tensor_tensor(
                                out=lb.ap[:, b, hh * 64:(hh + 1) * 64], in0=lb.ap[:, b, hh * 64:(hh + 1) * 64],
                                in1=mc.ap[:, 0:64], op=ALU.mult), reads=(lb, mc), writes=(lb,))

            def ss_ra(k):
                it = sits[k]
                if it["last"]:
                    return
                lb = sl_r[k % 3]; rn = sra_r[(k + 1) % 3]; rc = sra_r[k % 3]
                if it["first"]:
                    kb.op("dve", lambda e: e.tensor_copy(out=rn.ap, in_=lb.ap), reads=(lb,), writes=(rn,))
                else:
                    kb.op("dve", lambda e: e.tensor_tensor(out=rn.ap, in0=rc.ap, in1=lb.ap, op=ALU.add),
                          reads=(rc, lb), writes=(rn,))

            def ss_c(k):
                it = sits[k]; lb = sl_r[k % 3]; rc = sra_r[k % 3]; cp = zs_ps[k % 3]
                for b in range(2):
                    kb.op("pe", lambda e, b=b: e.matmul(cp.ap[:, b, 0:256], tri8.ap, lb.ap[:, b, :], start=False,
                                                        stop=it["first"], skip_group_check=True),
                          reads=(tri8, lb), writes=(cp,), signal=(it["first"] and b == 1))
                    if not it["first"]:
                        kb.op("pe", lambda e, b=b: e.matmul(cp.ap[:, b, 0:256], ones8.ap, rc.ap[:, b, :], start=False,
                                                            stop=True, skip_group_check=True),
                              reads=(ones8, rc), writes=(cp,), signal=(b == 1))

            def ss_w(k):
                it = sits[k]; cp = zs_ps[k % 3]; wb = sw_r[k % 3]
                kb.op("act", lambda e: e.activation(out=wb.ap, in_=cp.ap[:, :, 0:256], func=AF.Softplus, scale=0.125,
                                                    bias=-SHIFT), reads=(cp,), writes=(wb,))
                if it["j"] == 16:
                    for b in range(2):
                        for hh in range(4):
                            kb.op("dve", lambda e, b=b, hh=hh: e.tensor_tensor(
                                out=wb.ap[:, b, hh * 64:(hh + 1) * 64], in0=wb.ap[:, b, hh * 64:(hh + 1) * 64],
                                in1=mc.ap[:, 0:64], op=ALU.mult), reads=(wb, mc), writes=(wb,))

            def ss_pv(k):
                it = sits[k]; wb = sw_r[k % 3]; op_ = os_ps[0]; s = it["s"]
                for hh in range(4):
                    for b in range(2):
                        h = 2 * hh + b
                        kb.op("pe", lambda e, b=b, hh=hh, h=h: e.matmul(
                            op_.ap[b * 64:(b + 1) * 64, hh * 64:(hh + 1) * 64], v_blk(it, h),
                            wb.ap[:, b, hh * 64:(hh + 1) * 64], start=(it["first"] and hh == 0),
                            stop=(it["last"] and hh == 3), skip_group_check=True),
                            reads=(vc, vs_sb, wb), writes=(op_,), signal=(hh == 3 and b == 1))
                if it["last"]:
                    os_ = sos[s % 2]
                    kb.op("dve", lambda e: e.tensor_scalar(out=os_.ap.rearrange("p h q -> p (h q)"), in0=op_.ap[:, 0:256],
                                                           scalar1=ESHIFT, scalar2=None, op0=ALU.mult),
                          reads=(op_,), writes=(os_,))
                    kb.store([(ot[0:4, :, S + s * 64:S + (s + 1) * 64].rearrange("h p n -> p h n"), os_.ap)], os_,
                             writes=(dram_ots,))

            ss_qk(0)
            for r in range(NS + 3):
                if r + 1 < NS:
                    ss_qk(r + 1)
                if r < NS:
                    ss_l(r)
                if 0 <= r - 1 < NS:
                    ss_w(r - 1)
                if r < NS:
                    ss_ra(r)
                    ss_c(r)
                if 0 <= r - 2 < NS:
                    ss_pv(r - 2)
            scnt = 0
            for s in range(2):
                op_ = os_ps[0]; dn = dns_ps[0]
                order = [4, 3, 2, 1, 0]
                for n_, j in enumerate(order):
                    z = zs_ps[scnt % 2]; wb = sw_r[scnt % 3]; scnt += 1
                    for hh in range(4):
                        for b in range(2):
                            kt_ap = (ks_bd.ap[b * 64:(b + 1) * 64, hh, s, :] if j == 4
                                     else ktb.ap[b * 64:(b + 1) * 64, s, hh, j * 128:(j + 1) * 128])
                            kb.op("pe", lambda e, b=b, hh=hh, z=z, kt_ap=kt_ap: e.matmul(
                                z.ap[:, b, hh * 64:(hh + 1) * 64], kt_ap,
                                qs_bd.ap[b * 64:(b + 1) * 64, hh, s * 64:(s + 1) * 64], start=True, stop=True),
                                reads=(ktb, ks_bd, qs_bd), writes=(z,), signal=(hh == 3 and b == 1))
                    kb.op("act", lambda e, z=z, wb=wb: e.activation(out=wb.ap, in_=z.ap[:, :, 0:256], func=AF.Exp,
                                                                    scale=0.125), reads=(z,), writes=(wb,))
                    if j == 4:
                        kb.op("dve", lambda e, wb=wb: e.tensor_tensor(out=wb.ap, in0=wb.ap, in1=fsn.ap, op=ALU.mult),
                              reads=(wb, fsn), writes=(wb,))
                    elif j == 3:
                        kb.op("dve", lambda e, wb=wb: e.tensor_tensor(out=wb.ap, in0=wb.ap, in1=fs3.ap, op=ALU.mult),
                              reads=(wb, fs3), writes=(wb,))
                    fst_, lst_ = (n_ == 0), (n_ == 4)
                    for hh in range(4):
                        for b in range(2):
                            h = 2 * hh + b
                            v_ap = vs_bd.ap[:, s, h * 64:(h + 1) * 64] if j == 4 else vbc.ap[:, s, j, h * 64:(h + 1) * 64]
                            kb.op("pe", lambda e, b=b, hh=hh, wb=wb, v_ap=v_ap, fst_=fst_, lst_=lst_: e.matmul(
                                op_.ap[b * 64:(b + 1) * 64, hh * 64:(hh + 1) * 64], v_ap,
                                wb.ap[:, b, hh * 64:(hh + 1) * 64], start=(fst_ and hh == 0),
                                stop=(lst_ and hh == 3), skip_group_check=True),
                                reads=(vbc, vs_bd, wb), writes=(op_,), signal=False)
                    for b in range(2):
                        kb.op("pe", lambda e, b=b, wb=wb, fst_=fst_, lst_=lst_: e.matmul(
                            dn.ap[b * 64:(b + 1) * 64, :], ones1.ap, wb.ap[:, b, :], start=False, stop=lst_,
                            skip_group_check=True), reads=(ones1, wb), writes=(dn, op_), signal=(b == 1))
                rd = srd[s % 2]; os_ = sos[s % 2]
                kb.op("dve", lambda e, rd=rd: e.reciprocal(out=rd.ap, in_=dn.ap), reads=(dn, op_), writes=(rd,))
                kb.op("dve", lambda e, rd=rd, os_=os_: e.tensor_tensor(out=os_.ap.rearrange("p h q -> p (h q)"),
                                                                       in0=op_.ap[:, 0:256], in1=rd.ap, op=ALU.mult),
                      reads=(op_, rd), writes=(os_,))
                kb.store([(ot[4:8, :, S + s * 64:S + (s + 1) * 64].rearrange("h p n -> p h n"), os_.ap)], os_,
                         writes=(dram_ots,))
            kb.barrier()
            kb.release_dsems(cst + sos + [qs_sb, ks_sb, qs_bd, ks_bd, vs_sb, vs_bd])

            if stop_after == "SA":
                raise _Stop()
            kb.new_phase("O")
            kb.sb_off = const_end
            w_out_sb = sb(8 * D, BF16, (8, D), "w_out_sb")
            wst2 = ring(2, 4 * D, F32, (4, D), "wst2")
            for g in range(2):
                st = wst2[g % 2]
                kb.load([(st.ap, w_out[g * 512:(g + 1) * 512, :].rearrange("(c p) n -> p c n", p=128))], st)
                for c in range(4):
                    kb.op("dve", lambda e, st=st, c=c, g=g: e.tensor_scalar(
                        out=w_out_sb.ap[:, g * 4 + c, :], in0=st.ap[:, c, :], scalar1=gout_c.ap[:, g * 4 + c:g * 4 + c + 1],
                        scalar2=None, op0=ALU.mult), reads=(st, gout_c), writes=(w_out_sb,))
            oT_r = ring(2, 8 * 512, BF16, (8, 512), "oT")
            osq_r = ring(2, 8 * 512, BF16, (8, 512), "osq")
            xo_r = ring(3, D, F32, name="xo")
            h_r = ring(3, D, F32, name="h")
            rs_r = ring(4, 8, F32, name="rs")
            st_ps = [pbuf(bank(0)[:, 0:2]), pbuf(bank(1)[:, 0:2])]
            oo_ps = [(pbuf(bank(2, 2)), pbuf(bank(4, 2)))]
            dram_h = {}
            ocnt = 0
            for ti, (t0, n, is_s) in enumerate(tiles):
                oT = oT_r[ti % 2]; osq = osq_r[ti % 2]
                if is_s:
                    rd = [dram_ots]
                else:
                    rd = [dram_ot[(c, ti)] for c in range(8)]
                kb.load([(oT.ap[:, :, 0:n], ot[:, :, t0:t0 + n].rearrange("h p n -> p h n"))], oT, reads=rd)
                kb.op("act", lambda e, oT=oT, osq=osq: e.activation(out=osq.ap[:, :, 0:n], in_=oT.ap[:, :, 0:n],
                                                                    func=AF.Square), reads=(oT,), writes=(osq,))
                for blk in range(n // 128):
                    r0 = t0 + blk * 128
                    xi = xo_r[ocnt % 3]; hb = h_r[ocnt % 3]; rs = rs_r[ocnt % 4]; sp_ = st_ps[ocnt % 2]
                    pa, pb_ = oo_ps[0]
                    ocnt += 1
                    rows = x_s[blk * 128:(blk + 1) * 128, :] if is_s else x_p[r0:r0 + 128, :]
                    kb.load([(xi.ap, rows)], xi)
                    for g in range(2):
                        for c in range(4):
                            kb.op("pe", lambda e, g=g, c=c, sp_=sp_, osq=osq, blk=blk: e.matmul(
                                sp_.ap[:, g:g + 1], osq.ap[:, g * 4 + c, blk * 128:(blk + 1) * 128], ones1.ap[:, 0:1],
                                start=(c == 0), stop=(c == 3), skip_group_check=True),
                                reads=(osq, ones1), writes=(sp_,), signal=(g == 1 and c == 3))
                    kb.op("act", lambda e, rs=rs, sp_=sp_: e.activation(out=rs.ap[:, 0:2], in_=sp_.ap, func=AF.Ln,
                                                                        scale=1.0 / W, bias=EPS),
                          reads=(sp_,), writes=(rs,))
                    kb.op("act", lambda e, rs=rs: e.activation(out=rs.ap[:, 2:4], in_=rs.ap[:, 0:2], func=AF.Exp,
                                                               scale=-0.5), reads=(rs,), writes=(rs,))
                    for g, pg in ((0, pa), (1, pb_)):
                        for nh in range(2):
                            for c in range(4):
                                kb.op("pe", lambda e, g=g, nh=nh, c=c, pg=pg, oT=oT, blk=blk: e.matmul(
                                    pg.ap[:, nh * 512:(nh + 1) * 512], oT.ap[:, g * 4 + c, blk * 128:(blk + 1) * 128],
                                    w_out_sb.ap[:, g * 4 + c, nh * 512:(nh + 1) * 512], start=(c == 0), stop=(c == 3)),
                                    reads=(oT, w_out_sb), writes=(pg,), signal=(nh == 1 and c == 3))
                    kb.op("dve", lambda e, hb=hb, pa=pa, rs=rs, xi=xi: e.scalar_tensor_tensor(
                        out=hb.ap, in0=pa.ap, scalar=rs.ap[:, 2:3], in1=xi.ap, op0=ALU.mult, op1=ALU.add),
                        reads=(pa, rs, xi), writes=(hb,))
                    kb.op("dve", lambda e, hb=hb, pb_=pb_, rs=rs: e.scalar_tensor_tensor(
                        out=hb.ap, in0=pb_.ap, scalar=rs.ap[:, 3:4], in1=hb.ap, op0=ALU.mult, op1=ALU.add),
                        reads=(pb_, rs, hb), writes=(hb,))
                    d_ = Buf(None); dram_h[r0 // 128] = d_
                    kb.store([(hs[r0:r0 + 128, :], hb.ap)], hb, writes=(d_,))
            kb.barrier()
            kb.release_dsems(wst2 + oT_r + xo_r + h_r)

            if stop_after == "O":
                raise _Stop()
            kb.new_phase("F")
            kb.sb_off = const_end
            w_up_sb = sb(8 * DFF, BF16, (8, DFF), "w_up_sb")
            w_dn_sb = sb(32 * D, BF16, (32, D), "w_dn_sb")
            s4f = ring(4, 4, F32, name="s4f")
            s4g = ring(4, 4, F32, name="s4g")
            wst3_off = kb.sb_off
            wst3 = ring(2, DFF, F32, name="wst3")
            for c in range(8):
                st = wst3[c % 2]
                kb.load([(st.ap, w_up[c * 128:(c + 1) * 128, :])], st)
                kb.op("dve", lambda e, st=st, c=c: e.tensor_scalar(
                    out=w_up_sb.ap[:, c, :], in0=st.ap, scalar1=gffn_c.ap[:, c:c + 1], scalar2=None, op0=ALU.mult),
                    reads=(st, gffn_c), writes=(w_up_sb,))
            for g in range(8):
                st = wst3[g % 2]
                stv = st.ap.rearrange("p (c n) -> p c n", n=D)
                kb.load([(stv, w_down[g * 512:(g + 1) * 512, :].rearrange("(c p) n -> p c n", p=128))], st)
                kb.op("act" if g % 2 else "dve",
                      (lambda e, stv=stv, g=g: e.activation(out=w_dn_sb.ap[:, g * 4:(g + 1) * 4, :], in_=stv, func=AF.Copy))
                      if g % 2 else
                      (lambda e, stv=stv, g=g: e.tensor_copy(out=w_dn_sb.ap[:, g * 4:(g + 1) * 4, :], in_=stv)),
                      reads=(st,), writes=(w_dn_sb,))
            FT = 256
            kb.barrier()
            kb.release_dsems(wst3)
            kb.sb_off = wst3_off
            hin = ring(4, D, F32, name="hin")
            junk = ring(1, D, BF16, name="junk")
            hn_r = ring(2, D, BF16, name="hn")
            hnT = ring(2, 8 * FT, BF16, (8, FT), "hnT")
            aT = ring(1, 32 * FT, BF16, (32, FT), "aT")
            rl = ring(3, FT, BF16, name="rl")
            up_ps = [pbuf(bank(0)), pbuf(bank(1))]
            dn_ps2 = [pbuf(bank(2, 2)), pbuf(bank(4, 2))]
            ps_T = [pbuf(bank(6).bitcast(BF16)), pbuf(bank(7).bitcast(BF16))]
            ftiles = []
            for (t0, n, is_s) in tiles:
                for a in range(0, n, FT):
                    ftiles.append((t0 + a, min(FT, n - a), is_s, a))
            fstate = {"fcnt": 0, "ucnt": 0}
            fh = {}

            def f_norm(fi):
                (t0, n, is_s, a0) = ftiles[fi]
                hT = hnT[fi % 2]
                hbufs = []
                for blk in range(n // 128):
                    fcnt = fstate["fcnt"]
                    r0 = t0 + blk * 128
                    hi = hin[fcnt % 4]; ho = hn_r[fcnt % 2]; s4 = s4f[fcnt % 4]
                    hbufs.append((hi, r0))
                    kb.load([(hi.ap, hs[r0:r0 + 128, :])], hi, reads=[dram_h[r0 // 128]])
                    kb.op("act", lambda e: e.activation(out=ho.ap, in_=hi.ap, func=AF.Square, accum_out=s4.ap[:, 0:1]),
                          reads=(hi,), writes=(ho, s4))
                    kb.op("act", lambda e: e.activation(out=s4.ap[:, 1:2], in_=s4.ap[:, 0:1], func=AF.Ln,
                                                        scale=1.0 / D, bias=EPS), reads=(s4,), writes=(s4,))
                    kb.op("act", lambda e: e.activation(out=s4.ap[:, 2:3], in_=s4.ap[:, 1:2], func=AF.Exp,
                                                        scale=-0.5), reads=(s4,), writes=(s4,))
                    kb.op("dve", lambda e: e.tensor_scalar(out=ho.ap, in0=hi.ap, scalar1=s4.ap[:, 2:3],
                                                           scalar2=None, op0=ALU.mult),
                          reads=(hi, s4), writes=(ho,))
                    pT = ps_T[fcnt % 2]
                    for c in range(8):
                        kb.op("pe", lambda e, c=c: e.transpose(pT.ap[:, c * 128:(c + 1) * 128],
                                                               ho.ap[:, c * 128:(c + 1) * 128], ident.ap),
                              reads=(ho, ident), writes=(pT,), signal=(c == 7))
                    kb.op("dve", lambda e: e.tensor_copy(
                        out=hT.ap[:, :, blk * 128:(blk + 1) * 128], in_=pT.ap.rearrange("p (c n) -> p c n", n=128)),
                        reads=(pT,), writes=(hT,))
                    fstate["fcnt"] += 1
                fh[fi] = hbufs

            def f_up(fi):
                (t0, n, is_s, a0) = ftiles[fi]
                hT = hnT[fi % 2]; at = aT[0]
                for fc in range(32):
                    ucnt = fstate["ucnt"]
                    pu = up_ps[ucnt % 2]; rb = rl[ucnt % 3]; fstate["ucnt"] += 1
                    for kc in range(8):
                        kb.op("pe", lambda e, kc=kc: e.matmul(
                            pu.ap[:, 0:n], w_up_sb.ap[:, kc, fc * 128:(fc + 1) * 128], hT.ap[:, kc, 0:n],
                            start=(kc == 0), stop=(kc == 7)), reads=(w_up_sb, hT), writes=(pu,), signal=(kc == 7))
                    kb.op("act", lambda e: e.activation(out=rb.ap[:, 0:n], in_=pu.ap[:, 0:n], func=AF.Relu),
                          reads=(pu,), writes=(rb,))
                    kb.op("dve", lambda e: e.tensor_tensor(out=at.ap[:, fc, 0:n], in0=rb.ap[:, 0:n],
                                                           in1=rb.ap[:, 0:n], op=ALU.mult),
                          reads=(rb,), writes=(at,))

            def f_down(fi):
                (t0, n, is_s, a0) = ftiles[fi]
                at = aT[0]
                for blk in range(n // 128):
                    hi, r0 = fh[fi][blk]
                    pd = dn_ps2[blk % 2]
                    for nh in range(2):
                        for fc in range(32):
                            kb.op("pe", lambda e, nh=nh, fc=fc: e.matmul(
                                pd.ap[:, nh * 512:(nh + 1) * 512], at.ap[:, fc, blk * 128:(blk + 1) * 128],
                                w_dn_sb.ap[:, fc, nh * 512:(nh + 1) * 512], start=(fc == 0), stop=(fc == 31)),
                                reads=(at, w_dn_sb), writes=(pd,), signal=(nh == 1 and fc == 31))
                    s4 = s4g[(fi * 2 + blk) % 4]; jk = junk[0]
                    kb.op("dve", lambda e: e.tensor_tensor(out=hi.ap, in0=pd.ap, in1=hi.ap, op=ALU.add),
                          reads=(pd, hi), writes=(hi,))
                    kb.op("act", lambda e: e.activation(out=jk.ap, in_=hi.ap, func=AF.Square, accum_out=s4.ap[:, 0:1]),
                          reads=(hi,), writes=(jk, s4))
                    kb.op("act", lambda e: e.activation(out=s4.ap[:, 1:2], in_=s4.ap[:, 0:1], func=AF.Ln,
                                                        scale=1.0 / D, bias=EPS), reads=(s4,), writes=(s4,))
                    kb.op("act", lambda e: e.activation(out=s4.ap[:, 2:3], in_=s4.ap[:, 1:2], func=AF.Exp,
                                                        scale=-0.5), reads=(s4,), writes=(s4,))
                    kb.op("dve", lambda e: e.scalar_tensor_tensor(
                        out=hi.ap, in0=hi.ap, scalar=s4.ap[:, 2:3], in1=gfin_t.ap, op0=ALU.mult, op1=ALU.mult),
                        reads=(hi, s4, gfin_t), writes=(hi,))
                    dst = y_s[r0 - S:r0 - S + 128, :] if is_s else y_p[r0:r0 + 128, :]
                    kb.store([(dst, hi.ap)], hi)

            f_norm(0)
            for fi in range(len(ftiles)):
                f_up(fi)
                if fi + 1 < len(ftiles):
                    f_norm(fi + 1)
                f_down(fi)
            kb.barrier()

        try:
            plan()
        except _Stop:
            kb.barrier()
        with nc.Block() as block:
            @block.sync
            def _(e):
                for f in kb.q["sp"]:
                    f(e)

            @block.tensor
            def _(e):
                for f in kb.q["pe"]:
                    f(e)

            @block.scalar
            def _(e):
                for f in kb.q["act"]:
                    f(e)

            @block.vector
            def _(e):
                for f in kb.q["dve"]:
                    f(e)

            @block.gpsimd
            def _(e):
                for f in kb.q["pool"]:
                    f(e)
    return nc


_CACHE = {}


def _run(S, per_core_inputs, trace=False):
    if S not in _CACHE:
        _CACHE[S] = build_program(S)
    nc = _CACHE[S]
    return run_bass_kernel_spmd(nc, per_core_inputs, core_ids=list(range(len(per_core_inputs))), trace=trace)


def make_core_inputs(c, S, x_prompt, x_sample, cache_sb_k, cache_sb_v, cache_band_k, cache_band_v,
                     norm_mix_g, w_in, rel_bias, norm_sb_g, norm_band_g, w_out,
                     norm_ffn_g, w_up, w_down, norm_final_g):
    f = lambda a: np.ascontiguousarray(np.asarray(a, dtype=np.float32))
    return {
        "x_p": f(x_prompt[c, :S]),
        "x_s": f(x_sample[2 * c:2 * c + 2]).reshape(TS, D),
        "csk": f(cache_sb_k[0, 2 * c:2 * c + 2]).reshape(2, PAST, W),
        "csv": f(cache_sb_v[0, 2 * c:2 * c + 2]).reshape(2, PAST, W),
        "cbk": f(cache_band_k[0, 2 * c:2 * c + 2]).reshape(2, BROWS, W),
        "cbv": f(cache_band_v[0, 2 * c:2 * c + 2]).reshape(2, BROWS, W),
        "w_in": f(w_in[0]), "w_out": f(w_out[0]), "w_up": f(w_up[0]), "w_down": f(w_down[0]),
        "g_mix": f(norm_mix_g[0]), "g_ffn": f(norm_ffn_g[0]), "g_sb": f(norm_sb_g[0]), "g_bd": f(norm_band_g[0]),
        "g_fin": f(norm_final_g), "relb": f(rel_bias[0]),
    }


def assemble(results, S, nb):
    KEEP = min(512, S)
    g = lambda k: np.stack([np.asarray(r[k], dtype=np.float32) for r in results])
    y_p = g("y_p")
    y_s = g("y_s").reshape(2 * nb, 64, D)
    sbk_p = g("sbk_p").reshape(1, nb, S, 8, 64)
    sbv_p = g("sbv_p").reshape(1, nb, S, 8, 64)
    bdk_p = g("bdk_p").reshape(1, nb, KEEP, 8, 64)
    bdv_p = g("bdv_p").reshape(1, nb, KEEP, 8, 64)
    sbk_s = g("sbk_s").reshape(1, 2 * nb, 64, 8, 64)
    sbv_s = g("sbv_s").reshape(1, 2 * nb, 64, 8, 64)
    bdk_s = g("bdk_s").reshape(1, 2 * nb, 64, 8, 64)
    bdv_s = g("bdv_s").reshape(1, 2 * nb, 64, 8, 64)
    return (y_p, y_s, sbk_p, sbv_p, bdk_p, bdv_p, sbk_s, sbv_s, bdk_s, bdv_s)


def kernel(x_prompt, x_sample, cache_sb_k, cache_sb_v, cache_band_k, cache_band_v,
           norm_mix_g, w_in, rel_bias, norm_sb_g, norm_band_g, w_out,
           norm_ffn_g, w_up, w_down, norm_final_g):
    x_prompt = np.asarray(x_prompt)
    nb, S = x_prompt.shape[0], x_prompt.shape[1]
    args = (x_prompt, np.asarray(x_sample), np.asarray(cache_sb_k), np.asarray(cache_sb_v),
            np.asarray(cache_band_k), np.asarray(cache_band_v), np.asarray(norm_mix_g), np.asarray(w_in),
            np.asarray(rel_bias), np.asarray(norm_sb_g), np.asarray(norm_band_g), np.asarray(w_out),
            np.asarray(norm_ffn_g), np.asarray(w_up), np.asarray(w_down), np.asarray(norm_final_g))
    in_maps = [make_core_inputs(c, S, *args) for c in range(nb)]
    res = _run(S, in_maps)
    return assemble(res.results, S, nb)
```

```python
import contextlib
import numpy as np
import concourse.bass as bass
import concourse.mybir as mybir
from concourse.bass_utils import run_bass_kernel_spmd

F32 = mybir.dt.float32
BF16 = mybir.dt.bfloat16
U8 = mybir.dt.uint8
AF = mybir.ActivationFunctionType
ALU = mybir.AluOpType

D = 1024
W = 512
DFF = 4096
EPS = 1e-6
PAST = 2048
BROWS = 512
TS = 128
LX = 384
SB_TOTAL = 206 * 1024
SHIFT = 20.0
ESHIFT = float(np.exp(20.0))


class Buf:
    __slots__ = ("ap", "w", "r", "lsem", "ssem", "name")

    def __init__(self, ap, name=""):
        self.ap = ap
        self.w = {}
        self.r = {}
        self.lsem = None
        self.ssem = None
        self.name = name


class DSem:
    __slots__ = ("sem", "cnt", "kind")

    def __init__(self, sem):
        self.sem = sem
        self.cnt = 0
        self.kind = "pool"


ENGS = ("pe", "act", "dve", "pool", "sp")


class _Rec:
    def __init__(self):
        self.calls = []

    def __getattr__(self, name):
        def f(*a, **k):
            self.calls.append((name, a, k))
            return self
        return f


class KB:
    def __init__(self, nc, stack):
        self.nc = nc
        self.stack = stack
        self.q = {e: [] for e in ENGS}
        self.esem = {}
        self.ecnt = {}
        self.last = {}
        self.waited = {}
        self.pend = {e: [] for e in ENGS}
        self.nsem = 0
        self.dsems = []
        self.free_dsems = {}
        self.sb_off = 0
        self.sb_mark = 0

    def new_sem(self, name):
        self.nsem += 1
        return self.stack.enter_context(self.nc.semaphore(f"{name}_{self.nsem}"))

    def new_phase(self, name):
        for e in ("pe", "act", "dve", "pool"):
            self.esem[e] = self.new_sem(f"{name}_{e}")
            self.ecnt[e] = 0

    def get_dsem(self, kind="pool"):
        fl = self.free_dsems.setdefault(kind, [])
        if fl:
            return fl.pop()
        d = DSem(self.new_sem("dma" + kind))
        d.kind = kind
        self.dsems.append(d)
        return d

    def _wait(self, eng, tok):
        sem, val, peng = tok
        if peng == "pe" and eng == "pe":
            return
        key = (eng, id(sem))
        if self.waited.get(key, 0) >= val:
            return
        self.waited[key] = val
        self.q[eng].append(lambda e, sem=sem, val=val: e.wait_ge(sem, val))

    @staticmethod
    def _merge(d, tok):
        k = id(tok[0])
        if k not in d or d[k][1] < tok[1]:
            d[k] = tok

    def _deps(self, eng, reads, writes):
        for b in reads:
            for t in b.w.values():
                self._wait(eng, t)
        for b in writes:
            for t in b.w.values():
                self._wait(eng, t)
            for t in b.r.values():
                self._wait(eng, t)

    def _commit(self, tok, reads, writes):
        for b in reads:
            self._merge(b.r, tok)
        for b in writes:
            b.w = {id(tok[0]): tok}
            b.r = {}

    def op(self, eng, fn, reads=(), writes=(), signal=True):
        rec = _Rec()
        fn(rec)
        assert len(rec.calls) == 1
        mname, margs, mkw = rec.calls[0]
        fn = lambda e, mname=mname, margs=margs, mkw=mkw: getattr(e, mname)(*margs, **mkw)
        self._deps(eng, reads, writes)
        if not signal:
            self.q[eng].append(lambda e, fn=fn: fn(e))
            self.pend[eng].append((tuple(reads), tuple(writes)))
            return None
        sem = self.esem[eng]
        self.ecnt[eng] += 1
        tok = (sem, self.ecnt[eng], eng)
        self.q[eng].append(lambda e, fn=fn, sem=sem: fn(e).then_inc(sem, 1))
        for (rs, ws) in self.pend[eng]:
            self._commit(tok, rs, ws)
        self.pend[eng] = []
        self._commit(tok, reads, writes)
        self.last[eng] = tok
        return tok

    def dma(self, qeng, pairs, dsem, reads=(), writes=(), slow=False):
        self._deps(qeng, reads, writes)
        for (o, i) in pairs:
            if slow:
                self.q[qeng].append(
                    lambda e, o=o, i=i, s=dsem.sem: e.dma_start(
                        out=o, in_=i, allow_slow_non_contiguous=True).then_inc(s, 16))
            else:
                self.q[qeng].append(
                    lambda e, o=o, i=i, s=dsem.sem: e.dma_start(out=o, in_=i).then_inc(s, 16))
        dsem.cnt += 16 * len(pairs)
        tok = (dsem.sem, dsem.cnt, "dma")
        self._commit(tok, reads, writes)
        return tok

    def load(self, pairs, dst, reads=()):
        if dst.lsem is None:
            dst.lsem = self.get_dsem("sp")
        return self.dma("sp", pairs, dst.lsem, reads=reads, writes=(dst,))

    def store(self, pairs, src, writes=(), qeng="pool"):
        if src.ssem is None:
            src.ssem = self.get_dsem(qeng)
        return self.dma(qeng, pairs, src.ssem, reads=(src,), writes=writes)

    def barrier(self):
        toks = [self.last[e] for e in ("pe", "act", "dve", "pool") if e in self.last]
        toks += [(d.sem, d.cnt, "dma") for d in self.dsems if d.cnt > 0]
        for e in ENGS:
            assert not self.pend[e], e
            for t in toks:
                if t[2] == e:
                    continue
                sem, val, _ = t
                key = (e, id(sem))
                if self.waited.get(key, 0) >= val:
                    continue
                self.waited[key] = val
                self.q[e].append(lambda en, sem=sem, val=val: en.wait_ge(sem, val))

    def release_dsems(self, bufs):
        for b in bufs:
            for a in ("lsem", "ssem"):
                d = getattr(b, a)
                if d is not None:
                    self.free_dsems.setdefault(d.kind, []).append(d)
                    setattr(b, a, None)


def build_program(S, stop_after=None):
    assert S % 512 == 0
    NT = S + TS
    KEEP = min(512, S)
    NQT = S // 512
    NKB = S // 128

    nc = bass.Bass("TRN2", target_bir_lowering=False)

    def din(name, shape, dt=F32):
        return nc.dram_tensor(name, list(shape), dt, kind="ExternalInput").ap()

    def dout(name, shape, dt=F32):
        return nc.dram_tensor(name, list(shape), dt, kind="ExternalOutput").ap()

    def dscr(name, shape, dt):
        return nc.dram_tensor(name, list(shape), dt, kind="Internal").ap()

    x_p = din("x_p", [S, D])
    x_s = din("x_s", [TS, D])
    csk = din("csk", [2, PAST, W])
    csv = din("csv", [2, PAST, W])
    cbk = din("cbk", [2, BROWS, W])
    cbv = din("cbv", [2, BROWS, W])
    w_in = din("w_in", [D, 3 * D])
    w_out = din("w_out", [D, D])
    w_up = din("w_up", [D, DFF])
    w_down = din("w_down", [DFF, D])
    g_mix = din("g_mix", [D])
    g_ffn = din("g_ffn", [D])
    g_sb = din("g_sb", [W])
    g_bd = din("g_bd", [W])
    g_fin = din("g_fin", [D])
    relb = din("relb", [8, 257])

    y_p = dout("y_p", [S, D])
    y_s = dout("y_s", [TS, D])
    sbk_p = dout("sbk_p", [S, W])
    sbv_p = dout("sbv_p", [S, W])
    bdk_p = dout("bdk_p", [KEEP, W])
    bdv_p = dout("bdv_p", [KEEP, W])
    sbk_s = dout("sbk_s", [TS, W])
    sbv_s = dout("sbv_s", [TS, W])
    bdk_s = dout("bdk_s", [TS, W])
    bdv_s = dout("bdv_s", [TS, W])

    qt_sb = dscr("qt_sb", [4, 128, NT], BF16)
    kt_sb = dscr("kt_sb", [4, 128, NT], BF16)
    qt_bd = dscr("qt_bd", [4, 128, NT], BF16)
    kt_bd = dscr("kt_bd", [4, 128, NT], BF16)
    v_sb = dscr("v_sb", [NT, W], BF16)
    v_bd = dscr("v_bd", [NT, W], BF16)
    ot = dscr("ot", [8, 128, NT], BF16)
    hs = dscr("hs", [NT, D], F32)
    e_all = dscr("e_all", [8, LX], F32)
    xrep = dscr("xrep", [8, 129 * LX], F32)

    stack = contextlib.ExitStack()
    with stack:
        big = stack.enter_context(nc.sbuf_tensor("big", [128, SB_TOTAL], U8))
        psum = stack.enter_context(nc.psum_tensor("psum", [128, 4096], F32))
        kb = KB(nc, stack)

        def sb(nelem, dt, shape=None, name=""):
            size = 4 if dt == F32 else 2
            off = (kb.sb_off + 63) // 64 * 64
            nbytes = nelem * size
            assert off + nbytes <= SB_TOTAL, (name, off, nbytes)
            kb.sb_off = off + nbytes
            ap = big[:, off:off + nbytes].bitcast(dt)
            if shape is not None:
                if len(shape) == 2:
                    ap = ap.rearrange("p (a b) -> p a b", b=shape[1])
                elif len(shape) == 3:
                    ap = ap.rearrange("p (a b c) -> p a b c", b=shape[1], c=shape[2])
            return Buf(ap, name)

        def ring(n, nelem, dt, shape=None, name=""):
            return [sb(nelem, dt, shape, f"{name}{i}") for i in range(n)]

        def bank(b, nb=1):
            return psum[:, b * 512:(b + nb) * 512]

        def pbuf(ap, name=""):
            return Buf(ap, name)

        ident = sb(128, BF16, name="ident")
        tri8 = sb(128, BF16, name="tri8")
        ones8 = sb(128, BF16, name="ones8")
        ones1 = sb(64, BF16, name="ones1")
        mc = sb(128, F32, name="mc")
        mfar = sb(128, F32, name="mfar")
        wn_b = sb(8 * 256, BF16, (8, 256), "wn_b")
        en_b = sb(8 * 128, BF16, (8, 128), "en_b")
        mfar_b = sb(128, BF16, name="mfar_b")
        gfin_t = sb(D, F32, name="gfin")
        gmix_c = sb(8, F32, name="gmixc")
        gffn_c = sb(8, F32, name="gffnc")
        gout_c = sb(8, F32, name="goutc")
        negc = sb(8, F32, name="negc")
        fs3 = sb(512, F32, (2, 256), "fs3")
        fsn = sb(512, F32, (2, 256), "fsn")
        const_end = kb.sb_off
        kb.sb_off = SB_TOTAL - 18 * 1024
        mnear = sb(256, F32, name="mnear")
        wn = sb(8 * 256, F32, (8, 256), "wn")
        en = sb(8 * 256, F32, (8, 256), "en")
        w_tmp_lo = SB_TOTAL - 18 * 1024
        kb.sb_off = const_end

        dram_e = Buf(None, "e_all")
        dram_x = Buf(None, "xrep")
        setup_sem = kb.get_dsem()

        class _Stop(Exception):
            pass

        def plan():
            kb.new_phase("W")
            kb.op("pool", lambda e: e.memset(ident.ap, 0.0), writes=(ident,))
            kb.op("pool", lambda e: e.affine_select(out=ident.ap, in_=ident.ap, pattern=[[-1, 128]],
                                                    compare_op=ALU.not_equal, fill=1.0, base=0,
                                                    channel_multiplier=1), writes=(ident,))
            kb.op("pool", lambda e: e.memset(tri8.ap, -8.0), writes=(tri8,))
            kb.op("pool", lambda e: e.affine_select(out=tri8.ap, in_=tri8.ap, pattern=[[-1, 128]],
                                                    compare_op=ALU.is_ge, fill=0.0, base=0,
                                                    channel_multiplier=1), writes=(tri8,))
            kb.op("pool", lambda e: e.memset(ones8.ap, -8.0), writes=(ones8,))
            kb.op("pool", lambda e: e.memset(ones1.ap, 1.0), writes=(ones1,))
            kb.op("pool", lambda e: e.memset(mc.ap, 1.0), writes=(mc,))
            kb.op("pool", lambda e: e.affine_select(out=mc.ap, in_=mc.ap, pattern=[[1, 128]],
                                                    compare_op=ALU.is_gt, fill=0.0, base=0,
                                                    channel_multiplier=-1), writes=(mc,))
            kb.op("pool", lambda e: e.memset(mfar.ap, 1.0), writes=(mfar,))
            kb.op("pool", lambda e: e.memset(mfar.ap[0:64, 64:128], 0.0), writes=(mfar,))
            kb.op("pool", lambda e: e.memset(mnear.ap, 1.0), writes=(mnear,))
            kb.op("pool", lambda e: e.memset(mnear.ap[64:128, 0:64], 0.0), writes=(mnear,))

            setup_sems = []

            def bc_load(dst, src_ap):
                ds = kb.get_dsem(); setup_sems.append(ds)
                kb.q["pool"].append(lambda e, o=dst.ap, i=src_ap, s=ds.sem:
                                    e.dma_start(out=o, in_=i, allow_slow_non_contiguous=True).then_inc(s, 16))
                ds.cnt += 16
                tok = (ds.sem, ds.cnt, "dma")
                dst.w = {id(tok[0]): tok}

            bc_load(gfin_t, g_fin.rearrange("(o n) -> o n", o=1).broadcast_to([128, D]))
            def col3(b):
                return Buf(b.ap.rearrange("p (c o) -> p c o", o=1))
            gm3 = col3(gmix_c); gf3 = col3(gffn_c)
            bc_load(gm3, g_mix.rearrange("(c p o) -> p c o", p=128, o=1))
            bc_load(gf3, g_ffn.rearrange("(c p o) -> p c o", p=128, o=1))
            gmix_c.w = dict(gm3.w); gffn_c.w = dict(gf3.w)
            gout_c_a = Buf(gout_c.ap[:, 0:4].rearrange("p (c o) -> p c o", o=1))
            gout_c_b = Buf(gout_c.ap[:, 4:8].rearrange("p (c o) -> p c o", o=1))
            bc_load(gout_c_a, g_sb.rearrange("(c p o) -> p c o", p=128, o=1))
            bc_load(gout_c_b, g_bd.rearrange("(c p o) -> p c o", p=128, o=1))
            gout_c.w = dict(gout_c_a.w); gout_c.w.update(gout_c_b.w)
            ng3 = col3(negc)
            bc_load(ng3, relb[:, 256:257].rearrange("(x h) o -> x h o", x=1).broadcast_to([128, 8, 1]))
            negc.w = dict(ng3.w)
            kb.op("dve", lambda e: e.tensor_scalar(out=negc.ap, in0=negc.ap, scalar1=-1.0, scalar2=None,
                                                   op0=ALU.mult), reads=(negc,), writes=(negc,))
            kb.dma("pool", [(e_all[:, 0:129], relb[:, 128:257]),
                          (e_all[:, 129:257].rearrange("h (n o) -> h n o", o=1),
                           relb[:, 256:257].rearrange("h (n o) -> h n o", o=1).broadcast_to([8, 128, 1])),
                          (e_all[:, 257:384], relb[:, 1:128])], setup_sem, writes=(dram_e,), slow=True)
            setup_sem2 = kb.get_dsem(); setup_sem3 = kb.get_dsem()
            kb.dma("pool", [(xrep.rearrange("h (r l) -> h r l", l=LX),
                           e_all.rearrange("h (o l) -> h o l", o=1).broadcast_to([8, 129, LX]))],
                   setup_sem2, reads=(dram_e,), writes=(dram_x,), slow=True)
            en_src = bass.AP(xrep.tensor, 0, [[LX - 1, 128], [129 * LX, 8], [1, 256]])
            kb.dma("pool", [(en.ap, en_src)], setup_sem3, reads=(dram_x,), writes=(en,), slow=True)
            for h in range(8):
                kb.op("act", lambda e, h=h: e.activation(out=en.ap[:, h, :], in_=en.ap[:, h, :], func=AF.Exp,
                                                         bias=negc.ap[:, h:h + 1], scale=1.0),
                      reads=(en, negc), writes=(en,))
            for h in range(8):
                kb.op("dve", lambda e, h=h: e.tensor_tensor(out=wn.ap[:, h, :], in0=en.ap[:, h, :],
                                                            in1=mnear.ap, op=ALU.mult),
                      reads=(en, mnear), writes=(wn,))
            kb.op("dve", lambda e: e.tensor_copy(out=wn_b.ap, in_=wn.ap), reads=(wn,), writes=(wn_b,))
            kb.op("dve", lambda e: e.tensor_copy(out=en_b.ap, in_=en.ap[:, :, 128:256]), reads=(en,), writes=(en_b,))
            kb.op("dve", lambda e: e.tensor_copy(out=mfar_b.ap, in_=mfar.ap), reads=(mfar,), writes=(mfar_b,))
            kb.op("pool", lambda e: e.memset(fsn.ap, 0.0), writes=(fsn,))
            for h in range(8):
                b_, hh = h % 2, h // 2
                kb.op("dve", lambda e, h=h, b_=b_, hh=hh: e.tensor_copy(
                    out=fs3.ap[:, b_, hh * 64:(hh + 1) * 64], in_=en.ap[:, h, 128:192]),
                    reads=(en,), writes=(fs3,))
                kb.op("dve", lambda e, h=h, b_=b_, hh=hh: e.tensor_copy(
                    out=fsn.ap[0:64, b_, hh * 64:(hh + 1) * 64], in_=en.ap[0:64, h, 0:64]),
                    reads=(en,), writes=(fsn,))

            if stop_after == "W":
                raise _Stop()
            kb.sb_off = const_end
            w_in_sb = sb(8 * 3072, BF16, (8, 3072), "w_in_sb")
            wst = ring(2, 3072, F32, name="wst")
            for c in range(8):
                st = wst[c % 2]
                kb.load([(st.ap, w_in[c * 128:(c + 1) * 128, :])], st)
                kb.op("dve", lambda e, c=c, st=st: e.tensor_scalar(
                    out=w_in_sb.ap[:, c, :], in0=st.ap, scalar1=gmix_c.ap[:, c:c + 1], scalar2=None,
                    op0=ALU.mult), reads=(st, gmix_c), writes=(w_in_sb,))
            p_mark = kb.sb_off
            xin = ring(4, D, F32, name="xin")
            xn = ring(2, D, BF16, name="xn")
            ssb = ring(4, 4, F32, name="ss")
            xnT = ring(2, 8 * 512, BF16, (8, 512), "xnT")
            fmst = ring(2, 16 * 512, BF16, (16, 512), "fmst")
            tmst = ring(4, 512, F32, name="tmst")
            vst = ring(4, 512, BF16, name="vst")
            assert kb.sb_off <= w_tmp_lo, kb.sb_off
            ps_fm = [pbuf(bank(0)), pbuf(bank(1)), pbuf(bank(2))]
            ps_tm = [pbuf(bank(3)), pbuf(bank(4)), pbuf(bank(5))]
            ps_T = [pbuf(bank(6).bitcast(BF16)), pbuf(bank(7).bitcast(BF16))]
            dram_q = {}

            def dbuf(key):
                if key not in dram_q:
                    dram_q[key] = Buf(None, str(key))
                return dram_q[key]

            tiles = [(i * 512, 512, False) for i in range(NQT)] + [(S, TS, True)]
            cnt = {"blk": 0, "fm": 0, "tm": 0, "tile": 0, "ev": 0, "pt": 0}

            def rms_block(src_rows, xi, xo, s4):
                kb.load([(xi.ap, src_rows)], xi)
                kb.op("act", lambda e: e.activation(out=xo.ap, in_=xi.ap, func=AF.Square,
                                                    accum_out=s4.ap[:, 0:1]),
                      reads=(xi,), writes=(xo, s4))
                kb.op("act", lambda e: e.activation(out=s4.ap[:, 1:2], in_=s4.ap[:, 0:1], func=AF.Ln,
                                                    scale=1.0 / D, bias=EPS), reads=(s4,), writes=(s4,))
                kb.op("act", lambda e: e.activation(out=s4.ap[:, 2:3], in_=s4.ap[:, 1:2], func=AF.Exp,
                                                    scale=-0.5), reads=(s4,), writes=(s4,))
                kb.op("dve", lambda e: e.tensor_scalar(out=xo.ap, in0=xi.ap, scalar1=s4.ap[:, 2:3],
                                                       scalar2=None, op0=ALU.mult),
                      reads=(xi, s4), writes=(xo,))

            def transpose_block(xo, dstT, blk, evac_eng):
                pT = ps_T[cnt["pt"] % 2]; cnt["pt"] += 1
                for c in range(8):
                    kb.op("pe", lambda e, c=c, pT=pT: e.transpose(pT.ap[:, c * 128:(c + 1) * 128],
                                                                 xo.ap[:, c * 128:(c + 1) * 128], ident.ap),
                          reads=(xo, ident), writes=(pT,), signal=(c == 7))
                src = pT.ap.rearrange("p (c n) -> p c n", n=128)
                dst = dstT.ap[:, :, blk * 128:(blk + 1) * 128]
                if evac_eng == "act":
                    kb.op("act", lambda e: e.activation(out=dst, in_=src, func=AF.Copy),
                          reads=(pT,), writes=(dstT,))
                else:
                    kb.op("dve", lambda e: e.tensor_copy(out=dst, in_=src), reads=(pT,), writes=(dstT,))

            def p_norm(ti):
                (t0, n, is_s) = tiles[ti]
                xT = xnT[ti % 2]
                for blk in range(n // 128):
                    bi = cnt["blk"]
                    xi = xin[bi % 4]; xo = xn[bi % 2]; s4 = ssb[bi % 4]
                    rows = x_s[blk * 128:(blk + 1) * 128, :] if is_s else x_p[t0 + blk * 128:t0 + (blk + 1) * 128, :]
                    rms_block(rows, xi, xo, s4)
                    transpose_block(xo, xT, blk, "dve")
                    cnt["blk"] += 1

            def p_mm(ti):
                (t0, n, is_s) = tiles[ti]
                nb = n // 128
                xT = xnT[ti % 2]
                fst = fmst[ti % 2]
                fm_cols = [0 * 512, 1 * 512, 3 * 512, 4 * 512]
                for g4 in (0, 2, 3):
                    for c4 in range(4):
                        oc = g4 * 4 + c4
                        col0 = fm_cols[g4] + c4 * 128
                        pf = ps_fm[cnt["fm"] % 3]; cnt["fm"] += 1
                        for kc in range(8):
                            kb.op("pe", lambda e, kc=kc, pf=pf, col0=col0: e.matmul(
                                pf.ap[:, 0:n], w_in_sb.ap[:, kc, col0:col0 + 128], xT.ap[:, kc, 0:n],
                                start=(kc == 0), stop=(kc == 7)),
                                reads=(w_in_sb, xT), writes=(pf,), signal=(kc == 7))
                        if cnt["ev"] % 3 != 2:
                            kb.op("act", lambda e, pf=pf, oc=oc: e.activation(out=fst.ap[:, oc, 0:n], in_=pf.ap[:, 0:n],
                                                                              func=AF.Copy),
                                  reads=(pf,), writes=(fst,))
                        else:
                            kb.op("dve", lambda e, pf=pf, oc=oc: e.tensor_copy(out=fst.ap[:, oc, 0:n], in_=pf.ap[:, 0:n]),
                                  reads=(pf,), writes=(fst,))
                        cnt["ev"] += 1
                need_bd = is_s or (t0 + n > S - KEEP)
                for blk in range(nb):
                    r0 = t0 + blk * 128
                    groups = [("k_sb", 512), ("v_sb", 1024), ("v_bd", 2560)]
                    if need_bd:
                        groups.append(("k_bd", 2048))
                    for (gname, gcol) in groups:
                        pt = ps_tm[cnt["tm"] % 3]
                        for kc in range(8):
                            kb.op("pe", lambda e, kc=kc, pt=pt, gcol=gcol, blk=blk: e.matmul(
                                pt.ap, xT.ap[:, kc, blk * 128:(blk + 1) * 128], w_in_sb.ap[:, kc, gcol:gcol + 512],
                                start=(kc == 0), stop=(kc == 7)),
                                reads=(w_in_sb, xT), writes=(pt,), signal=(kc == 7))
                        ts_ = tmst[cnt["tm"] % 4]
                        vs_ = vst[cnt["tm"] % 4]
                        cnt["tm"] += 1
                        kb.op("dve", lambda e, pt=pt, ts_=ts_: e.tensor_copy(out=ts_.ap, in_=pt.ap),
                              reads=(pt,), writes=(ts_,))
                        outs = []
                        if gname == "k_sb":
                            outs.append(sbk_s[blk * 128:(blk + 1) * 128, :] if is_s else sbk_p[r0:r0 + 128, :])
                        elif gname == "v_sb":
                            outs.append(sbv_s[blk * 128:(blk + 1) * 128, :] if is_s else sbv_p[r0:r0 + 128, :])
                        elif gname == "k_bd":
                            outs.append(bdk_s[blk * 128:(blk + 1) * 128, :] if is_s
                                        else bdk_p[r0 - (S - KEEP):r0 - (S - KEEP) + 128, :])
                        elif gname == "v_bd" and need_bd:
                            outs.append(bdv_s[blk * 128:(blk + 1) * 128, :] if is_s
                                        else bdv_p[r0 - (S - KEEP):r0 - (S - KEEP) + 128, :])
                        if outs:
                            kb.store([(o, ts_.ap) for o in outs], ts_)
                        if gname == "k_sb":
                            kbf = vs_
                            kb.op("act", lambda e, ts_=ts_, kbf=kbf: e.activation(out=kbf.ap, in_=ts_.ap, func=AF.Copy),
                                  reads=(ts_,), writes=(kbf,))
                            kdefer = kbf
                        if gname in ("v_sb", "v_bd"):
                            kb.op("act", lambda e, ts_=ts_, vs_=vs_: e.activation(out=vs_.ap, in_=ts_.ap, func=AF.Copy),
                                  reads=(ts_,), writes=(vs_,))
                            dstv = v_sb if gname == "v_sb" else v_bd
                            kb.store([(dstv[r0:r0 + 128, :], vs_.ap)], vs_, writes=(dbuf((gname, r0 // 128)),))
                    pT = ps_T[cnt["pt"] % 2]; cnt["pt"] += 1
                    for c4 in range(4):
                        kb.op("pe", lambda e, c4=c4, pT=pT, kdefer=kdefer: e.transpose(
                            pT.ap[:, c4 * 128:(c4 + 1) * 128], kdefer.ap[:, c4 * 128:(c4 + 1) * 128], ident.ap),
                            reads=(kdefer, ident), writes=(pT,), signal=(c4 == 3))
                    kb.op("dve", lambda e, pT=pT, blk=blk: e.tensor_copy(
                        out=fst.ap[:, 4:8, blk * 128:(blk + 1) * 128],
                        in_=pT.ap[:, 0:512].rearrange("p (c n) -> p c n", n=128)),
                        reads=(pT,), writes=(fst,))
                pairs = []
                for g4, dst in enumerate((qt_sb, kt_sb, qt_bd, kt_bd)):
                    pairs.append((dst[:, :, t0:t0 + n].rearrange("h p n -> p h n"), fst.ap[:, g4 * 4:(g4 + 1) * 4, 0:n]))
                kb.store(pairs, fst, writes=(dbuf(("fm", ti)),))
            p_norm(0)
            for ti in range(len(tiles)):
                if ti + 1 < len(tiles):
                    p_norm(ti + 1)
                p_mm(ti)
            kb.barrier()
            kb.release_dsems(wst + xin + fmst + tmst + vst)

            if stop_after == "P":
                raise _Stop()
            kb.new_phase("SB")
            kb.sb_off = const_end
            qkv = []
            for i in range(2):
                qkv.append((sb(S, BF16, name=f"QT{i}"), sb(S, BF16, name=f"KT{i}"),
                            sb(NKB * 128, BF16, (NKB, 128), f"V{i}")))
            l_r = ring(3, 1024, BF16, (2, 512), "L")
            w_r = ring(3, 1024, BF16, (2, 512), "w")
            ra_r = ring(3, 1024, BF16, (2, 512), "ra")
            ost = ring(2, 512, BF16, name="ost")
            zc_ps = [pbuf(bank(2 * i_, 2).rearrange("p (b n) -> p b n", n=512)) for i_ in range(3)]
            o_ps = [pbuf(bank(6)), pbuf(bank(7))]
            dram_ot = {}

            def load_qkv(hp, slot, qsrc, ksrc, vsrc, vkey):
                QT, KT, V = qkv[slot]
                rd = [dbuf(("fm", t)) for t in range(NQT)]
                kb.load([(QT.ap, qsrc[hp, :, 0:S])], QT, reads=rd)
                kb.load([(KT.ap, ksrc[hp, :, 0:S])], KT, reads=rd)
                rdv = [dbuf((vkey, b)) for b in range(NKB)]
                pairs = []
                for b0 in range(0, NKB, 16):
                    b1 = min(NKB, b0 + 16)
                    pairs.append((V.ap[:, b0:b1, :],
                                  vsrc[b0 * 128:b1 * 128, hp * 128:(hp + 1) * 128].rearrange("(b p) f -> p b f", p=128)))
                kb.load(pairs, V, reads=rdv)

            its = []
            for hp in range(4):
                for i in range(NQT):
                    js = list(range(4 * i + 3, -1, -1))
                    for n_, j in enumerate(js):
                        m = j - 4 * i
                        c0 = 128 * m if m > 0 else 0
                        its.append(dict(hp=hp, i=i, j=j, c0=c0, diag=(m >= 0), first=(n_ == 0),
                                        last=(n_ == len(js) - 1), slot=hp % 2, qt=hp * NQT + i))
            NIT = len(its)
            load_qkv(0, 0, qt_sb, kt_sb, v_sb, "v_sb")
            loaded = {0}

            def st_qk(k):
                it = its[k]
                QT, KT, V = qkv[it["slot"]]
                z = zc_ps[k % 3]; c0 = it["c0"]; i = it["i"]; j = it["j"]
                for b in range(2):
                    kb.op("pe", lambda e, b=b: e.matmul(
                        z.ap[:, b, c0:512], KT.ap[b * 64:(b + 1) * 64, j * 128:(j + 1) * 128],
                        QT.ap[b * 64:(b + 1) * 64, i * 512 + c0:(i + 1) * 512], start=True, stop=True),
                        reads=(QT, KT), writes=(z,), signal=(b == 1))

            def st_l(k):
                it = its[k]; z = zc_ps[k % 3]; lb = l_r[k % 3]; c0 = it["c0"]
                if c0 > 0:
                    kb.op("pool", lambda e: e.memset(lb.ap[:, :, 0:c0], 0.0), writes=(lb,))
                kb.op("act", lambda e: e.activation(out=lb.ap[:, :, c0:512], in_=z.ap[:, :, c0:512],
                                                    func=AF.Softplus, scale=0.125), reads=(z,), writes=(lb,))
                if it["diag"]:
                    for b in range(2):
                        kb.op("dve", lambda e, b=b: e.tensor_tensor(out=lb.ap[:, b, c0:c0 + 128],
                                                                    in0=lb.ap[:, b, c0:c0 + 128], in1=mc.ap,
                                                                    op=ALU.mult), reads=(lb, mc), writes=(lb,))

            def st_ra(k):
                it = its[k]
                if it["last"]:
                    return
                lb = l_r[k % 3]; rn = ra_r[(k + 1) % 3]; rc = ra_r[k % 3]
                if it["first"]:
                    kb.op("dve", lambda e: e.tensor_copy(out=rn.ap, in_=lb.ap), reads=(lb,), writes=(rn,))
                else:
                    kb.op("dve", lambda e: e.tensor_tensor(out=rn.ap, in0=rc.ap, in1=lb.ap, op=ALU.add),
                          reads=(rc, lb), writes=(rn,))

            def st_c(k):
                it = its[k]; lb = l_r[k % 3]; rc = ra_r[k % 3]; cp = zc_ps[k % 3]
                for b in range(2):
                    kb.op("pe", lambda e, b=b: e.matmul(cp.ap[:, b, :], tri8.ap, lb.ap[:, b, :], start=False,
                                                        stop=it["first"], skip_group_check=True),
                          reads=(tri8, lb), writes=(cp,), signal=(it["first"] and b == 1))
                    if not it["first"]:
                        kb.op("pe", lambda e, b=b: e.matmul(cp.ap[:, b, :], ones8.ap, rc.ap[:, b, :], start=False,
                                                            stop=True, skip_group_check=True),
                              reads=(ones8, rc), writes=(cp,), signal=(b == 1))

            def st_w(k):
                it = its[k]; cp = zc_ps[k % 3]; wb = w_r[k % 3]; c0 = it["c0"]
                if c0 > 0:
                    kb.op("pool", lambda e: e.memset(wb.ap[:, :, 0:c0], 0.0), writes=(wb,))
                kb.op("act", lambda e: e.activation(out=wb.ap[:, :, c0:512], in_=cp.ap[:, :, c0:512],
                                                    func=AF.Softplus, scale=0.125, bias=-SHIFT),
                      reads=(cp,), writes=(wb,))
                if it["diag"]:
                    for b in range(2):
                        kb.op("dve", lambda e, b=b: e.tensor_tensor(out=wb.ap[:, b, c0:c0 + 128],
                                                                    in0=wb.ap[:, b, c0:c0 + 128], in1=mc.ap,
                                                                    op=ALU.mult), reads=(wb, mc), writes=(wb,))

            def st_pv(k):
                it = its[k]; wb = w_r[k % 3]; QT, KT, V = qkv[it["slot"]]
                op_ = o_ps[it["qt"] % 2]; j = it["j"]
                for b in range(2):
                    kb.op("pe", lambda e, b=b: e.matmul(op_.ap[b * 64:(b + 1) * 64, :], V.ap[:, j, b * 64:(b + 1) * 64],
                                                        wb.ap[:, b, :], start=it["first"], stop=it["last"]),
                          reads=(V, wb), writes=(op_,), signal=(b == 1))
                if it["last"]:
                    os_ = ost[it["qt"] % 2]
                    kb.op("dve", lambda e: e.tensor_scalar(out=os_.ap, in0=op_.ap, scalar1=ESHIFT, scalar2=None,
                                                           op0=ALU.mult), reads=(op_,), writes=(os_,))
                    t0 = it["i"] * 512
                    d_ = Buf(None); dram_ot[(it["hp"], it["i"])] = d_
                    kb.store([(ot[it["hp"], :, t0:t0 + 512], os_.ap)], os_, writes=(d_,))

            st_qk(0)
            for r in range(NIT + 3):
                if 0 <= r - 2 < NIT:
                    it = its[r - 2]
                    if it["i"] == 0 and it["first"] and it["hp"] + 1 < 4 and (it["hp"] + 1) not in loaded:
                        load_qkv(it["hp"] + 1, (it["hp"] + 1) % 2, qt_sb, kt_sb, v_sb, "v_sb")
                        loaded.add(it["hp"] + 1)
                    if it["i"] == 0 and it["first"] and it["hp"] == 3 and "bd0" not in loaded:
                        load_qkv(0, 0, qt_bd, kt_bd, v_bd, "v_bd")
                        loaded.add("bd0")
                if r + 1 < NIT:
                    st_qk(r + 1)
                if r < NIT:
                    st_l(r)
                if 0 <= r - 1 < NIT:
                    st_w(r - 1)
                if r < NIT:
                    st_ra(r)
                    st_c(r)
                if 0 <= r - 2 < NIT:
                    st_pv(r - 2)
            kb.barrier()

            if stop_after == "SB":
                raise _Stop()
            kb.new_phase("BD")
            wb_r = ring(3, 1024, BF16, (2, 512), "wb")
            rd_r = ring(2, 512, F32, name="rden")
            obs_r = ring(2, 512, F32, name="obs")
            zb_ps = [pbuf(bank(2 * i_, 2).rearrange("p (b n) -> p b n", n=512)) for i_ in range(3)]
            ob_ps = [pbuf(bank(6))]
            dn_ps = [pbuf(bank(7))]
            if "bd0" not in loaded:
                load_qkv(0, 0, qt_bd, kt_bd, v_bd, "v_bd")
            brecs = []
            for hp in range(4):
                for i in range(NQT):
                    blocks = [(4 * i + m, 128 * m, 512, "near", m) for m in range(4)]
                    if i > 0:
                        blocks += [(4 * i - 4 + jj, 0, 128 * (jj + 1), "far", jj) for jj in range(4)]
                    for n_, (j, a, b_, kind, m) in enumerate(blocks):
                        brecs.append(dict(hp=hp, i=i, j=j, a=a, b_=b_, kind=kind, m=m, first=(n_ == 0),
                                          last=(n_ == len(blocks) - 1), qi=hp * NQT + i))
            NBR = len(brecs)

            def bd_qk(n):
                rc = brecs[n]; QT, KT, V = qkv[rc["hp"] % 2]; z = zb_ps[n % 3]
                j, a, b_, i = rc["j"], rc["a"], rc["b_"], rc["i"]
                for b in range(2):
                    kb.op("pe", lambda e, b=b: e.matmul(
                        z.ap[:, b, a:b_], KT.ap[b * 64:(b + 1) * 64, j * 128:(j + 1) * 128],
                        QT.ap[b * 64:(b + 1) * 64, i * 512 + a:i * 512 + b_], start=True, stop=True),
                        reads=(QT, KT), writes=(z,), signal=(b == 1))

            wbh = [[Buf(w_.ap[:, b]) for b in range(2)] for w_ in wb_r]

            def bd_exp(n):
                rc = brecs[n]; z = zb_ps[n % 3]; wb = wb_r[n % 3]; wh = wbh[n % 3]
                a, b_, kind, m, hp = rc["a"], rc["b_"], rc["kind"], rc["m"], rc["hp"]
                kb.op("act", lambda e: e.activation(out=wb.ap[:, :, a:b_], in_=z.ap[:, :, a:b_], func=AF.Exp,
                                                    scale=0.125), reads=(z,), writes=(wh[0], wh[1]))
                for b in range(2):
                    h = 2 * hp + b
                    if kind == "near":
                        wd = min(256, 512 - a)
                        kb.op("dve", lambda e, b=b, wd=wd, h=h: e.tensor_tensor(
                            out=wb.ap[:, b, a:a + wd], in0=wb.ap[:, b, a:a + wd], in1=wn_b.ap[:, h, 0:wd],
                            op=ALU.mult), reads=(wh[b], wn_b), writes=(wh[b],))
                    else:
                        kb.op("dve", lambda e, b=b: e.tensor_tensor(
                            out=wb.ap[:, b, b_ - 128:b_], in0=wb.ap[:, b, b_ - 128:b_], in1=mfar_b.ap,
                            op=ALU.mult), reads=(wh[b], mfar_b), writes=(wh[b],))
                        if m == 3:
                            kb.op("dve", lambda e, b=b, h=h: e.tensor_tensor(
                                out=wb.ap[:, b, 0:128], in0=wb.ap[:, b, 0:128], in1=en_b.ap[:, h, :],
                                op=ALU.mult), reads=(wh[b], en_b), writes=(wh[b],))

            def bd_pv(n):
                rc = brecs[n]; QT, KT, V = qkv[rc["hp"] % 2]; wb = wb_r[n % 3]; wh = wbh[n % 3]
                j, a, b_, qi = rc["j"], rc["a"], rc["b_"], rc["qi"]
                op_ = ob_ps[0]; dn = dn_ps[0]
                for b in range(2):
                    kb.op("pe", lambda e, b=b: e.matmul(
                        op_.ap[b * 64:(b + 1) * 64, a:b_], V.ap[:, j, b * 64:(b + 1) * 64], wb.ap[:, b, a:b_],
                        start=rc["first"], stop=rc["last"]), reads=(V, wh[b]), writes=(op_,), signal=False)
                for b in range(2):
                    kb.op("pe", lambda e, b=b: e.matmul(
                        dn.ap[b * 64:(b + 1) * 64, a:b_], ones1.ap, wb.ap[:, b, a:b_],
                        start=rc["first"], stop=rc["last"]), reads=(ones1, wh[b]), writes=(dn,), signal=(b == 1))
                if rc["last"]:
                    rd = rd_r[qi % 2]; os_ = ost[qi % 2]
                    ob_sb = obs_r[qi % 2]
                    kb.op("act", lambda e: e.activation(out=rd.ap, in_=dn.ap, func=AF.Ln), reads=(dn,), writes=(rd,))
                    kb.op("dve", lambda e: e.tensor_copy(out=ob_sb.ap, in_=op_.ap), reads=(op_,), writes=(ob_sb,))
                    kb.op("act", lambda e: e.activation(out=rd.ap, in_=rd.ap, func=AF.Exp, scale=-1.0),
                          reads=(rd,), writes=(rd,))
                    kb.op("dve", lambda e: e.tensor_tensor(out=os_.ap, in0=ob_sb.ap, in1=rd.ap, op=ALU.mult),
                          reads=(ob_sb, rd), writes=(os_,))
                    d_ = Buf(None); dram_ot[(4 + rc["hp"], rc["i"])] = d_
                    kb.store([(ot[4 + rc["hp"], :, rc["i"] * 512:(rc["i"] + 1) * 512], os_.ap)], os_, writes=(d_,))

            bloaded = {0}
            bd_qk(0)
            bd_qk(1)
            for r in range(NBR + 1):
                if 0 <= r - 1 < NBR:
                    rc = brecs[r - 1]
                    if rc["i"] == 0 and rc["first"] and rc["hp"] + 1 < 4 and (rc["hp"] + 1) not in bloaded:
                        load_qkv(rc["hp"] + 1, (rc["hp"] + 1) % 2, qt_bd, kt_bd, v_bd, "v_bd")
                        bloaded.add(rc["hp"] + 1)
                if r + 2 < NBR:
                    bd_qk(r + 2)
                if r < NBR:
                    bd_exp(r)
                if 0 <= r - 1 < NBR:
                    bd_pv(r - 1)
            kb.barrier()
            kb.release_dsems([b for t in qkv for b in t] + ost)

            if stop_after == "BD":
                raise _Stop()
            kb.new_phase("SA")
            kb.sb_off = const_end
            ktc = sb(2 * 4 * PAST, BF16, (2, 4, PAST), "ktc")
            vc = sb(2 * 16 * W, BF16, (2, 16, W), "vc")
            ktb = sb(2 * 4 * BROWS, BF16, (2, 4, BROWS), "ktb")
            vbc = sb(2 * 4 * W, BF16, (2, 4, W), "vbc")
            cst = ring(2, 4 * W, F32, (4, W), "cst")
            cbf = ring(2, 4 * W, BF16, (4, W), "cbf")
            qs_sb = sb(4 * TS, BF16, (4, TS), "qs_sb")
            ks_sb = sb(4 * 2 * 128, BF16, (4, 2, 128), "ks_sb")
            qs_bd = sb(4 * TS, BF16, (4, TS), "qs_bd")
            ks_bd = sb(4 * 2 * 128, BF16, (4, 2, 128), "ks_bd")
            vs_sb = sb(2 * W, BF16, (2, W), "vs_sb")
            vs_bd = sb(2 * W, BF16, (2, W), "vs_bd")
            sl_r = ring(3, 512, BF16, (2, 256), "sl")
            sw_r = ring(3, 512, BF16, (2, 256), "sw")
            sra_r = ring(3, 512, BF16, (2, 256), "sra")
            sos = ring(2, 256, BF16, (4, 64), "sos")
            srd = ring(2, 256, F32, name="srd")
            fm_s = [dbuf(("fm", NQT))]
            kb.op("pool", lambda e: e.memset(ks_sb.ap, 0.0), writes=(ks_sb,))
            kb.op("pool", lambda e: e.memset(ks_bd.ap, 0.0), writes=(ks_bd,))
            kb.op("pool", lambda e: e.memset(vs_sb.ap, 0.0), writes=(vs_sb,))
            kb.op("pool", lambda e: e.memset(vs_bd.ap, 0.0), writes=(vs_bd,))
            kb.load([(qs_sb.ap, qt_sb[:, :, S:S + TS].rearrange("h p n -> p h n"))], qs_sb, reads=fm_s)
            kb.load([(qs_bd.ap, qt_bd[:, :, S:S + TS].rearrange("h p n -> p h n"))], qs_bd, reads=fm_s)
            kb.load([(ks_sb.ap[:, :, s, 0:64], kt_sb[:, :, S + s * 64:S + (s + 1) * 64].rearrange("h p n -> p h n"))
                     for s in range(2)], ks_sb, reads=fm_s)
            kb.load([(ks_bd.ap[:, :, s, 0:64], kt_bd[:, :, S + s * 64:S + (s + 1) * 64].rearrange("h p n -> p h n"))
                     for s in range(2)], ks_bd, reads=fm_s)
            kb.load([(vs_sb.ap[0:64, s, :], v_sb[S + s * 64:S + (s + 1) * 64, :]) for s in range(2)], vs_sb,
                    reads=[dbuf(("v_sb", S // 128))])
            kb.load([(vs_bd.ap[0:64, s, :], v_bd[S + s * 64:S + (s + 1) * 64, :]) for s in range(2)], vs_bd,
                    reads=[dbuf(("v_bd", S // 128))])
            sT = [pbuf(bank(7).bitcast(BF16))]
            ccnt = 0
            tcnt = 0
            for (ksrc, vsrc, nblk, kdst, vdst) in ((csk, csv, 16, ktc, vc), (cbk, cbv, 4, ktb, vbc)):
                for s in range(2):
                    for g in range(nblk // 4):
                        st = cst[ccnt % 2]; cb = cbf[ccnt % 2]; ccnt += 1
                        kb.load([(st.ap, ksrc[s, g * 512:(g + 1) * 512, :].rearrange("(b p) f -> p b f", p=128))], st)
                        kb.op("dve", lambda e, st=st, cb=cb: e.tensor_copy(out=cb.ap, in_=st.ap), reads=(st,), writes=(cb,))
                        for hp in range(4):
                            pT = sT[0]; tcnt += 1
                            for b4 in range(4):
                                kb.op("pe", lambda e, pT=pT, cb=cb, b4=b4, hp=hp: e.transpose(
                                    pT.ap[:, b4 * 128:(b4 + 1) * 128], cb.ap[:, b4, hp * 128:(hp + 1) * 128], ident.ap),
                                    reads=(cb, ident), writes=(pT,), signal=(b4 == 3))
                            kb.op("act", lambda e, pT=pT, s=s, hp=hp, g=g, kdst=kdst: e.activation(
                                out=kdst.ap[:, s, hp, g * 512:(g + 1) * 512], in_=pT.ap[:, 0:512], func=AF.Copy),
                                reads=(pT,), writes=(kdst,))
                        st2 = cst[ccnt % 2]; ccnt += 1
                        kb.load([(st2.ap, vsrc[s, g * 512:(g + 1) * 512, :].rearrange("(b p) f -> p b f", p=128))], st2)
                        kb.op("dve", lambda e, st2=st2, s=s, g=g, vdst=vdst: e.tensor_copy(
                            out=vdst.ap[:, s, g * 4:(g + 1) * 4, :], in_=st2.ap), reads=(st2,), writes=(vdst,))

            zs_ps = [pbuf(bank(2 * i_, 2).rearrange("p (b n) -> p b n", n=512)) for i_ in range(3)]
            os_ps = [pbuf(bank(6))]
            dns_ps = [pbuf(bank(6)[:, 256:512])]
            dram_ots = Buf(None)
            sits = []
            for s in range(2):
                for n_, j in enumerate([16] + list(range(15, -1, -1))):
                    sits.append(dict(s=s, j=j, first=(n_ == 0), last=(n_ == 16)))
            NS = len(sits)

            def kt_blk(it, b, hh):
                if it["j"] == 16:
                    return ks_sb.ap[b * 64:(b + 1) * 64, hh, it["s"], :]
                return ktc.ap[b * 64:(b + 1) * 64, it["s"], hh, it["j"] * 128:(it["j"] + 1) * 128]

            def v_blk(it, h):
                if it["j"] == 16:
                    return vs_sb.ap[:, it["s"], h * 64:(h + 1) * 64]
                return vc.ap[:, it["s"], it["j"], h * 64:(h + 1) * 64]

            def ss_qk(k):
                it = sits[k]; z = zs_ps[k % 3]; s = it["s"]
                for hh in range(4):
                    for b in range(2):
                        kb.op("pe", lambda e, b=b, hh=hh: e.matmul(
                            z.ap[:, b, hh * 64:(hh + 1) * 64], kt_blk(it, b, hh),
                            qs_sb.ap[b * 64:(b + 1) * 64, hh, s * 64:(s + 1) * 64], start=(hh == 0), stop=(hh == 3),
                            skip_group_check=True),
                            reads=(ktc, ks_sb, qs_sb), writes=(z,), signal=(hh == 3 and b == 1))

            def ss_l(k):
                it = sits[k]; z = zs_ps[k % 3]; lb = sl_r[k % 3]
                kb.op("act", lambda e: e.activation(out=lb.ap, in_=z.ap[:, :, 0:256], func=AF.Softplus, scale=0.125),
                      reads=(z,), writes=(lb,))
                if it["j"] == 16:
                    for b in range(2):
                        for hh in range(4):
                            kb.op("dve", lambda e, b=b, hh=hh: e.tensor_tensor(
                                out=lb.ap[:, b, hh * 64:(hh + 1) * 64], in0=lb.ap[:, b, hh * 64:(hh + 1) * 64],
                                in1=mc.ap[:, 0:64], op=ALU.mult), reads=(lb, mc), writes=(lb,))

            def ss_ra(k):
                it = sits[k]
                if it["last"]:
                    return
                lb = sl_r[k % 3]; rn = sra_r[(k + 1) % 3]; rc = sra_r[k % 3]
                if it["first"]:
                    kb.op("dve", lambda e: e.tensor_copy(out=rn.ap, in_=lb.ap), reads=(lb,), writes=(rn,))
                else:
                    kb.op("dve", lambda e: e.tensor_tensor(out=rn.ap, in0=rc.ap, in1=lb.ap, op=ALU.add),
                          reads=(rc, lb), writes=(rn,))

            def ss_c(k):
                it = sits[k]; lb = sl_r[k % 3]; rc = sra_r[k % 3]; cp = zs_ps[k % 3]
                for b in range(2):
                    kb.op("pe", lambda e, b=b: e.matmul(cp.ap[:, b, 0:256], tri8.ap, lb.ap[:, b, :], start=False,
                                                        stop=it["first"], skip_group_check=True),
                          reads=(tri8, lb), writes=(cp,), signal=(it["first"] and b == 1))
                    if not it["first"]:
                        kb.op("pe", lambda e, b=b: e.matmul(cp.ap[:, b, 0:256], ones8.ap, rc.ap[:, b, :], start=False,
                                                            stop=True, skip_group_check=True),
                              reads=(ones8, rc), writes=(cp,), signal=(b == 1))

            def ss_w(k):
                it = sits[k]; cp = zs_ps[k % 3]; wb = sw_r[k % 3]
                kb.op("act", lambda e: e.activation(out=wb.ap, in_=cp.ap[:, :, 0:256], func=AF.Softplus, scale=0.125,
                                                    bias=-SHIFT), reads=(cp,), writes=(wb,))
                if it["j"] == 16:
                    for b in range(2):
                        for hh in range(4):
                            kb.op("dve", lambda e, b=b, hh=hh: e.tensor_tensor(
                                out=wb.ap[:, b, hh * 64:(hh + 1) * 64], in0=wb.ap[:, b, hh * 64:(hh + 1) * 64],
                                in1=mc.ap[:, 0:64], op=ALU.mult), reads=(wb, mc), writes=(wb,))

            def ss_pv(k):
                it = sits[k]; wb = sw_r[k % 3]; op_ = os_ps[0]; s = it["s"]
                for hh in range(4):
                    for b in range(2):
                        h = 2 * hh + b
                        kb.op("pe", lambda e, b=b, hh=hh, h=h: e.matmul(
                            op_.ap[b * 64:(b + 1) * 64, hh * 64:(hh + 1) * 64], v_blk(it, h),
                            wb.ap[:, b, hh * 64:(hh + 1) * 64], start=(it["first"] and hh == 0),
                            stop=(it["last"] and hh == 3), skip_group_check=True),
                            reads=(vc, vs_sb, wb), writes=(op_,), signal=(hh == 3 and b == 1))
                if it["last"]:
                    os_ = sos[s % 2]
                    kb.op("dve", lambda e: e.tensor_scalar(out=os_.ap.rearrange("p h q -> p (h q)"), in0=op_.ap[:, 0:256],
                                                           scalar1=ESHIFT, scalar2=None, op0=ALU.mult),
                          reads=(op_,), writes=(os_,))
                    kb.store([(ot[0:4, :, S + s * 64:S + (s + 1) * 64].rearrange("h p n -> p h n"), os_.ap)], os_,
                             writes=(dram_ots,))

            ss_qk(0)
            for r in range(NS + 3):
                if r + 1 < NS:
                    ss_qk(r + 1)
                if r < NS:
                    ss_l(r)
                if 0 <= r - 1 < NS:
                    ss_w(r - 1)
                if r < NS:
                    ss_ra(r)
                    ss_c(r)
                if 0 <= r - 2 < NS:
                    ss_pv(r - 2)
            scnt = 0
            for s in range(2):
                op_ = os_ps[0]; dn = dns_ps[0]
                order = [4, 3, 2, 1, 0]
                for n_, j in enumerate(order):
                    z = zs_ps[scnt % 2]; wb = sw_r[scnt % 3]; scnt += 1
                    for hh in range(4):
                        for b in range(2):
                            kt_ap = (ks_bd.ap[b * 64:(b + 1) * 64, hh, s, :] if j == 4
                                     else ktb.ap[b * 64:(b + 1) * 64, s, hh, j * 128:(j + 1) * 128])
                            kb.op("pe", lambda e, b=b, hh=hh, z=z, kt_ap=kt_ap: e.matmul(
                                z.ap[:, b, hh * 64:(hh + 1) * 64], kt_ap,
                                qs_bd.ap[b * 64:(b + 1) * 64, hh, s * 64:(s + 1) * 64], start=True, stop=True),
                                reads=(ktb, ks_bd, qs_bd), writes=(z,), signal=(hh == 3 and b == 1))
                    kb.op("act", lambda e, z=z, wb=wb: e.activation(out=wb.ap, in_=z.ap[:, :, 0:256], func=AF.Exp,
                                                                    scale=0.125), reads=(z,), writes=(wb,))
                    if j == 4:
                        kb.op("dve", lambda e, wb=wb: e.tensor_tensor(out=wb.ap, in0=wb.ap, in1=fsn.ap, op=ALU.mult),
                              reads=(wb, fsn), writes=(wb,))
                    elif j == 3:
                        kb.op("dve", lambda e, wb=wb: e.tensor_tensor(out=wb.ap, in0=wb.ap, in1=fs3.ap, op=ALU.mult),
                              reads=(wb, fs3), writes=(wb,))
                    fst_, lst_ = (n_ == 0), (n_ == 4)
                    for hh in range(4):
                        for b in range(2):
                            h = 2 * hh + b
                            v_ap = vs_bd.ap[:, s, h * 64:(h + 1) * 64] if j == 4 else vbc.ap[:, s, j, h * 64:(h + 1) * 64]
                            kb.op("pe", lambda e, b=b, hh=hh, wb=wb, v_ap=v_ap, fst_=fst_, lst_=lst_: e.matmul(
                                op_.ap[b * 64:(b + 1) * 64, hh * 64:(hh + 1) * 64], v_ap,
                                wb.ap[:, b, hh * 64:(hh + 1) * 64], start=(fst_ and hh == 0),
                                stop=(lst_ and hh == 3), skip_group_check=True),
                                reads=(vbc, vs_bd, wb), writes=(op_,), signal=False)
                    for b in range(2):
                        kb.op("pe", lambda e, b=b, wb=wb, fst_=fst_, lst_=lst_: e.matmul(
                            dn.ap[b * 64:(b + 1) * 64, :], ones1.ap, wb.ap[:, b, :], start=False, stop=lst_,
                            skip_group_check=True), reads=(ones1, wb), writes=(dn, op_), signal=(b == 1))
                rd = srd[s % 2]; os_ = sos[s % 2]
                kb.op("dve", lambda e, rd=rd: e.reciprocal(out=rd.ap, in_=dn.ap), reads=(dn, op_), writes=(rd,))
                kb.op("dve", lambda e, rd=rd, os_=os_: e.tensor_tensor(out=os_.ap.rearrange("p h q -> p (h q)"),
                                                                       in0=op_.ap[:, 0:256], in1=rd.ap, op=ALU.mult),
                      reads=(op_, rd), writes=(os_,))
                kb.store([(ot[4:8, :, S + s * 64:S + (s + 1) * 64].rearrange("h p n -> p h n"), os_.ap)], os_,
                         writes=(dram_ots,))
            kb.barrier()
            kb.release_dsems(cst + sos + [qs_sb, ks_sb, qs_bd, ks_bd, vs_sb, vs_bd])

            if stop_after == "SA":
                raise _Stop()
            kb.new_phase("O")
            kb.sb_off = const_end
            w_out_sb = sb(8 * D, BF16, (8, D), "w_out_sb")
            wst2 = ring(2, 4 * D, F32, (4, D), "wst2")
            for g in range(2):
                st = wst2[g % 2]
                kb.load([(st.ap, w_out[g * 512:(g + 1) * 512, :].rearrange("(c p) n -> p c n", p=128))], st)
                for c in range(4):
                    kb.op("dve", lambda e, st=st, c=c, g=g: e.tensor_scalar(
                        out=w_out_sb.ap[:, g * 4 + c, :], in0=st.ap[:, c, :], scalar1=gout_c.ap[:, g * 4 + c:g * 4 + c + 1],
                        scalar2=None, op0=ALU.mult), reads=(st, gout_c), writes=(w_out_sb,))
            oT_r = ring(2, 8 * 512, BF16, (8, 512), "oT")
            osq_r = ring(2, 8 * 512, BF16, (8, 512), "osq")
            xo_r = ring(3, D, F32, name="xo")
            h_r = ring(3, D, F32, name="h")
            rs_r = ring(4, 8, F32, name="rs")
            st_ps = [pbuf(bank(0)[:, 0:2]), pbuf(bank(1)[:, 0:2])]
            oo_ps = [(pbuf(bank(2, 2)), pbuf(bank(4, 2)))]
            dram_h = {}
            ocnt = 0
            for ti, (t0, n, is_s) in enumerate(tiles):
                oT = oT_r[ti % 2]; osq = osq_r[ti % 2]
                if is_s:
                    rd = [dram_ots]
                else:
                    rd = [dram_ot[(c, ti)] for c in range(8)]
                kb.load([(oT.ap[:, :, 0:n], ot[:, :, t0:t0 + n].rearrange("h p n -> p h n"))], oT, reads=rd)
                kb.op("act", lambda e, oT=oT, osq=osq: e.activation(out=osq.ap[:, :, 0:n], in_=oT.ap[:, :, 0:n],
                                                                    func=AF.Square), reads=(oT,), writes=(osq,))
                for blk in range(n // 128):
                    r0 = t0 + blk * 128
                    xi = xo_r[ocnt % 3]; hb = h_r[ocnt % 3]; rs = rs_r[ocnt % 4]; sp_ = st_ps[ocnt % 2]
                    pa, pb_ = oo_ps[0]
                    ocnt += 1
                    rows = x_s[blk * 128:(blk + 1) * 128, :] if is_s else x_p[r0:r0 + 128, :]
                    kb.load([(xi.ap, rows)], xi)
                    for g in range(2):
                        for c in range(4):
                            kb.op("pe", lambda e, g=g, c=c, sp_=sp_, osq=osq, blk=blk: e.matmul(
                                sp_.ap[:, g:g + 1], osq.ap[:, g * 4 + c, blk * 128:(blk + 1) * 128], ones1.ap[:, 0:1],
                                start=(c == 0), stop=(c == 3), skip_group_check=True),
                                reads=(osq, ones1), writes=(sp_,), signal=(g == 1 and c == 3))
                    kb.op("act", lambda e, rs=rs, sp_=sp_: e.activation(out=rs.ap[:, 0:2], in_=sp_.ap, func=AF.Ln,
                                                                        scale=1.0 / W, bias=EPS),
                          reads=(sp_,), writes=(rs,))
                    kb.op("act", lambda e, rs=rs: e.activation(out=rs.ap[:, 2:4], in_=rs.ap[:, 0:2], func=AF.Exp,
                                                               scale=-0.5), reads=(rs,), writes=(rs,))
                    for g, pg in ((0, pa), (1, pb_)):
                        for nh in range(2):
                            for c in range(4):
                                kb.op("pe", lambda e, g=g, nh=nh, c=c, pg=pg, oT=oT, blk=blk: e.matmul(
                                    pg.ap[:, nh * 512:(nh + 1) * 512], oT.ap[:, g * 4 + c, blk * 128:(blk + 1) * 128],
                                    w_out_sb.ap[:, g * 4 + c, nh * 512:(nh + 1) * 512], start=(c == 0), stop=(c == 3)),
                                    reads=(oT, w_out_sb), writes=(pg,), signal=(nh == 1 and c == 3))
                    kb.op("dve", lambda e, hb=hb, pa=pa, rs=rs, xi=xi: e.scalar_tensor_tensor(
                        out=hb.ap, in0=pa.ap, scalar=rs.ap[:, 2:3], in1=xi.ap, op0=ALU.mult, op1=ALU.add),
                        reads=(pa, rs, xi), writes=(hb,))
                    kb.op("dve", lambda e, hb=hb, pb_=pb_, rs=rs: e.scalar_tensor_tensor(
                        out=hb.ap, in0=pb_.ap, scalar=rs.ap[:, 3:4], in1=hb.ap, op0=ALU.mult, op1=ALU.add),
                        reads=(pb_, rs, hb), writes=(hb,))
                    d_ = Buf(None); dram_h[r0 // 128] = d_
                    kb.store([(hs[r0:r0 + 128, :], hb.ap)], hb, writes=(d_,))
            kb.barrier()
            kb.release_dsems(wst2 + oT_r + xo_r + h_r)

            if stop_after == "O":
                raise _Stop()
            kb.new_phase("F")
            kb.sb_off = const_end
            w_up_sb = sb(8 * DFF, BF16, (8, DFF), "w_up_sb")
            w_dn_sb = sb(32 * D, BF16, (32, D), "w_dn_sb")
            s4f = ring(4, 4, F32, name="s4f")
            s4g = ring(4, 4, F32, name="s4g")
            wst3_off = kb.sb_off
            wst3 = ring(2, DFF, F32, name="wst3")
            for c in range(8):
                st = wst3[c % 2]
                kb.load([(st.ap, w_up[c * 128:(c + 1) * 128, :])], st)
                kb.op("dve", lambda e, st=st, c=c: e.tensor_scalar(
                    out=w_up_sb.ap[:, c, :], in0=st.ap, scalar1=gffn_c.ap[:, c:c + 1], scalar2=None, op0=ALU.mult),
                    reads=(st, gffn_c), writes=(w_up_sb,))
            for g in range(8):
                st = wst3[g % 2]
                stv = st.ap.rearrange("p (c n) -> p c n", n=D)
                kb.load([(stv, w_down[g * 512:(g + 1) * 512, :].rearrange("(c p) n -> p c n", p=128))], st)
                kb.op("act" if g % 2 else "dve",
                      (lambda e, stv=stv, g=g: e.activation(out=w_dn_sb.ap[:, g * 4:(g + 1) * 4, :], in_=stv, func=AF.Copy))
                      if g % 2 else
                      (lambda e, stv=stv, g=g: e.tensor_copy(out=w_dn_sb.ap[:, g * 4:(g + 1) * 4, :], in_=stv)),
                      reads=(st,), writes=(w_dn_sb,))
            FT = 256
            kb.barrier()
            kb.release_dsems(wst3)
            kb.sb_off = wst3_off
            hin = ring(4, D, F32, name="hin")
            junk = ring(1, D, BF16, name="junk")
            hn_r = ring(2, D, BF16, name="hn")
            hnT = ring(2, 8 * FT, BF16, (8, FT), "hnT")
            aT = ring(1, 32 * FT, BF16, (32, FT), "aT")
            rl = ring(3, FT, BF16, name="rl")
            up_ps = [pbuf(bank(0)), pbuf(bank(1))]
            dn_ps2 = [pbuf(bank(2, 2)), pbuf(bank(4, 2))]
            ps_T = [pbuf(bank(6).bitcast(BF16)), pbuf(bank(7).bitcast(BF16))]
            ftiles = []
            for (t0, n, is_s) in tiles:
                for a in range(0, n, FT):
                    ftiles.append((t0 + a, min(FT, n - a), is_s, a))
            fstate = {"fcnt": 0, "ucnt": 0}
            fh = {}

            def f_norm(fi):
                (t0, n, is_s, a0) = ftiles[fi]
                hT = hnT[fi % 2]
                hbufs = []
                for blk in range(n // 128):
                    fcnt = fstate["fcnt"]
                    r0 = t0 + blk * 128
                    hi = hin[fcnt % 4]; ho = hn_r[fcnt % 2]; s4 = s4f[fcnt % 4]
                    hbufs.append((hi, r0))
                    kb.load([(hi.ap, hs[r0:r0 + 128, :])], hi, reads=[dram_h[r0 // 128]])
                    kb.op("act", lambda e: e.activation(out=ho.ap, in_=hi.ap, func=AF.Square, accum_out=s4.ap[:, 0:1]),
                          reads=(hi,), writes=(ho, s4))
                    kb.op("act", lambda e: e.activation(out=s4.ap[:, 1:2], in_=s4.ap[:, 0:1], func=AF.Ln,
                                                        scale=1.0 / D, bias=EPS), reads=(s4,), writes=(s4,))
                    kb.op("act", lambda e: e.activation(out=s4.ap[:, 2:3], in_=s4.ap[:, 1:2], func=AF.Exp,
                                                        scale=-0.5), reads=(s4,), writes=(s4,))
                    kb.op("dve", lambda e: e.tensor_scalar(out=ho.ap, in0=hi.ap, scalar1=s4.ap[:, 2:3],
                                                           scalar2=None, op0=ALU.mult),
                          reads=(hi, s4), writes=(ho,))
                    pT = ps_T[fcnt % 2]
                    for c in range(8):
                        kb.op("pe", lambda e, c=c: e.transpose(pT.ap[:, c * 128:(c + 1) * 128],
                                                               ho.ap[:, c * 128:(c + 1) * 128], ident.ap),
                              reads=(ho, ident), writes=(pT,), signal=(c == 7))
                    kb.op("dve", lambda e: e.tensor_copy(
                        out=hT.ap[:, :, blk * 128:(blk + 1) * 128], in_=pT.ap.rearrange("p (c n) -> p c n", n=128)),
                        reads=(pT,), writes=(hT,))
                    fstate["fcnt"] += 1
                fh[fi] = hbufs

            def f_up(fi):
                (t0, n, is_s, a0) = ftiles[fi]
                hT = hnT[fi % 2]; at = aT[0]
                for fc in range(32):
                    ucnt = fstate["ucnt"]
                    pu = up_ps[ucnt % 2]; rb = rl[ucnt % 3]; fstate["ucnt"] += 1
                    for kc in range(8):
                        kb.op("pe", lambda e, kc=kc: e.matmul(
                            pu.ap[:, 0:n], w_up_sb.ap[:, kc, fc * 128:(fc + 1) * 128], hT.ap[:, kc, 0:n],
                            start=(kc == 0), stop=(kc == 7)), reads=(w_up_sb, hT), writes=(pu,), signal=(kc == 7))
                    kb.op("act", lambda e: e.activation(out=rb.ap[:, 0:n], in_=pu.ap[:, 0:n], func=AF.Relu),
                          reads=(pu,), writes=(rb,))
                    kb.op("dve", lambda e: e.tensor_tensor(out=at.ap[:, fc, 0:n], in0=rb.ap[:, 0:n],
                                                           in1=rb.ap[:, 0:n], op=ALU.mult),
                          reads=(rb,), writes=(at,))

            def f_down(fi):
                (t0, n, is_s, a0) = ftiles[fi]
                at = aT[0]
                for blk in range(n // 128):
                    hi, r0 = fh[fi][blk]
                    pd = dn_ps2[blk % 2]
                    for nh in range(2):
                        for fc in range(32):
                            kb.op("pe", lambda e, nh=nh, fc=fc: e.matmul(
                                pd.ap[:, nh * 512:(nh + 1) * 512], at.ap[:, fc, blk * 128:(blk + 1) * 128],
                                w_dn_sb.ap[:, fc, nh * 512:(nh + 1) * 512], start=(fc == 0), stop=(fc == 31)),
                                reads=(at, w_dn_sb), writes=(pd,), signal=(nh == 1 and fc == 31))
                    s4 = s4g[(fi * 2 + blk) % 4]; jk = junk[0]
                    kb.op("dve", lambda e: e.tensor_tensor(out=hi.ap, in0=pd.ap, in1=hi.ap, op=ALU.add),
                          reads=(pd, hi), writes=(hi,))
                    kb.op("act", lambda e: e.activation(out=jk.ap, in_=hi.ap, func=AF.Square, accum_out=s4.ap[:, 0:1]),
                          reads=(hi,), writes=(jk, s4))
                    kb.op("act", lambda e: e.activation(out=s4.ap[:, 1:2], in_=s4.ap[:, 0:1], func=AF.Ln,
                                                        scale=1.0 / D, bias=EPS), reads=(s4,), writes=(s4,))
                    kb.op("act", lambda e: e.activation(out=s4.ap[:, 2:3], in_=s4.ap[:, 1:2], func=AF.Exp,
                                                        scale=-0.5), reads=(s4,), writes=(s4,))
                    kb.op("dve", lambda e: e.scalar_tensor_tensor(
                        out=hi.ap, in0=hi.ap, scalar=s4.ap[:, 2:3], in1=gfin_t.ap, op0=ALU.mult, op1=ALU.mult),
                        reads=(hi, s4, gfin_t), writes=(hi,))
                    dst = y_s[r0 - S:r0 - S + 128, :] if is_s else y_p[r0:r0 + 128, :]
                    kb.store([(dst, hi.ap)], hi)

            f_norm(0)
            for fi in range(len(ftiles)):
                f_up(fi)
                if fi + 1 < len(ftiles):
                    f_norm(fi + 1)
                f_down(fi)
            kb.barrier()

        try:
            plan()
        except _Stop:
            kb.barrier()
        with nc.Block() as block:
            @block.sync
            def _(e):
                for f in kb.q["sp"]:
                    f(e)

            @block.tensor
            def _(e):
                for f in kb.q["pe"]:
                    f(e)

            @block.scalar
            def _(e):
                for f in kb.q["act"]:
                    f(e)

            @block.vector
            def _(e):
                for f in kb.q["dve"]:
                    f(e)

            @block.gpsimd
            def _(e):
                for f in kb.q["pool"]:
                    f(e)
    return nc


_CACHE = {}


def _run(S, per_core_inputs, trace=False):
    if S not in _CACHE:
        _CACHE[S] = build_program(S)
    nc = _CACHE[S]
    return run_bass_kernel_spmd(nc, per_core_inputs, core_ids=list(range(len(per_core_inputs))), trace=trace)


def make_core_inputs(c, S, x_prompt, x_sample, cache_sb_k, cache_sb_v, cache_band_k, cache_band_v,
                     norm_mix_g, w_in, rel_bias, norm_sb_g, norm_band_g, w_out,
                     norm_ffn_g, w_up, w_down, norm_final_g):
    f = lambda a: np.ascontiguousarray(np.asarray(a, dtype=np.float32))
    return {
        "x_p": f(x_prompt[c, :S]),
        "x_s": f(x_sample[2 * c:2 * c + 2]).reshape(TS, D),
        "csk": f(cache_sb_k[0, 2 * c:2 * c + 2]).reshape(2, PAST, W),
        "csv": f(cache_sb_v[0, 2 * c:2 * c + 2]).reshape(2, PAST, W),
        "cbk": f(cache_band_k[0, 2 * c:2 * c + 2]).reshape(2, BROWS, W),
        "cbv": f(cache_band_v[0, 2 * c:2 * c + 2]).reshape(2, BROWS, W),
        "w_in": f(w_in[0]), "w_out": f(w_out[0]), "w_up": f(w_up[0]), "w_down": f(w_down[0]),
        "g_mix": f(norm_mix_g[0]), "g_ffn": f(norm_ffn_g[0]), "g_sb": f(norm_sb_g[0]), "g_bd": f(norm_band_g[0]),
        "g_fin": f(norm_final_g), "relb": f(rel_bias[0]),
    }


def assemble(results, S, nb):
    KEEP = min(512, S)
    g = lambda k: np.stack([np.asarray(r[k], dtype=np.float32) for r in results])
    y_p = g("y_p")
    y_s = g("y_s").reshape(2 * nb, 64, D)
    sbk_p = g("sbk_p").reshape(1, nb, S, 8, 64)
    sbv_p = g("sbv_p").reshape(1, nb, S, 8, 64)
    bdk_p = g("bdk_p").reshape(1, nb, KEEP, 8, 64)
    bdv_p = g("bdv_p").reshape(1, nb, KEEP, 8, 64)
    sbk_s = g("sbk_s").reshape(1, 2 * nb, 64, 8, 64)
    sbv_s = g("sbv_s").reshape(1, 2 * nb, 64, 8, 64)
    bdk_s = g("bdk_s").reshape(1, 2 * nb, 64, 8, 64)
    bdv_s = g("bdv_s").reshape(1, 2 * nb, 64, 8, 64)
    return (y_p, y_s, sbk_p, sbv_p, bdk_p, bdv_p, sbk_s, sbv_s, bdk_s, bdv_s)


def kernel(x_prompt, x_sample, cache_sb_k, cache_sb_v, cache_band_k, cache_band_v,
           norm_mix_g, w_in, rel_bias, norm_sb_g, norm_band_g, w_out,
           norm_ffn_g, w_up, w_down, norm_final_g):
    x_prompt = np.asarray(x_prompt)
    nb, S = x_prompt.shape[0], x_prompt.shape[1]
    args = (x_prompt, np.asarray(x_sample), np.asarray(cache_sb_k), np.asarray(cache_sb_v),
            np.asarray(cache_band_k), np.asarray(cache_band_v), np.asarray(norm_mix_g), np.asarray(w_in),
            np.asarray(rel_bias), np.asarray(norm_sb_g), np.asarray(norm_band_g), np.asarray(w_out),
            np.asarray(norm_ffn_g), np.asarray(w_up), np.asarray(w_down), np.asarray(norm_final_g))
    in_maps = [make_core_inputs(c, S, *args) for c in range(nb)]
    res = _run(S, in_maps)
    return assemble(res.results, S, nb)
```
